# Optimizing a Trainium2 kernel written in Bass

```python
import jax
import jax.numpy as jnp
from jax import lax
import numpy as np

D_MODEL = 1024
BATCH = 2
SEQ = 16384
DEPTH = 1

GRID_W = 64
CTX_LEN = 256
NA_HEADS = 8
NA_HEAD_DIM = 64
NA_WIDTH = NA_HEADS * NA_HEAD_DIM
NA_WIN_ROWS = 8
NA_WIN_COLS = 16
RW_HEADS = 8
RW_HEAD_DIM = 64
RW_WIDTH = RW_HEADS * RW_HEAD_DIM
MIX_WIDTH = NA_WIDTH + RW_WIDTH
DECAY_LORA = 64
AAA_LORA = 64
GATE_LORA = 128
RW_PROJ = 3 * RW_WIDTH + 2 * DECAY_LORA + AAA_LORA + GATE_LORA
IN_PROJ = 3 * NA_WIDTH + RW_PROJ
D_FF = -(-8 * D_MODEL // (3 * 256)) * 256
NORM_EPS = 1e-6
RW_GN_EPS = 64e-5

kernel_name = 'hybrid_na_rwkv7_prefix_dit_layer'


def rms_norm(x, g, eps=NORM_EPS):
    xf = x.astype(jnp.float32)
    y = xf * lax.rsqrt(jnp.mean(xf * xf, axis=-1, keepdims=True) + eps)
    return (y * g.astype(jnp.float32)).astype(x.dtype)


def modulate(h, shift, scale):
    return h * (1.0 + scale[..., None, :]) + shift[..., None, :]


def split_heads(t, n_heads):
    return t.reshape(t.shape[:-1] + (n_heads, t.shape[-1] // n_heads))


def swiglu(u, w_in, w_out):
    gate, up = jnp.split(u @ w_in, 2, axis=-1)
    return (jax.nn.silu(gate) * up) @ w_out


def na_qkv(p, q_g, k_g):
    q, k, v = jnp.split(p, 3, axis=-1)
    q = rms_norm(split_heads(q, NA_HEADS), q_g) * (NA_HEAD_DIM ** -0.5)
    k = rms_norm(split_heads(k, NA_HEADS), k_g)
    return q, k, split_heads(v, NA_HEADS)


def neighborhood_attention(q, k, v, k_ctx, v_ctx, rpb):
    B, T, H, dh = q.shape
    rows = T // GRID_W
    wr = min(NA_WIN_ROWS, rows)
    wc = NA_WIN_COLS
    qg = q.reshape(B, rows, GRID_W, H, dh).transpose(1, 0, 3, 2, 4)
    kg = k.reshape(B, rows, GRID_W, H, dh).transpose(0, 3, 1, 2, 4)
    vg = v.reshape(B, rows, GRID_W, H, dh).transpose(0, 3, 1, 2, 4)
    kc = k_ctx.transpose(0, 2, 1, 3)
    vc = v_ctx.transpose(0, 2, 1, 3)
    col = jnp.arange(GRID_W)
    col_start = jnp.clip(col - wc // 2, 0, GRID_W - wc)
    col_idx = col_start[:, None] + jnp.arange(wc)[None, :]
    dj = col_idx - col[:, None] + (NA_WIN_COLS - 1)
    n_win = wr * wc

    def row_block(args):
        i, q_row = args
        rs = jnp.clip(i - wr // 2, 0, rows - wr)
        k_rows = lax.dynamic_slice_in_dim(kg, rs, wr, axis=2)
        v_rows = lax.dynamic_slice_in_dim(vg, rs, wr, axis=2)
        k_win = k_rows[:, :, :, col_idx]
        v_win = v_rows[:, :, :, col_idx]
        di = rs + jnp.arange(wr) - i + (NA_WIN_ROWS - 1)
        bias = rpb[:, di[:, None, None], dj[None]]
        s_win = jnp.einsum('bhqd,bhrqcd->bhqrc', q_row, k_win) + bias.transpose(0, 2, 1, 3)[None]
        s_ctx = jnp.einsum('bhqd,bhld->bhql', q_row, kc)
        s = jnp.concatenate([s_win.reshape(B, H, GRID_W, n_win), s_ctx], axis=-1)
        pr = jax.nn.softmax(s.astype(jnp.float32), axis=-1).astype(v.dtype)
        p_win = pr[..., :n_win].reshape(B, H, GRID_W, wr, wc)
        return (jnp.einsum('bhqrc,bhrqcd->bhqd', p_win, v_win)
                + jnp.einsum('bhql,bhld->bhqd', pr[..., n_win:], vc))

    out = lax.map(row_block, (jnp.arange(rows), qg))
    return out.transpose(1, 0, 3, 2, 4).reshape(B, T, H * dh)


def context_attention(q, k, v):
    s = jnp.einsum('bqhd,bkhd->bhqk', q, k).astype(jnp.float32)
    pr = jax.nn.softmax(s, axis=-1).astype(v.dtype)
    o = jnp.einsum('bhqk,bkhd->bqhd', pr, v)
    return o.reshape(o.shape[:2] + (NA_WIDTH,))


def token_shift(p, mu_prev, mu_next):
    prev = jnp.pad(p[:, :-1], ((0, 0), (1, 0), (0, 0)))
    nxt = jnp.pad(p[:, 1:], ((0, 0), (0, 1), (0, 0)))
    return p + mu_prev * (prev - p) + mu_next * (nxt - p)


def rwkv_prep(p, mu_prev, mu_next, w0, w_up, a0, a_up, g_up, k_k, k_a):
    p = token_shift(p, mu_prev, mu_next).astype(jnp.float32)
    offs = np.cumsum([RW_WIDTH, RW_WIDTH, RW_WIDTH, 2 * DECAY_LORA, AAA_LORA]).tolist()
    r, k, v, wd, ad, gd = jnp.split(p, offs, axis=-1)
    wd = wd.reshape(wd.shape[:-1] + (2, DECAY_LORA))
    logw = -jax.nn.softplus(-(w0 + jnp.einsum('btdr,drc->btdc', jnp.tanh(wd), w_up))) - 0.5
    decay = jnp.exp(-jnp.exp(logw))
    a = jax.nn.sigmoid(a0 + jnp.einsum('btr,drc->btdc', ad, a_up))
    g = jax.nn.sigmoid(gd) @ g_up
    kk = split_heads(k * k_k, RW_HEADS)
    kk = kk / jnp.maximum(jnp.sqrt(jnp.sum(kk * kk, axis=-1, keepdims=True)), 1e-12)
    k_dir = k[:, :, None, :] * (1.0 + (a - 1.0) * k_a)
    return r, v, kk, decay, a, k_dir, g


def dir_inputs(feat, d):
    r, v, kk, decay, a, k_dir, _ = feat
    hs = lambda t: split_heads(t, RW_HEADS)
    return hs(r), hs(decay[:, :, d]), hs(k_dir[:, :, d]), hs(v), -kk, kk * hs(a[:, :, d])


def wkv_scan(s0, r, w, k, v, a, b, reverse):
    def step(s, inp):
        r_t, w_t, k_t, v_t, a_t, b_t = inp
        sa = jnp.einsum('bhvk,bhk->bhv', s, a_t)
        s = s * w_t[:, :, None, :] + sa[..., None] * b_t[:, :, None, :] + v_t[..., None] * k_t[:, :, None, :]
        return s, jnp.einsum('bhvk,bhk->bhv', s, r_t)
    xs = tuple(jnp.moveaxis(t, 1, 0) for t in (r, w, k, v, a, b))
    s, ys = lax.scan(step, s0, xs, reverse=reverse)
    return s, jnp.moveaxis(ys, 0, 1)


def rwkv_bonus(inp, r_k):
    r, _, k, v, _, _ = inp
    return jnp.sum(r * k * r_k, axis=-1, keepdims=True) * v


def rwkv_finish(wkv, bonus, g, ln_g, ln_b):
    mu = jnp.mean(wkv, axis=-1, keepdims=True)
    var = jnp.mean(jnp.square(wkv - mu), axis=-1, keepdims=True)
    y = (wkv - mu) * lax.rsqrt(var + RW_GN_EPS) * ln_g.reshape(RW_HEADS, RW_HEAD_DIM) + ln_b.reshape(RW_HEADS, RW_HEAD_DIM)
    y = y + bonus
    return y.reshape(y.shape[:2] + (RW_WIDTH,)) * g


def rwkv_bidir(feat, featc, r_k, ln_g, ln_b, want_ctx):
    B = feat[0].shape[0]
    s0 = jnp.zeros((B, RW_HEADS, RW_HEAD_DIM, RW_HEAD_DIM), jnp.float32)
    wkv, bonus, wkv_c, bonus_c = 0.0, 0.0, 0.0, 0.0
    for d, rev in enumerate((False, True)):
        ix = dir_inputs(feat, d)
        ic = dir_inputs(featc, d)
        s_ctx, y_c = wkv_scan(s0, *ic, reverse=rev)
        _, y_x = wkv_scan(s_ctx, *ix, reverse=rev)
        wkv = wkv + y_x
        bonus = bonus + rwkv_bonus(ix, r_k)
        if want_ctx:
            wkv_c = wkv_c + y_c
            bonus_c = bonus_c + rwkv_bonus(ic, r_k)
    o_x = rwkv_finish(wkv, bonus, feat[6], ln_g, ln_b)
    o_c = rwkv_finish(wkv_c, bonus_c, featc[6], ln_g, ln_b) if want_ctx else None
    return o_x, o_c


def setup_inputs(seed: int = 0) -> dict:
    key = jax.random.key(seed)
    ks = jax.random.split(key, 28)
    nrm = lambda k, shape, s: jax.random.normal(k, shape, jnp.float32) * s
    L = DEPTH
    return {
        'x': nrm(ks[0], (BATCH, SEQ, D_MODEL), 1.0),
        'c': nrm(ks[1], (BATCH, D_MODEL), 1.0),
        'ctx': nrm(ks[2], (BATCH, CTX_LEN, D_MODEL), 1.0),
        'c_ctx': nrm(ks[3], (D_MODEL,), 1.0),
        'norm1_g': 1.0 + nrm(ks[4], (L, D_MODEL), 0.02),
        'norm2_g': 1.0 + nrm(ks[5], (L, D_MODEL), 0.02),
        'w_ada': nrm(ks[6], (L, D_MODEL, 6 * D_MODEL), D_MODEL ** -0.5),
        'b_ada': nrm(ks[7], (L, 6 * D_MODEL), 0.01),
        'w_in': nrm(ks[8], (L, D_MODEL, IN_PROJ), D_MODEL ** -0.5),
        'na_q_g': 1.0 + nrm(ks[9], (L, NA_HEAD_DIM), 0.02),
        'na_k_g': 1.0 + nrm(ks[10], (L, NA_HEAD_DIM), 0.02),
        'na_rpb': nrm(ks[11], (L, NA_HEADS, 2 * NA_WIN_ROWS - 1, 2 * NA_WIN_COLS - 1), 0.2),
        'rw_mu_prev': jax.random.uniform(ks[12], (L, RW_PROJ), jnp.float32, 0.0, 0.5),
        'rw_mu_next': jax.random.uniform(ks[13], (L, RW_PROJ), jnp.float32, 0.0, 0.5),
        'rw_w0': jax.random.uniform(ks[14], (L, 2, RW_WIDTH), jnp.float32, -5.0, 0.0),
        'rw_w_up': nrm(ks[15], (L, 2, DECAY_LORA, RW_WIDTH), 0.5 * DECAY_LORA ** -0.5),
        'rw_a0': nrm(ks[16], (L, 2, RW_WIDTH), 0.5),
        'rw_a_up': nrm(ks[17], (L, 2, AAA_LORA, RW_WIDTH), AAA_LORA ** -0.5),
        'rw_g_up': nrm(ks[18], (L, GATE_LORA, RW_WIDTH), GATE_LORA ** -0.5),
        'rw_k_k': 0.85 + nrm(ks[19], (L, RW_WIDTH), 0.05),
        'rw_k_a': 1.0 + nrm(ks[20], (L, RW_WIDTH), 0.05),
        'rw_r_k': nrm(ks[21], (L, RW_HEADS, RW_HEAD_DIM), 0.1),
        'rw_ln_g': 1.0 + nrm(ks[22], (L, RW_WIDTH), 0.02),
        'rw_ln_b': nrm(ks[23], (L, RW_WIDTH), 0.01),
        'w_out': nrm(ks[24], (L, MIX_WIDTH, D_MODEL), MIX_WIDTH ** -0.5),
        'ffn_w_in': nrm(ks[25], (L, D_MODEL, 2 * D_FF), D_MODEL ** -0.5),
        'ffn_w_out': nrm(ks[26], (L, D_FF, D_MODEL), D_FF ** -0.5),
    }


def reference(x, c, ctx, c_ctx, norm1_g, norm2_g, w_ada, b_ada, w_in, na_q_g, na_k_g, na_rpb,
              rw_mu_prev, rw_mu_next, rw_w0, rw_w_up, rw_a0, rw_a_up, rw_g_up, rw_k_k, rw_k_a,
              rw_r_k, rw_ln_g, rw_ln_b, w_out, ffn_w_in, ffn_w_out):
    silu_c = jax.nn.silu(c)
    silu_cc = jax.nn.silu(c_ctx)
    h, hc = x, ctx
    for l in range(DEPTH):
        last = l == DEPTH - 1
        mod = jnp.split(silu_c @ w_ada[l] + b_ada[l], 6, axis=-1)
        modc = jnp.split(silu_cc @ w_ada[l] + b_ada[l], 6, axis=-1)
        u = modulate(rms_norm(h, norm1_g[l]), mod[0], mod[1])
        uc = modulate(rms_norm(hc, norm1_g[l]), modc[0], modc[1])
        p = u @ w_in[l]
        pc = uc @ w_in[l]
        q, k, v = na_qkv(p[..., :3 * NA_WIDTH], na_q_g[l], na_k_g[l])
        qc, kc, vc = na_qkv(pc[..., :3 * NA_WIDTH], na_q_g[l], na_k_g[l])
        o_na = neighborhood_attention(q, k, v, kc, vc, na_rpb[l])
        rw_args = (rw_mu_prev[l], rw_mu_next[l], rw_w0[l], rw_w_up[l], rw_a0[l], rw_a_up[l],
                   rw_g_up[l], rw_k_k[l], rw_k_a[l])
        feat = rwkv_prep(p[..., 3 * NA_WIDTH:], *rw_args)
        featc = rwkv_prep(pc[..., 3 * NA_WIDTH:], *rw_args)
        o_rw, o_rwc = rwkv_bidir(feat, featc, rw_r_k[l], rw_ln_g[l], rw_ln_b[l], not last)
        mix = jnp.concatenate([o_na, o_rw.astype(h.dtype)], axis=-1) @ w_out[l]
        h = h + mod[2][:, None, :] * mix
        h = h + mod[5][:, None, :] * swiglu(modulate(rms_norm(h, norm2_g[l]), mod[3], mod[4]),
                                            ffn_w_in[l], ffn_w_out[l])
        if not last:
            mixc = jnp.concatenate([context_attention(qc, kc, vc), o_rwc.astype(hc.dtype)], axis=-1) @ w_out[l]
            hc = hc + modc[2] * mixc
            hc = hc + modc[5] * swiglu(modulate(rms_norm(hc, norm2_g[l]), modc[3], modc[4]),
                                       ffn_w_in[l], ffn_w_out[l])
    return h
```

```python
import contextlib
import numpy as np
import concourse.bass as bass
import concourse.mybir as mybir
from concourse.bass_utils import run_bass_kernel_spmd

F32 = mybir.dt.float32
BF16 = mybir.dt.bfloat16
I32 = mybir.dt.int32
AF = mybir.ActivationFunctionType
ALU = mybir.AluOpType
AX = mybir.AxisListType

EP = 30000
T = 16384
TC = 256
NT = T + TC
D = 1024
NEG = -30000.0
EPS = 1e-6


class Buf:
    def __init__(self, name=""):
        self.name = name
        self.w = None
        self.r = {}


class Prog:
    ENGS = ['pe', 'act', 'dve', 'pool', 'sp']

    _uid = [0]

    def __init__(self, nc, ndma=8):
        self.nc = nc
        Prog._uid[0] += 1
        self.uid = Prog._uid[0]
        self.q = {e: [] for e in self.ENGS}
        self.cnt = {e: 0 for e in self.ENGS}
        self.seen = {e: {} for e in self.ENGS}
        self.ndma = ndma
        self.dma_cnt = {}
        self.dma_eng = {}
        self.dma_next = {e: 0 for e in self.ENGS}

    def _deps(self, eng, reads, writes):
        deps = {}

        def add(tok):
            if tok is None:
                return
            k, v = tok
            if deps.get(k, 0) < v:
                deps[k] = v
        for b in reads:
            add(b.w)
        for b in writes:
            add(b.w)
            for k, v in b.r.items():
                if k == eng:
                    continue
                add((k, v))
        waits = []
        for k, v in deps.items():
            if self.seen[eng].get(k, 0) < v:
                self.seen[eng][k] = v
                waits.append((k, v))
        return waits

    def op(self, eng, fn, reads=(), writes=()):
        waits = self._deps(eng, reads, writes)
        self.cnt[eng] += 1
        idx = self.cnt[eng]
        self.q[eng].append(('op', waits, fn, idx))
        for b in reads:
            b.r[eng] = idx
        for b in writes:
            b.w = (eng, idx)
            b.r = {}

    def dma(self, eng, fn, reads=(), writes=(), inc=16):
        slot = self.dma_next[eng]
        self.dma_next[eng] = (slot + 1) % self.ndma
        key = ('dma', eng, slot)
        prev = self.dma_cnt.get(key, 0)
        waits = self._deps(eng, reads, writes)
        if prev > 0 and self.seen[eng].get(key, 0) < prev:
            self.seen[eng][key] = prev
            waits.append((key, prev))
        val = prev + inc
        self.dma_cnt[key] = val
        self.dma_eng[key] = eng
        self.q[eng].append(('dma', waits, fn, key, inc))
        for b in reads:
            b.r[key] = val
        for b in writes:
            b.w = (key, val)
            b.r = {}
        return (key, val)

    def flush(self):
        nc = self.nc
        for key, val in self.dma_cnt.items():
            self.q[self.dma_eng[key]].append(('wait', [(key, val)]))
        with contextlib.ExitStack() as st:
            sems = {}
            for e in self.ENGS:
                nep = self.cnt[e] // EP + 1
                for k in range(nep):
                    sems[(e, k)] = st.enter_context(nc.semaphore(f"s{self.uid}_{e}_{k}"))
            for key in self.dma_cnt:
                sems[key] = st.enter_context(nc.semaphore(f"d{self.uid}_{key[1]}_{key[2]}"))
            block = st.enter_context(nc.Block())

            def emit_wait(eng, k, v):
                if isinstance(k, tuple):
                    eng.wait_ge(sems[k], v)
                else:
                    ep = (v - 1) // EP
                    eng.wait_ge(sems[(k, ep)], v - ep * EP)

            def run(ename, eng):
                for it in self.q[ename]:
                    if it[0] == 'op':
                        _, waits, fn, idx = it
                        for k, v in waits:
                            emit_wait(eng, k, v)
                        ep = (idx - 1) // EP
                        fn(eng).then_inc(sems[(ename, ep)], 1)
                    elif it[0] == 'dma':
                        _, waits, fn, key, inc = it
                        for k, v in waits:
                            emit_wait(eng, k, v)
                        fn(eng).then_inc(sems[key], inc)
                    elif it[0] == 'raw':
                        it[1](eng)
                    else:
                        for k, v in it[1]:
                            emit_wait(eng, k, v)

            @block.tensor
            def _(e):
                run('pe', e)

            @block.scalar
            def _(e):
                run('act', e)

            @block.vector
            def _(e):
                run('dve', e)

            @block.gpsimd
            def _(e):
                run('pool', e)

            @block.sync
            def _(e):
                run('sp', e)


IN_SPECS = {
    "xcat": ([NT, D], F32), "x_own": ([4096, D], F32), "c2T": ([128, 8, 2], F32),
    "w_ada": ([D, 6 * D], F32), "b_adaT": ([128, 48], F32), "b_ada_bc": ([128, 4 * D], F32),
    "g1T": ([128, 8], F32), "g2_bc": ([128, D], F32),
    "w_na": ([D, 384], F32), "w_rw": ([D, 704], F32),
    "muT": ([128, 2, 6], F32), "gqk_bc": ([128, 2, 64], F32),
    "bias": ([128, 2, 21, 128], F32),
    "w_upT": ([128, 128], F32), "a_upT": ([64, 2, 128], F32), "g_up": ([128, 128], F32),
    "cv": ([128, 9], F32),
    "w_out": ([D, D], F32), "ffn_w_in": ([D, 5632], F32), "ffn_w_out": ([2816, D], F32),
    "qoff": ([1, 1], I32),
    "c_ident": ([128, 128], F32), "c_bones": ([128, 128], F32), "c_mk": ([128, 2, 4, 2, 128], F32),
    "c_mst": ([128, 2, 4, 64], F32), "c_mkd": ([128, 2, 4, 2, 64], F32), "c_istack": ([128, 4, 64], F32), "c_reset": ([128, 256], F32),
}


def build_nc():
    nc = bass.Bass("TRN2", target_bir_lowering=False)
    I = {k: nc.dram_tensor(k, s, d, kind="ExternalInput").ap() for k, (s, d) in IN_SPECS.items()}
    out = nc.dram_tensor("out", [4096, D], F32, kind="ExternalOutput").ap()
    U = nc.dram_tensor("U_scr", [D, NT], BF16).ap()
    gin = nc.dram_tensor("gin", [8, 256, 2048], BF16)
    gout = nc.dram_tensor("gout", [8, 1024, 2048], BF16)
    Uv = U.rearrange("(kc p) t -> p kc t", p=128)
    dbgU = dbgG = dbgW = dbgF = None

    outer = contextlib.ExitStack()
    with outer:
        def sbo(name, shape, dt):
            return outer.enter_context(nc.sbuf_tensor('S0_' + name, shape, dt))
        A1 = sbo("A1", [128, 2, 8], F32)
        SH1 = sbo("SH1", [128, 2, 8], F32)
        modbc = sbo("modbc", [128, 4, D], F32)
        identb = sbo("identb", [128, 128], BF16)
        identf = sbo("identf", [128, 128], F32)
        bones = sbo("bones", [128, 128], F32)

        with contextlib.ExitStack() as st:
            P = Prog(nc)

            def sb(name, shape, dt):
                return st.enter_context(nc.sbuf_tensor('S%d_' % P.uid + name, shape, dt))

            def ps(name, shape, dt):
                return st.enter_context(nc.psum_tensor('P%d_' % P.uid + name, shape, dt))
            c2 = sb("c2", [128, 8, 2], F32)
            sT = sb("sT", [128, 8, 2], F32)
            sbc = sb("sbc", [128, 8, 128], F32)
            onesf = sb("onesf", [128, 128], F32)
            bT = sb("bT", [128, 48], F32)
            bbc = sb("bbc", [128, 4 * D], F32)
            g1 = sb("g1", [128, 8], F32)
            g2bc = sb("g2bc", [128, D], F32)
            modT = sb("modT", [128, 2, 48], F32)
            wblk = [sb(f"wblk{i}", [128, 8, D], F32) for i in range(2)]
            pmod = ps("pmod", [128, 8, 2], F32)
            pbc = [ps(f"pbc{i}", [128, 512], F32) for i in range(2)]
            B = {n: Buf(n) for n in ["c2", "sT", "sbc", "onesf", "bT", "bbc", "g1", "g2bc", "modT", "wblk0", "wblk1",
                                     "pmod", "pbc0", "pbc1", "A1", "SH1", "modbc", "ident", "bones"]}
            P.dma('sp', lambda e: e.dma_start(out=c2[:], in_=I["c2T"]), writes=[B["c2"]])
            P.dma('sp', lambda e: e.dma_start(out=bT[:], in_=I["b_adaT"]), writes=[B["bT"]])
            P.dma('sp', lambda e: e.dma_start(out=bbc[:], in_=I["b_ada_bc"]), writes=[B["bbc"]])
            P.dma('sp', lambda e: e.dma_start(out=g1[:], in_=I["g1T"]), writes=[B["g1"]])
            P.dma('sp', lambda e: e.dma_start(out=g2bc[:], in_=I["g2_bc"]), writes=[B["g2bc"]])
            P.dma('sp', lambda e: e.dma_start(out=identf[:], in_=I["c_ident"]), writes=[B["ident"]])
            P.dma('sp', lambda e: e.dma_start(out=bones[:], in_=I["c_bones"]), writes=[B["bones"]])
            P.op('dve', lambda e: e.tensor_copy(out=identb[:], in_=identf[:]), reads=[B["ident"]], writes=[B["ident"]])
            P.op('act', lambda e: e.activation(out=sT[:], in_=c2[:], func=AF.Silu), reads=[B["c2"]], writes=[B["sT"]])
            P.op('dve', lambda e: e.memset(onesf[:], 1.0), writes=[B["onesf"]])
            for kc in range(8):
                P.op('dve', lambda e, kc=kc: e.tensor_scalar(out=sbc[:, kc, :], in0=onesf[:], scalar1=sT[:, kc, 0:1],
                                                             scalar2=None, op0=ALU.mult),
                     reads=[B["onesf"], B["sT"]], writes=[B["sbc"]])
            wv = I["w_ada"].rearrange("(kc p) n -> p kc n", p=128)
            for m in range(6):
                wb = wblk[m % 2]
                Bw = B[f"wblk{m % 2}"]
                for hh in range(2):
                    P.dma('sp', lambda e, m=m, wb=wb, hh=hh: e.dma_start(out=wb[:, 4 * hh:4 * hh + 4, :],
                                                                         in_=wv[:, 4 * hh:4 * hh + 4, m * D:(m + 1) * D]),
                          writes=[Bw])
                for jj in range(8):
                    for kc in range(8):
                        P.op('pe', lambda e, wb=wb, jj=jj, kc=kc: e.matmul(pmod[:, jj, :], lhsT=wb[:, kc, jj * 128:(jj + 1) * 128],
                                                                           rhs=sT[:, kc, :], start=(kc == 0), stop=(kc == 7)),
                             reads=[Bw, B["sT"]], writes=[B["pmod"]])
                for i in range(2):
                    P.op('dve', lambda e, m=m, i=i: e.tensor_tensor(out=modT[:, i, m * 8:(m + 1) * 8], in0=pmod[:, :, i],
                                                                   in1=bT[:, m * 8:(m + 1) * 8], op=ALU.add),
                         reads=[B["pmod"], B["bT"]], writes=[B["modT"]])
                if m >= 2:
                    for nh in range(2):
                        pb = pbc[nh]
                        for kc in range(8):
                            P.op('pe', lambda e, wb=wb, pb=pb, nh=nh, kc=kc: e.matmul(
                                pb[:, :], lhsT=sbc[:, kc, :], rhs=wb[:, kc, nh * 512:(nh + 1) * 512],
                                start=(kc == 0), stop=(kc == 7)),
                                reads=[Bw, B["sbc"]], writes=[B[f"pbc{nh}"]])
                        P.op('dve', lambda e, m=m, pb=pb, nh=nh: e.tensor_tensor(
                            out=modbc[:, m - 2, nh * 512:(nh + 1) * 512], in0=pb[:, :],
                            in1=bbc[:, (m - 2) * D + nh * 512:(m - 2) * D + (nh + 1) * 512], op=ALU.add),
                            reads=[B[f"pbc{nh}"], B["bbc"]], writes=[B["modbc"]])
            P.op('dve', lambda e: e.scalar_tensor_tensor(out=modbc[:, 2, :], in0=modbc[:, 2, :], scalar=1.0, in1=g2bc[:],
                                                         op0=ALU.add, op1=ALU.mult),
                 reads=[B["modbc"], B["g2bc"]], writes=[B["modbc"]])
            for i in range(2):
                P.op('dve', lambda e, i=i: e.scalar_tensor_tensor(out=A1[:, i, :], in0=modT[:, i, 8:16], scalar=1.0, in1=g1[:],
                                                                  op0=ALU.add, op1=ALU.mult),
                     reads=[B["modT"], B["g1"]], writes=[B["A1"]])
                P.op('dve', lambda e, i=i: e.tensor_copy(out=SH1[:, i, :], in_=modT[:, i, 0:8]),
                     reads=[B["modT"]], writes=[B["SH1"]])

            NB = 3
            xt = [sb(f"xt{i}", [128, D], F32) for i in range(NB)]
            sq = sb("sqscr", [128, D], F32)
            ss = [sb(f"ss{i}", [128, 1], F32) for i in range(NB)]
            rs = [sb(f"rs{i}", [128, 1], F32) for i in range(NB)]
            xn = [sb(f"xn{i}", [128, D], BF16) for i in range(NB)]
            uT = [sb(f"uT{i}", [128, 8, 128], BF16) for i in range(NB)]
            ptr = [ps(f"ptr{i}", [128, 8, 128], BF16) for i in range(2)]
            Bx = [Buf() for _ in range(NB)]
            Bss = [Buf() for _ in range(NB)]
            Bxn = [Buf() for _ in range(NB)]
            BuT = [Buf() for _ in range(NB)]
            Bpt = [Buf() for _ in range(2)]
            Bsq = Buf()
            BU = Buf()
            for ti in range(NT // 128):
                k = ti % NB
                i = 1 if ti < 2 else 0
                P.dma('sp', lambda e, ti=ti, k=k: e.dma_start(out=xt[k][:], in_=I["xcat"][ti * 128:(ti + 1) * 128, :]),
                      writes=[Bx[k]])
                P.op('act', lambda e, k=k: e.activation(out=sq[:], in_=xt[k][:], func=AF.Square, accum_out=ss[k][:]),
                     reads=[Bx[k]], writes=[Bsq, Bss[k]])
                P.op('dve', lambda e, k=k: e.tensor_scalar(out=rs[k][:], in0=ss[k][:], scalar1=1.0 / D, scalar2=EPS,
                                                          op0=ALU.mult, op1=ALU.add), reads=[Bss[k]], writes=[Bss[k]])
                P.op('act', lambda e, k=k: e.activation(out=rs[k][:], in_=rs[k][:], func=AF.Sqrt), reads=[Bss[k]], writes=[Bss[k]])
                P.op('dve', lambda e, k=k: e.reciprocal(out=rs[k][:], in_=rs[k][:]), reads=[Bss[k]], writes=[Bss[k]])
                P.op('dve', lambda e, k=k: e.tensor_scalar(out=xn[k][:], in0=xt[k][:], scalar1=rs[k][:, 0:1], scalar2=None,
                                                          op0=ALU.mult), reads=[Bx[k], Bss[k]], writes=[Bxn[k]])
                pk = ti % 2
                for kc in range(8):
                    P.op('pe', lambda e, k=k, pk=pk, kc=kc: e.transpose(out=ptr[pk][:, kc, :], in_=xn[k][:, kc * 128:(kc + 1) * 128],
                                                                        identity=identb[:]),
                         reads=[Bxn[k], B["ident"]], writes=[Bpt[pk]])
                for kc in range(8):
                    eng = 'act' if kc % 2 == 0 else 'dve'
                    if eng == 'act':
                        P.op('act', lambda e, k=k, pk=pk, kc=kc, i=i: e.activation(
                            out=uT[k][:, kc, :], in_=ptr[pk][:, kc, :], func=AF.Identity,
                            bias=SH1[:, i, kc:kc + 1], scale=A1[:, i, kc:kc + 1]),
                            reads=[Bpt[pk], B["A1"], B["SH1"]], writes=[BuT[k]])
                    else:
                        P.op('dve', lambda e, k=k, pk=pk, kc=kc, i=i: e.tensor_scalar(
                            out=uT[k][:, kc, :], in0=ptr[pk][:, kc, :], scalar1=A1[:, i, kc:kc + 1],
                            scalar2=SH1[:, i, kc:kc + 1], op0=ALU.mult, op1=ALU.add),
                            reads=[Bpt[pk], B["A1"], B["SH1"]], writes=[BuT[k]])
                P.dma('pool', lambda e, ti=ti, k=k: e.dma_start(out=Uv[:, :, ti * 128:(ti + 1) * 128], in_=uT[k][:]),
                      reads=[BuT[k]], writes=[BU])
            P.flush()
        nc.all_engine_barrier()
        phase_na(nc, I, Uv, gin, identb)
        nc.all_engine_barrier()
        phase_rw(nc, I, Uv, gin, identb, bones, dbgW, dbgF)
        nc.all_engine_barrier()
        phase_tail(nc, I, gin, gout, out, modbc, identb, U, dbgU, dbgG)
    return nc


def na_blocks():
    res = []
    for m in range(128):
        if m == 0:
            res.append([(kt, 5 + kt) for kt in range(4)])
        elif m == 1:
            res.append([(kt, 9 + kt) for kt in range(4)])
        elif m == 126:
            res.append([(124 + j, 13 + j) for j in range(4)])
        elif m == 127:
            res.append([(124 + j, 17 + j) for j in range(4)])
        else:
            res.append([(m + dl, dl + 2) for dl in range(-2, 3)])
    return res


def phase_na(nc, I, Uv, gin, identb):
    with contextlib.ExitStack() as st:
        P = Prog(nc)

        def sb(name, shape, dt):
            return st.enter_context(nc.sbuf_tensor('S%d_' % P.uid + name, shape, dt))

        def ps(name, shape, dt):
            return st.enter_context(nc.psum_tensor('P%d_' % P.uid + name, shape, dt))
        NTI = NT // 128
        qT = sb("qT", [128, NT], BF16)
        kT = sb("kT", [128, NT], BF16)
        vS = sb("vS", [128, NTI, 2, 65], BF16)
        biasS = sb("biasS", [128, 2, 21, 128], F32)
        wst = sb("wst", [128, 8, 384], F32)
        wb = sb("wnab", [128, 8, 384], BF16)
        gqk = sb("gqk", [128, 2, 64], F32)
        Bq, Bk, Bv, Bbias, Bw, Bg = Buf(), Buf(), Buf(), Buf(), Buf(), Buf()
        Bid = Buf()
        P.dma('sp', lambda e: e.dma_start(out=wst[:], in_=I["w_na"].rearrange("(kc p) n -> p kc n", p=128)), writes=[Bw])
        P.op('dve', lambda e: e.tensor_copy(out=wb[:], in_=wst[:]), reads=[Bw], writes=[Bw])
        P.dma('sp', lambda e: e.dma_start(out=biasS[:], in_=I["bias"]), writes=[Bbias])
        P.dma('sp', lambda e: e.dma_start(out=gqk[:], in_=I["gqk_bc"]), writes=[Bg])
        P.op('pool', lambda e: e.memset(vS[:, :, :, 64:65], 1.0), writes=[Bv])
        NB = 3
        uT = [sb(f"nuT{i}", [128, 8, 128], BF16) for i in range(NB)]
        BuT = [Buf() for _ in range(NB)]
        pp = [ps(f"npp{i}", [128, 512], F32) for i in range(2)]
        nbf = ps("nbf", [128, 1024], BF16)
        Bpp = [Buf() for _ in range(2)]
        sq = sb("nsq", [128, 256], F32)
        ssq = sb("nssq", [128, 4], F32)
        qkn = [sb(f"qkn{i}", [128, 256], BF16) for i in range(2)]
        Bsq, Bssq = Buf(), Buf()
        Bqkn = [Buf() for _ in range(2)]
        ptq = [nbf[:, 256 * i:256 * i + 256].rearrange("p (a b) -> p a b", b=128) for i in range(2)]
        Bnbf = Buf()
        Bptq = [Bnbf, Bnbf]
        for ti in range(NTI):
            k = ti % NB
            k2 = ti % 2
            P.dma('sp', lambda e, ti=ti, k=k: e.dma_start(out=uT[k][:], in_=Uv[:, :, ti * 128:(ti + 1) * 128]), writes=[BuT[k]])
            for kc in range(8):
                P.op('pe', lambda e, k=k, k2=k2, kc=kc: e.matmul(pp[k2][:, 0:384], lhsT=uT[k][:, kc, :], rhs=wb[:, kc, :],
                                                                start=(kc == 0), stop=(kc == 7)),
                     reads=[BuT[k], Bw], writes=[Bpp[k2]])
            P.op('act', lambda e, k2=k2: e.activation(out=sq[:], in_=pp[k2][:, 0:256], func=AF.Square), reads=[Bpp[k2]], writes=[Bsq])
            P.op('dve', lambda e: e.tensor_reduce(out=ssq[:], in_=sq[:].rearrange("p (a b) -> p a b", b=64), axis=AX.X, op=ALU.add),
                 reads=[Bsq], writes=[Bssq])
            P.op('dve', lambda e: e.tensor_scalar(out=ssq[:, 0:2], in0=ssq[:, 0:2], scalar1=64 * EPS, scalar2=None,
                                                  op0=ALU.add), reads=[Bssq], writes=[Bssq])
            P.op('dve', lambda e: e.tensor_scalar(out=ssq[:, 2:4], in0=ssq[:, 2:4], scalar1=1.0 / 64, scalar2=EPS,
                                                  op0=ALU.mult, op1=ALU.add), reads=[Bssq], writes=[Bssq])
            P.op('act', lambda e: e.activation(out=ssq[:], in_=ssq[:], func=AF.Sqrt), reads=[Bssq], writes=[Bssq])
            P.op('dve', lambda e: e.reciprocal(out=ssq[:], in_=ssq[:]), reads=[Bssq], writes=[Bssq])
            for j in range(4):
                P.op('dve', lambda e, k2=k2, j=j: e.scalar_tensor_tensor(
                    out=qkn[k2][:, j * 64:(j + 1) * 64], in0=pp[k2][:, j * 64:(j + 1) * 64], scalar=ssq[:, j:j + 1],
                    in1=gqk[:, j // 2, :], op0=ALU.mult, op1=ALU.mult),
                    reads=[Bpp[k2], Bssq, Bg], writes=[Bqkn[k2]])
            P.op('act', lambda e, k2=k2, ti=ti: e.activation(out=vS[:, ti, :, 0:64],
                                                             in_=pp[k2][:, 256:384].rearrange("p (h d) -> p h d", d=64),
                                                             func=AF.Copy), reads=[Bpp[k2]], writes=[Bv])
            for j in range(2):
                P.op('pe', lambda e, k2=k2, j=j: e.transpose(out=ptq[k2][:, j, :], in_=qkn[k2][:, j * 128:(j + 1) * 128],
                                                             identity=identb[:]), reads=[Bqkn[k2], Bid], writes=[Bptq[k2]])
            P.op('act', lambda e, k2=k2, ti=ti: e.activation(out=qT[:, ti * 128:(ti + 1) * 128], in_=ptq[k2][:, 0, :], func=AF.Copy),
                 reads=[Bptq[k2]], writes=[Bq])
            P.op('dve', lambda e, k2=k2, ti=ti: e.tensor_copy(out=kT[:, ti * 128:(ti + 1) * 128], in_=ptq[k2][:, 1, :]),
                 reads=[Bptq[k2]], writes=[Bk])
        pS = [ps(f"pS{i}", [128, 8, 128], F32) for i in range(2)]
        BpS = [Buf() for _ in range(2)]
        sS = [sb(f"sS{i}", [128, 5, 128], F32) for i in range(2)]
        BsS = [Buf() for _ in range(2)]
        pT = [sb(f"pT{i}", [128, 7, 128], BF16) for i in range(2)]
        BpT = [Buf() for _ in range(2)]
        pOb = ps("pOb", [128, 512], F32)
        pO = [pOb[:, 256 * i:256 * i + 130].rearrange("p (a b) -> p a b", b=65) for i in range(2)]
        _b = Buf()
        BpO = [_b, _b]
        rc = [sb(f"rc{i}", [128, 2], F32) for i in range(2)]
        Brc = [Buf() for _ in range(2)]
        oS = [sb(f"oS{i}", [128, 128], BF16) for i in range(2)]
        BoS = [Buf() for _ in range(2)]
        pOT = [nbf[:, 512:640]]
        BpOT = [Bnbf]
        oT = [sb(f"oT{i}", [128, 128], BF16) for i in range(2)]
        BoT = [Buf() for _ in range(2)]
        Bgin = Buf()
        blocks = na_blocks()
        it = 0
        for m in range(128):
            kl = blocks[m]
            nk = len(kl)
            qc0 = (m + 2) * 128
            mb = m % 2
            for h in range(2):
                x2 = it % 2
                it += 1
                hs = slice(64 * h, 64 * h + 64)
                tiles = [kt + 2 for kt, _ in kl] + [0, 1]
                for j, tt in enumerate(tiles):
                    P.op('pe', lambda e, x2=x2, j=j, tt=tt, hs=hs, qc0=qc0: e.matmul(
                        pS[x2][:, j, :], lhsT=kT[hs, tt * 128:(tt + 1) * 128], rhs=qT[hs, qc0:qc0 + 128], start=True, stop=True),
                        reads=[Bq, Bk], writes=[BpS[x2]])
                s0 = kl[0][1]
                P.op('dve', lambda e, x2=x2, nk=nk, h=h, s0=s0: e.tensor_tensor(
                    out=sS[x2][:, 0:nk, :], in0=pS[x2][:, 0:nk, :], in1=biasS[:, h, s0:s0 + nk, :], op=ALU.add),
                    reads=[BpS[x2], Bbias], writes=[BsS[x2]])
                P.op('act', lambda e, x2=x2, nk=nk: e.activation(out=pT[x2][:, 0:nk, :], in_=sS[x2][:, 0:nk, :], func=AF.Exp),
                     reads=[BsS[x2]], writes=[BpT[x2]])
                P.op('act', lambda e, x2=x2, nk=nk: e.activation(out=pT[x2][:, nk:nk + 2, :], in_=pS[x2][:, nk:nk + 2, :], func=AF.Exp),
                     reads=[BpS[x2]], writes=[BpT[x2]])
                for j, tt in enumerate(tiles):
                    P.op('pe', lambda e, x2=x2, j=j, tt=tt, h=h, mb=mb, n=len(tiles): e.matmul(
                        pO[mb][:, h, :], lhsT=pT[x2][:, j, :], rhs=vS[:, tt, h, :], start=(j == 0), stop=(j == n - 1)),
                        reads=[BpT[x2], Bv], writes=[BpO[mb]])
            P.op('dve', lambda e, mb=mb: e.reciprocal(out=rc[mb][:], in_=pO[mb][:, :, 64]), reads=[BpO[mb]], writes=[Brc[mb]])
            for h in range(2):
                P.op('dve', lambda e, mb=mb, h=h: e.tensor_scalar(out=oS[mb][:, h * 64:(h + 1) * 64], in0=pO[mb][:, h, 0:64],
                                                                   scalar1=rc[mb][:, h:h + 1], scalar2=None, op0=ALU.mult),
                     reads=[BpO[mb], Brc[mb]], writes=[BoS[mb]])
            P.op('pe', lambda e, mb=mb: e.transpose(out=pOT[0][:, :], in_=oS[mb][:, :], identity=identb[:]),
                 reads=[BoS[mb]], writes=[BpOT[0]])
            P.op('act', lambda e, mb=mb: e.activation(out=oT[mb][:], in_=pOT[0][:, :], func=AF.Copy),
                 reads=[BpOT[0]], writes=[BoT[mb]])
            P.dma('pool', lambda e, mb=mb, m=m: e.dma_start(out=gin.ap()[m // 16, 0:128, (m % 16) * 128:(m % 16 + 1) * 128], in_=oT[mb][:]),
                  reads=[BoT[mb]], writes=[Bgin])
        P.flush()


def phase_rw(nc, I, Uv, gin, identb, bones, dbgW=None, dbgF=None):
    with contextlib.ExitStack() as st:
        P = Prog(nc)

        def sb(name, shape, dt):
            return st.enter_context(nc.sbuf_tensor('S%d_' % P.uid + name, shape, dt))

        def ps(name, shape, dt):
            return st.enter_context(nc.psum_tensor('P%d_' % P.uid + name, shape, dt))
        wb = sb("rwb", [128, 8, 704], BF16)
        mu = sb("mu", [128, 2, 6], F32)
        c0 = sb("c0", [128, 6], F32)
        cv = sb("cv", [128, 9], F32)
        omka = sb("omka", [128, 1], F32)
        omka2 = sb("omka2", [128, 1], F32)
        wupf = sb("wupf", [128, 128], F32)
        wupb = sb("wupb", [128, 128], BF16)
        aupf = sb("aupf", [64, 2, 128], F32)
        aupb = sb("aupb", [64, 2, 128], BF16)
        gupf = sb("gupf", [128, 128], F32)
        gupb = sb("gupb", [128, 128], BF16)
        mk = sb("mk", [128, 2, 4, 2, 128], F32)
        mst = sb("mst", [128, 2, 4, 64], F32)
        mkd = sb("mkd", [128, 2, 4, 2, 64], F32)
        istack = sb("istack", [128, 4, 64], F32)
        reset = sb("reset", [128, 256], F32)
        bavg = sb("bavg", [128, 128], F32)
        wkvT = sb("wkvT", [128, T], F32)
        wst = wkvT[:, 0:8 * 704].rearrange("p (a b) -> p a b", b=704)
        Bc = Buf("consts")
        Bwkv = Buf("wkv")
        for dst, src in [(None, I["w_rw"].rearrange("(kc p) n -> p kc n", p=128)), (mu, I["muT"]), (cv, I["cv"]), (wupf, I["w_upT"]),
                         (aupf, I["a_upT"]), (gupf, I["g_up"]), (mk, I["c_mk"]), (mst, I["c_mst"]), (mkd, I["c_mkd"]), (istack, I["c_istack"]),
                         (reset, I["c_reset"])]:
            if dst is None:
                P.dma('sp', lambda e, src=src: e.dma_start(out=wst, in_=src), writes=[Bc, Bwkv])
            else:
                P.dma('sp', lambda e, dst=dst, src=src: e.dma_start(out=dst[:], in_=src), writes=[Bc])
        P.op('dve', lambda e: e.tensor_copy(out=wb[:], in_=wst), reads=[Bc, Bwkv], writes=[Bc])
        P.op('dve', lambda e: e.tensor_copy(out=wupb[:], in_=wupf[:]), reads=[Bc], writes=[Bc])
        P.op('dve', lambda e: e.tensor_copy(out=aupb[:], in_=aupf[:]), reads=[Bc], writes=[Bc])
        P.op('dve', lambda e: e.tensor_copy(out=gupb[:], in_=gupf[:]), reads=[Bc], writes=[Bc])
        P.op('dve', lambda e: e.tensor_tensor(out=c0[:], in0=mu[:, 0, :], in1=mu[:, 1, :], op=ALU.add), reads=[Bc], writes=[Bc])
        P.op('dve', lambda e: e.tensor_scalar(out=c0[:], in0=c0[:], scalar1=-1.0, scalar2=1.0, op0=ALU.mult, op1=ALU.add),
             reads=[Bc], writes=[Bc])
        P.op('dve', lambda e: e.tensor_scalar(out=omka[:], in0=cv[:, 5:6], scalar1=-1.0, scalar2=1.0, op0=ALU.mult, op1=ALU.add),
             reads=[Bc], writes=[Bc])
        P.op('dve', lambda e: e.tensor_scalar(out=omka2[:], in0=omka[:], scalar1=2.0, scalar2=None, op0=ALU.mult),
             reads=[Bc], writes=[Bc])
        P.op('dve', lambda e: e.tensor_scalar(out=bavg[:], in0=bones[:], scalar1=1.0 / 64, scalar2=None, op0=ALU.mult),
             reads=[Bc], writes=[Bc])
        NBU = 2
        uT = [sb(f"ruT{i}", [128, 8, 258], BF16) for i in range(NBU)]
        BuT = [Buf() for _ in range(NBU)]
        _ppj = ps("rpp", [128, 512], F32)
        ppj = [_ppj, _ppj]
        _bj = Buf()
        Bppj = [_bj, _bj]
        pmisc = ps("rpmisc", [128, 512], F32)
        pax = [pmisc[:, 256:512], pmisc[:, 256:512]]
        _bp = Buf()
        Bpax = [_bp, _bp]
        FT = {}

        def feat(name, dt=F32, n=2, w=256):
            FT[name] = ([sb(f"f_{name}{i}", [128, w], dt) for i in range(n)], [Buf() for _ in range(n)])
        for nm in ["t1", "r", "k", "v", "lw", "rate", "kk", "kdir", "bb", "cum", "epos", "eneg", "eprev", "tmp", "tmp2"]:
            feat(nm)
        feat("vb", BF16)
        feat("wd", BF16)
        feat("gd", BF16)
        feat("ad", BF16)
        feat("ptot", F32, 2, 4)
        FM = [sb(f"FM{i}", [128, 4, 4, 64], BF16) for i in range(2)]
        BFM = [Buf() for _ in range(2)]
        pTM = ps("pTM", [128, 4, 4, 64], BF16)
        pSM = ps("pSM", [128, 4, 2, 128], F32)
        pPOW = ps("pPOW", [128, 4, 128], F32)
        pW0 = ps("pW0", [128, 4, 128], F32)
        pWS = ps("pWS", [128, 4, 128], F32)
        pY = [pmisc[:, 0:256]]
        BpTM, BpSM, BpPOW, BpW0, BpWS = Buf(), Buf(), Buf(), Buf(), Buf()
        BpY = [_bp]
        TM = sb("TM", [128, 4, 4, 64], BF16)
        SM = sb("SM", [128, 4, 2, 128], BF16)
        L1 = sb("L1", [128, 4, 64], F32)
        LTd = sb("LTd", [128, 4, 2, 64], F32)
        POW = sb("POW", [128, 4, 4, 128], F32)
        Wf = sb("Wf", [128, 4, 128], F32)
        Cf = sb("Cf", [128, 4, 128], F32)
        Wb = sb("Wb", [128, 4, 128], BF16)
        QT = sb("QT", [128, 4, 64], BF16)
        GT = sb("GT", [128, 4, 64], BF16)
        BTM, BSM, BL1, BLTd, BWf, BCf, BWb, BQT, BGT = [Buf() for _ in range(9)]
        BPOW = [Buf() for _ in range(4)]
        Hf = sb("Hf", [128, 64], F32)
        Hb = [sb(f"Hb{i}", [128, 64], BF16) for i in range(2)]
        BHf = Buf()
        BHb = [Buf() for _ in range(2)]
        yS = sb("yS", [128, 256], F32)
        cen = sb("cen", [128, 256], F32)
        sq2 = sb("sq2", [128, 256], F32)
        rstd = sb("rstd", [128, 256], F32)
        bon = sb("bon", [128, 256], F32)
        gS = sb("gS", [128, 256], F32)
        oR = [sb(f"oR{i}", [128, 256], BF16) for i in range(2)]
        ByS, Bcen, Bsq2, Brstd, Bbon, BgS = Buf(), Buf(), Buf(), Buf(), Buf(), Buf()
        BoR = [Buf() for _ in range(2)]
        Bgin = Buf()
        hsl = [slice(0, 64), slice(64, 128)]
        state = {"hcur": 0, "tile_it": 0, "lane_it": 0}

        def F(name, k):
            a, b = FT[name]
            return a[k], b[k]

        def proj(cc, k, ku, ncol=128):
            pk = state.setdefault("pk", 0)
            state["pk"] = 1 - pk
            for kc in range(8):
                P.op('pe', lambda e, pk=pk, kc=kc, cc=cc, ku=ku, ncol=ncol: e.matmul(
                    ppj[pk][0:ncol, 0:258], lhsT=wb[:, kc, cc * 128:cc * 128 + ncol], rhs=uT[ku][:, kc, :],
                    start=(kc == 0), stop=(kc == 7)), reads=[BuT[ku], Bc], writes=[Bppj[pk]])
            return pk

        def shift(cc, pk, k, dst, Bdst, ncol=128):
            t1, Bt1 = F("t1", k)
            P.op('act', lambda e: e.activation(out=t1[0:ncol, :], in_=ppj[pk][0:ncol, 1:257], func=AF.Identity, scale=c0[0:ncol, cc:cc + 1]),
                 reads=[Bppj[pk], Bc], writes=[Bt1])
            P.op('dve', lambda e: e.scalar_tensor_tensor(out=t1[0:ncol, :], in0=ppj[pk][0:ncol, 0:256], scalar=mu[0:ncol, 0, cc:cc + 1],
                                                         in1=t1[0:ncol, :], op0=ALU.mult, op1=ALU.add),
                 reads=[Bppj[pk], Bt1, Bc], writes=[Bt1])
            P.op('dve', lambda e: e.scalar_tensor_tensor(out=dst[0:ncol, :], in0=ppj[pk][0:ncol, 2:258], scalar=mu[0:ncol, 1, cc:cc + 1],
                                                         in1=t1[0:ncol, :], op0=ALU.mult, op1=ALU.add),
                 reads=[Bppj[pk], Bt1, Bc], writes=[Bdst])

        def do_tile(d, tok0, is_ctx, first, last):
            k = state["tile_it"] % 2
            state["tile_it"] += 1
            ku = k
            lo = 0 if first else -1
            hi = 256 if last else 257
            if first:
                P.op('pool', lambda e: e.memset(uT[ku][:, :, 0:1], 0.0), writes=[BuT[ku]])
            if last:
                P.op('pool', lambda e: e.memset(uT[ku][:, :, 257:258], 0.0), writes=[BuT[ku]])
            P.dma('sp', lambda e: e.dma_start(out=uT[ku][:, :, 1 + lo:1 + hi], in_=Uv[:, :, tok0 + lo:tok0 + hi]), writes=[BuT[ku]])
            r, Br = F("r", k)
            kf, Bk = F("k", k)
            v, Bv = F("v", k)
            vb, Bvb = F("vb", k)
            wd, Bwd = F("wd", k)
            gd, Bgd = F("gd", k)
            ad, Bad = F("ad", k)
            tmp, Btmp = F("tmp", k)
            tmp2, Btmp2 = F("tmp2", k)
            lw, Blw = F("lw", k)
            rate, Brate = F("rate", k)
            kk, Bkk = F("kk", k)
            kdir, Bkdir = F("kdir", k)
            bb, Bbb = F("bb", k)
            cum, Bcum = F("cum", k)
            epos, Bepos = F("epos", k)
            eneg, Beneg = F("eneg", k)
            eprev, Beprev = F("eprev", k)
            ptot, Bptot = F("ptot", k)
            for cc, dst, Bd in [(0, r, Br), (1, kf, Bk), (2, v, Bv), (3, tmp, Btmp)]:
                pk = proj(cc, k, ku)
                shift(cc, pk, k, dst, Bd)
                if cc == 3:
                    P.op('act', lambda e: e.activation(out=wd[:], in_=tmp[:], func=AF.Tanh), reads=[Btmp], writes=[Bwd])
            pk = proj(5, k, ku, 64)
            shift(5, pk, k, tmp2, Btmp2, 64)
            P.op('act', lambda e: e.activation(out=ad[0:64, :], in_=tmp2[0:64, :], func=AF.Copy), reads=[Btmp2], writes=[Bad])
            P.op('pool', lambda e: e.tensor_copy(out=vb[:], in_=v[:]), reads=[Bv], writes=[Bvb])
            ds_ = slice(64 * d, 64 * d + 64)
            P.op('pe', lambda e: e.matmul(pax[0][:, :], lhsT=wupb[ds_, :], rhs=wd[ds_, :], start=True, stop=True),
                 reads=[Bwd, Bc], writes=[Bpax[0]])
            P.op('act', lambda e: e.activation(out=lw[:], in_=pax[0][:, :], func=AF.Sigmoid, bias=cv[:, d:d + 1]),
                 reads=[Bpax[0], Bc], writes=[Blw])
            P.op('pool', lambda e: e.tensor_scalar(out=lw[:], in0=lw[:], scalar1=-0.6065306597126334, scalar2=None, op0=ALU.mult),
                 reads=[Blw], writes=[Blw])
            P.op('pe', lambda e: e.matmul(pax[1][:, :], lhsT=aupb[:, d, :], rhs=ad[0:64, :], start=True, stop=True),
                 reads=[Bad, Bc], writes=[Bpax[1]])
            P.op('act', lambda e: e.activation(out=rate[:], in_=pax[1][:, :], func=AF.Sigmoid, bias=cv[:, 2 + d:3 + d]),
                 reads=[Bpax[1], Bc], writes=[Brate])
            P.op('dve', lambda e: e.tensor_scalar(out=kk[:], in0=kf[:], scalar1=cv[:, 4:5], scalar2=None, op0=ALU.mult),
                 reads=[Bk, Bc], writes=[Bkk])
            P.op('pool', lambda e: e.tensor_tensor(out=tmp[:], in0=kk[:], in1=kk[:], op=ALU.mult), reads=[Bkk], writes=[Btmp])
            P.op('pe', lambda e: e.matmul(pax[0][:, :], lhsT=bones[:], rhs=tmp[:], start=True, stop=True),
                 reads=[Btmp, Bc], writes=[Bpax[0]])
            P.op('dve', lambda e: e.tensor_scalar(out=tmp2[:], in0=pax[0][:, :], scalar1=1e-24, scalar2=None, op0=ALU.max),
                 reads=[Bpax[0]], writes=[Btmp2])
            P.op('act', lambda e: e.activation(out=tmp2[:], in_=tmp2[:], func=AF.Sqrt), reads=[Btmp2], writes=[Btmp2])
            P.op('dve', lambda e: e.reciprocal(out=tmp2[:], in_=tmp2[:]), reads=[Btmp2], writes=[Btmp2])
            P.op('dve', lambda e: e.tensor_tensor(out=kk[:], in0=kk[:], in1=tmp2[:], op=ALU.mult), reads=[Bkk, Btmp2], writes=[Bkk])
            P.op('dve', lambda e: e.tensor_scalar(out=kdir[:], in0=rate[:], scalar1=cv[:, 5:6], scalar2=omka[:, 0:1],
                                                  op0=ALU.mult, op1=ALU.add), reads=[Brate, Bc], writes=[Bkdir])
            P.op('pool', lambda e: e.tensor_tensor(out=kdir[:], in0=kdir[:], in1=kf[:], op=ALU.mult), reads=[Bkdir, Bk], writes=[Bkdir])
            P.op('pool', lambda e: e.tensor_tensor(out=bb[:], in0=kk[:], in1=rate[:], op=ALU.mult), reads=[Bkk, Brate], writes=[Bbb])
            P.op('dve', lambda e: e.tensor_tensor_scan(out=cum[:], data0=reset[:], data1=lw[:], initial=0.0, op0=ALU.mult, op1=ALU.add),
                 reads=[Blw, Bc], writes=[Bcum])
            if d == 1:
                for c in range(4):
                    cs = slice(64 * c, 64 * c + 64)
                    P.op('dve', lambda e, c=c, cs=cs: e.tensor_scalar(out=tmp[:, cs], in0=cum[:, cs], scalar1=-1.0,
                                                                      scalar2=cum[:, 64 * c + 63:64 * c + 64],
                                                                      op0=ALU.mult, op1=ALU.add), reads=[Bcum], writes=[Btmp])
                P.op('dve', lambda e: e.tensor_tensor(out=cum[:], in0=tmp[:], in1=lw[:], op=ALU.add), reads=[Btmp, Blw], writes=[Bcum])
            P.op('act', lambda e: e.activation(out=epos[:], in_=cum[:], func=AF.Exp), reads=[Bcum], writes=[Bepos])
            P.op('act', lambda e: e.activation(out=eneg[:], in_=cum[:], func=AF.Exp, scale=-1.0), reads=[Bcum], writes=[Beneg])
            P.op('pool', lambda e: e.tensor_tensor(out=tmp2[:], in0=cum[:], in1=lw[:], op=ALU.subtract), reads=[Bcum, Blw], writes=[Btmp2])
            P.op('act', lambda e: e.activation(out=eprev[:], in_=tmp2[:], func=AF.Exp), reads=[Btmp2], writes=[Beprev])
            last_col = 63 if d == 0 else 0
            P.op('pool', lambda e: e.tensor_copy(out=ptot[:, 0:4], in_=epos[:].rearrange("p (c t) -> p c t", t=64)[:, :, last_col]),
                 reads=[Bepos], writes=[Bptot])
            fm = FM[k]
            v3 = lambda a: a[:].rearrange("p (c t) -> p c t", t=64)
            P.op('dve', lambda e: e.tensor_tensor(out=fm[:, :, 0, :], in0=v3(bb), in1=v3(eneg), op=ALU.mult),
                 reads=[Bbb, Beneg], writes=[BFM[k]])
            P.op('pool', lambda e: e.tensor_tensor(out=fm[:, :, 1, :], in0=v3(kdir), in1=v3(eneg), op=ALU.mult),
                 reads=[Bkdir, Beneg], writes=[BFM[k]])
            P.op('dve', lambda e: e.scalar_tensor_tensor(out=fm[:, :, 2, :], in0=v3(kk), scalar=-1.0, in1=v3(eprev),
                                                         op0=ALU.mult, op1=ALU.mult), reads=[Bkk, Beprev], writes=[BFM[k]])
            P.op('pool', lambda e: e.tensor_tensor(out=fm[:, :, 3, :], in0=v3(r), in1=v3(epos), op=ALU.mult),
                 reads=[Br, Bepos], writes=[BFM[k]])
            if dbgF is not None and d == 0 and tok0 == TC:
                for j, (a_, b_) in enumerate([(r, Br), (kf, Bk), (v, Bv), (lw, Blw), (rate, Brate), (kk, Bkk), (kdir, Bkdir), (bb, Bbb),
                                              (cum, Bcum), (epos, Bepos), (eneg, Beneg), (eprev, Beprev)]):
                    P.dma('sp', lambda e, j=j, a_=a_: e.dma_start(out=dbgF[j], in_=a_[:]), reads=[b_])
            do_chunks(d, k, is_ctx, vb, Bvb, ptot, Bptot)
            if is_ctx:
                return
            lt0 = tok0 - TC
            if d == 0:
                P.op('act', lambda e: e.activation(out=wkvT[:, lt0:lt0 + 256], in_=pY[0][:, :], func=AF.Copy),
                     reads=[BpY[0]], writes=[Bwkv])
                return
            P.op('dve', lambda e: e.tensor_tensor(out=yS[:], in0=pY[0][:, :], in1=wkvT[:, lt0:lt0 + 256], op=ALU.add),
                 reads=[BpY[0], Bwkv], writes=[ByS])
            P.op('pe', lambda e: e.matmul(pax[0][:, :], lhsT=bavg[:], rhs=yS[:], start=True, stop=True),
                 reads=[ByS, Bc], writes=[Bpax[0]])
            P.op('dve', lambda e: e.tensor_tensor(out=cen[:], in0=yS[:], in1=pax[0][:, :], op=ALU.subtract),
                 reads=[ByS, Bpax[0]], writes=[Bcen])
            P.op('pool', lambda e: e.tensor_tensor(out=sq2[:], in0=cen[:], in1=cen[:], op=ALU.mult), reads=[Bcen], writes=[Bsq2])
            P.op('pe', lambda e: e.matmul(pax[1][:, :], lhsT=bavg[:], rhs=sq2[:], start=True, stop=True),
                 reads=[Bsq2, Bc], writes=[Bpax[1]])
            P.op('dve', lambda e: e.tensor_scalar(out=rstd[:], in0=pax[1][:, :], scalar1=64e-5, scalar2=None, op0=ALU.add),
                 reads=[Bpax[1]], writes=[Brstd])
            P.op('act', lambda e: e.activation(out=rstd[:], in_=rstd[:], func=AF.Sqrt), reads=[Brstd], writes=[Brstd])
            P.op('dve', lambda e: e.reciprocal(out=rstd[:], in_=rstd[:]), reads=[Brstd], writes=[Brstd])
            P.op('dve', lambda e: e.tensor_tensor(out=cen[:], in0=cen[:], in1=rstd[:], op=ALU.mult), reads=[Bcen, Brstd], writes=[Bcen])
            P.op('dve', lambda e: e.tensor_scalar(out=cen[:], in0=cen[:], scalar1=cv[:, 7:8], scalar2=cv[:, 8:9], op0=ALU.mult, op1=ALU.add),
                 reads=[Bcen, Bc], writes=[Bcen])
            P.op('pe', lambda e: e.matmul(pax[0][:, :], lhsT=aupb[:, 0, :], rhs=ad[0:64, :], start=True, stop=True),
                 reads=[Bad, Bc, Bcen], writes=[Bpax[0]])
            P.op('act', lambda e: e.activation(out=tmp[:], in_=pax[0][:, :], func=AF.Sigmoid, bias=cv[:, 2:3]),
                 reads=[Bpax[0], Bc], writes=[Btmp])
            P.op('dve', lambda e: e.tensor_tensor(out=tmp[:], in0=tmp[:], in1=rate[:], op=ALU.add), reads=[Btmp, Brate], writes=[Btmp])
            P.op('dve', lambda e: e.tensor_scalar(out=tmp[:], in0=tmp[:], scalar1=cv[:, 5:6], scalar2=None, op0=ALU.mult),
                 reads=[Btmp, Bc], writes=[Btmp])
            P.op('dve', lambda e: e.tensor_scalar(out=tmp[:], in0=tmp[:], scalar1=omka2[:, 0:1], scalar2=None, op0=ALU.add),
                 reads=[Btmp, Bc], writes=[Btmp])
            P.op('dve', lambda e: e.tensor_tensor(out=tmp[:], in0=tmp[:], in1=kf[:], op=ALU.mult), reads=[Btmp, Bk], writes=[Btmp])
            P.op('dve', lambda e: e.scalar_tensor_tensor(out=tmp[:], in0=tmp[:], scalar=cv[:, 6:7], in1=r[:], op0=ALU.mult, op1=ALU.mult),
                 reads=[Btmp, Br, Bc], writes=[Btmp])
            P.op('pe', lambda e: e.matmul(pax[1][:, :], lhsT=bones[:], rhs=tmp[:], start=True, stop=True),
                 reads=[Btmp, Bc, Brstd], writes=[Bpax[1]])
            P.op('dve', lambda e: e.tensor_tensor(out=bon[:], in0=pax[1][:, :], in1=v[:], op=ALU.mult), reads=[Bpax[1], Bv], writes=[Bbon])
            P.op('dve', lambda e: e.tensor_tensor(out=cen[:], in0=cen[:], in1=bon[:], op=ALU.add), reads=[Bcen, Bbon], writes=[Bcen])
            pk = proj(4, k, ku)
            shift(4, pk, k, tmp2, Btmp2)
            P.op('act', lambda e: e.activation(out=gd[:], in_=tmp2[:], func=AF.Sigmoid), reads=[Btmp2], writes=[Bgd])
            P.op('pe', lambda e: e.matmul(pax[0][:, :], lhsT=gupb[:], rhs=gd[:], start=True, stop=True),
                 reads=[Bgd, Bc], writes=[Bpax[0]])
            P.op('dve', lambda e: e.tensor_tensor(out=oR[k][:], in0=cen[:], in1=pax[0][:, :], op=ALU.mult),
                 reads=[Bcen, Bpax[0]], writes=[BoR[k]])
            P.dma('pool', lambda e: e.dma_start(out=gin.ap()[lt0 // 2048, 128:256, lt0 % 2048:lt0 % 2048 + 256], in_=oR[k][:]), reads=[BoR[k]], writes=[Bgin])

        def do_chunks(d, k, is_ctx, vb, Bvb, ptot, Bptot):
            fm = FM[k]
            Bfm = BFM[k]
            CH = [(c, h, hsl[h]) for c in range(4) for h in range(2)]
            for c, h, hs in CH:
                for j, src in enumerate([fm[hs, c, 2, :], fm[hs, c, 0, :], fm[hs, c, 1, :], vb[hs, 64 * c:64 * c + 64]]):
                    P.op('pe', lambda e, hs=hs, c=c, j=j, src=src: e.transpose(out=pTM[hs, c, j, :], in_=src, identity=identb[hs, hs]),
                         reads=[Bfm, Bvb], writes=[BpTM])
            P.op('act', lambda e: e.activation(out=TM[:], in_=pTM[:], func=AF.Copy), reads=[BpTM], writes=[BTM])
            for c, h, hs in CH:
                for j in range(2):
                    P.op('pe', lambda e, hs=hs, c=c, j=j: e.matmul(pSM[hs, c, j, :], lhsT=fm[hs, c, j, :],
                                                                   rhs=fm[hs, c, 2:4, :], start=True, stop=True),
                         reads=[Bfm], writes=[BpSM])
            P.op('dve', lambda e: e.tensor_tensor(out=SM[:], in0=pSM[:], in1=mk[:, d], op=ALU.mult),
                 reads=[BpSM, Bc], writes=[BSM])
            for jj in range(2):
                P.op('dve', lambda e, jj=jj: e.tensor_tensor(out=LTd[:, :, jj, :], in0=pSM[:, :, 0, 0:64],
                                                             in1=mkd[:, d, :, jj, :], op=ALU.mult),
                     reads=[BpSM, Bc], writes=[BLTd])
            for c, h, hs in CH:
                P.op('pe', lambda e, hs=hs, c=c: e.matmul(pWS[hs, c, 0:64], lhsT=fm[hs, c, 2, :], rhs=fm[hs, c, 0, :], start=True, stop=True),
                     reads=[Bfm], writes=[BpWS])
            P.op('dve', lambda e: e.tensor_tensor(out=L1[:], in0=pWS[:, :, 0:64], in1=mst[:, d], op=ALU.mult),
                 reads=[BpWS, Bc], writes=[BL1])
            for lvl in range(4):
                if lvl == 0:
                    LTp = lambda hs, c: LTd[hs, c, 0, :]
                    Lp = lambda hs, c: L1[hs, c, :]
                    rd = [BLTd, BL1]
                else:
                    LTp = lambda hs, c, lvl=lvl: POW[hs, lvl - 1, c, 0:64]
                    Lp = lambda hs, c, lvl=lvl: POW[hs, lvl - 1, c, 64:128]
                    rd = [BPOW[lvl - 1]]
                for c, h, hs in CH:
                    P.op('pe', lambda e, hs=hs, c=c, LTp=LTp, Lp=Lp: e.matmul(pPOW[hs, c, 0:64], lhsT=Lp(hs, c), rhs=LTp(hs, c), start=True, stop=True),
                         reads=rd, writes=[BpPOW])
                    P.op('pe', lambda e, hs=hs, c=c, LTp=LTp, Lp=Lp: e.matmul(pPOW[hs, c, 64:128], lhsT=LTp(hs, c), rhs=Lp(hs, c), start=True, stop=True),
                         reads=rd, writes=[BpPOW])
                if lvl % 2 == 0:
                    P.op('act', lambda e, lvl=lvl: e.activation(out=POW[:, lvl], in_=pPOW[:], func=AF.Copy),
                         reads=[BpPOW], writes=[BPOW[lvl]])
                else:
                    P.op('dve', lambda e, lvl=lvl: e.tensor_copy(out=POW[:, lvl], in_=pPOW[:]),
                         reads=[BpPOW], writes=[BPOW[lvl]])
            for c, h, hs in CH:
                P.op('pe', lambda e, hs=hs, c=c: e.matmul(pW0[hs, c, 64:128], lhsT=SM[hs, c, 1, 0:64], rhs=TM[hs, c, 3, :], start=True, stop=True),
                     reads=[BSM, BTM], writes=[BpW0])
            P.op('pool', lambda e: e.tensor_copy(out=Wf[:, :, 0:64], in_=TM[:, :, 0, :]), reads=[BTM], writes=[BWf])
            P.op('dve', lambda e: e.tensor_copy(out=Wf[:, :, 64:128], in_=pW0[:, :, 64:128]), reads=[BpW0], writes=[BWf])

            def doubling(Xf, BX):
                for lvl in range(5):
                    if lvl == 0:
                        LTp = lambda hs, c: LTd[hs, c, 0, :]
                        rd = [BLTd]
                    else:
                        LTp = lambda hs, c, lvl=lvl: POW[hs, lvl - 1, c, 0:64]
                        rd = [BPOW[lvl - 1]]
                    for c, h, hs in CH:
                        P.op('pe', lambda e, hs=hs, c=c, LTp=LTp: e.matmul(pWS[hs, c, :], lhsT=LTp(hs, c), rhs=Xf[hs, c, :], start=True, stop=True),
                             reads=rd + [BX], writes=[BpWS])
                    P.op('dve', lambda e: e.tensor_tensor(out=Xf[:], in0=Xf[:], in1=pWS[:], op=ALU.add),
                         reads=[BX, BpWS], writes=[BX])
            doubling(Wf, BWf)
            for c, h, hs in CH:
                P.op('pe', lambda e, hs=hs, c=c: e.matmul(pW0[hs, c, :], lhsT=LTd[hs, c, 1, :], rhs=Wf[hs, c, :], start=True, stop=True),
                     reads=[BLTd, BWf], writes=[BpW0])
            P.op('act', lambda e: e.activation(out=Cf[:], in_=pW0[:], func=AF.Copy), reads=[BpW0], writes=[BCf])
            doubling(Cf, BCf)
            P.op('dve', lambda e: e.tensor_tensor(out=Wf[:], in0=Wf[:], in1=Cf[:], op=ALU.add),
                 reads=[BWf, BCf], writes=[BWf])
            P.op('act', lambda e: e.activation(out=Wb[:], in_=Wf[:], func=AF.Copy), reads=[BWf], writes=[BWb])
            for c, h, hs in CH:
                if not is_ctx:
                    P.op('pe', lambda e, hs=hs, c=c: e.matmul(pPOW[hs, c, 0:64], lhsT=Wb[hs, c, 0:64], rhs=SM[hs, c, 0, 64:128], start=True, stop=True),
                         reads=[BWb, BSM], writes=[BpPOW])
                P.op('pe', lambda e, hs=hs, c=c: e.matmul(pPOW[hs, c, 64:128], lhsT=Wb[hs, c, 0:64], rhs=TM[hs, c, 1, :], start=True, stop=True),
                     reads=[BWb, BTM], writes=[BpPOW])
            if not is_ctx:
                P.op('dve', lambda e: e.tensor_tensor(out=QT[:], in0=pPOW[:, :, 0:64], in1=fm[:, :, 3, :], op=ALU.add),
                     reads=[BpPOW, Bfm], writes=[BQT])
            P.op('dve', lambda e: e.tensor_tensor(out=GT[:], in0=pPOW[:, :, 64:128], in1=istack[:], op=ALU.add),
                 reads=[BpPOW, Bc], writes=[BGT])
            corder = range(4) if d == 0 else range(3, -1, -1)
            for c in corder:
                hc = state["hcur"]
                hn = 1 - hc
                if not is_ctx:
                    for h in range(2):
                        hs = hsl[h]
                        ysl = slice(64 * c, 64 * c + 64)
                        P.op('pe', lambda e, hs=hs, ysl=ysl, c=c: e.matmul(pY[0][hs, ysl], lhsT=Wb[hs, c, 64:128], rhs=SM[hs, c, 0, 64:128], start=True, stop=False),
                             reads=[BWb, BSM], writes=[BpY[0]])
                        P.op('pe', lambda e, hs=hs, ysl=ysl, c=c: e.matmul(pY[0][hs, ysl], lhsT=TM[hs, c, 3, :], rhs=SM[hs, c, 1, 64:128], start=False, stop=False),
                             reads=[BTM, BSM], writes=[BpY[0]])
                        P.op('pe', lambda e, hs=hs, ysl=ysl, c=c, hc=hc: e.matmul(pY[0][hs, ysl], lhsT=Hb[hc][hs, :], rhs=QT[hs, c, :], start=False, stop=True),
                             reads=[BHb[hc], BQT], writes=[BpY[0]])
                for h in range(2):
                    hs = hsl[h]
                    P.op('pe', lambda e, hs=hs, c=c: e.matmul(pW0[hs, c, 0:64], lhsT=TM[hs, c, 1, :], rhs=Wb[hs, c, 64:128], start=True, stop=False),
                         reads=[BTM, BWb], writes=[BpW0])
                    P.op('pe', lambda e, hs=hs, c=c: e.matmul(pW0[hs, c, 0:64], lhsT=TM[hs, c, 2, :], rhs=TM[hs, c, 3, :], start=False, stop=False),
                         reads=[BTM], writes=[BpW0])
                    P.op('pe', lambda e, hs=hs, c=c, hc=hc: e.matmul(pW0[hs, c, 0:64], lhsT=GT[hs, c, :], rhs=Hb[hc][hs, :], start=False, stop=True),
                         reads=[BGT, BHb[hc]], writes=[BpW0])
                P.op('act', lambda e, c=c: e.activation(out=Hf[:], in_=pW0[:, c, 0:64], func=AF.Identity, scale=ptot[:, c:c + 1]),
                     reads=[BpW0, Bptot], writes=[BHf])
                P.op('dve', lambda e, hn=hn: e.tensor_copy(out=Hb[hn][:], in_=Hf[:]), reads=[BHf], writes=[BHb[hn]])
                state["hcur"] = hn

        for d in range(2):
            if d == 1 and dbgW is not None:
                for j in range(8):
                    P.dma('sp', lambda e, j=j: e.dma_start(out=dbgW[:, j * 2048:(j + 1) * 2048], in_=wkvT[:, j * 2048:(j + 1) * 2048]),
                          reads=[Bwkv])
            P.op('dve', lambda e: e.memset(Hb[state["hcur"]][:], 0.0), writes=[BHb[state["hcur"]]])
            do_tile(d, 0, True, True, True)
            order = range(64) if d == 0 else range(63, -1, -1)
            for ti in order:
                do_tile(d, TC + ti * 256, False, ti == 0, ti == 63)
        P.flush()


def phase_tail(nc, I, gin, gout, out, modbc, identb, U=None, dbgU=None, dbgG=None):
    with contextlib.ExitStack() as st:
        P = Prog(nc)

        def sb(name, shape, dt):
            return st.enter_context(nc.sbuf_tensor('S%d_' % P.uid + name, shape, dt))

        def ps(name, shape, dt):
            return st.enter_context(nc.psum_tensor('P%d_' % P.uid + name, shape, dt))
        Bg = Buf()
        if dbgU is not None:
            for j in range(8):
                P.dma('sp', lambda e, j=j: e.dma_start(out=dbgU[j * 128:(j + 1) * 128, :], in_=U[j * 128:(j + 1) * 128, :]))
                P.dma('sp', lambda e, j=j: e.dma_start(out=dbgG[j], in_=gin.ap()[j]))
        for gi in range(8):
            P.dma('pool', lambda e, gi=gi: e.collective_compute("AllGather", ALU.bypass, replica_groups=[[0, 1, 2, 3], [4, 5, 6, 7]],
                                                                 ins=[gin.ap()[gi].opt()], outs=[gout.ap()[gi].opt()]), writes=[Bg], inc=1)
        wo = sb("wo", [128, 8, D], BF16)
        w1 = sb("w1", [128, 8, 5632], BF16)
        w2 = sb("w2", [128, 22, D], BF16)
        stg = [sb(f"stg{i}", [128, 1024], F32) for i in range(2)]
        Bstg = [Buf() for _ in range(2)]
        Bw = Buf()
        si = 0
        jobs = []
        wov = I["w_out"].rearrange("(kc p) n -> p kc n", p=128)
        w1v = I["ffn_w_in"].rearrange("(kc p) n -> p kc n", p=128)
        w2v = I["ffn_w_out"].rearrange("(kc p) n -> p kc n", p=128)
        for kc in range(0, 8, 2):
            pass
        for kc in range(8):
            jobs.append((wov[:, kc:kc + 1, :], wo[:, kc:kc + 1, :], 1, D))
        for kc in range(8):
            for n0 in range(0, 5632, 1024):
                n1 = min(5632, n0 + 1024)
                jobs.append((w1v[:, kc:kc + 1, n0:n1], w1[:, kc:kc + 1, n0:n1], 1, n1 - n0))
        for kc in range(22):
            jobs.append((w2v[:, kc:kc + 1, :], w2[:, kc:kc + 1, :], 1, D))
        for ji, (src, dst, a, b) in enumerate(jobs):
            k = ji % 2
            sv = stg[k][:, 0:a * b].rearrange("p (a b) -> p a b", b=b)
            P.dma('sp', lambda e, sv=sv, src=src: e.dma_start(out=sv, in_=src), writes=[Bstg[k]])
            eng = ['dve', 'pool', 'act'][ji % 3]
            if eng == 'act':
                P.op('act', lambda e, sv=sv, dst=dst: e.activation(out=dst, in_=sv, func=AF.Copy), reads=[Bstg[k]], writes=[Bw])
            else:
                P.op(eng, lambda e, sv=sv, dst=dst: e.tensor_copy(out=dst, in_=sv), reads=[Bstg[k]], writes=[Bw])
        reg = st.enter_context(nc.sync.register("qoffr"))
        stt = {"val": None}

        def ldreg(e):
            ins = e.reg_load(reg, I["qoff"][0:1, 0:1])
            stt["val"] = e.snap(reg)
            return ins
        P.q['sp'].append(('raw', ldreg))
        goutq = gout.ap().rearrange("(q a) f t -> q a f t", a=2)
        oq = nc.dram_tensor("oq_scr", [1024, 4096], BF16).ap()
        oqv = oq.rearrange("(fc p) t -> p fc t", p=128)
        Boq = Buf()
        for fc in range(8):
            for a in range(2):
                def cpq(e, fc=fc, a=a):
                    return e.dma_start(out=oq[fc * 128:(fc + 1) * 128, a * 2048:(a + 1) * 2048],
                                       in_=goutq[bass.ds(stt["val"], 1), a, fc * 128:(fc + 1) * 128, :].squeeze(0))
                P.dma('sp', cpq, reads=[Bg], writes=[Boq])
        NB = 2
        oT = [sb(f"toT{i}", [128, 8, 128], BF16) for i in range(NB)]
        xo = [sb(f"xo{i}", [128, D], F32) for i in range(1)] * 2
        BoT = [Buf() for _ in range(NB)]
        Bxo = [Buf()] * 2
        pM = [ps(f"pM{i}", [128, 512], F32) for i in range(2)]
        BpM = [Buf() for _ in range(2)]
        h1 = [sb(f"h1{i}", [128, D], F32) for i in range(1)] * 2
        Bh1 = [Buf()] * 2
        ss = sb("tss", [128, 1], F32)
        Bss = Buf()
        u2 = sb("u2", [128, D], F32)
        u2b = sb("u2b", [128, D], BF16)
        Bu2, Bu2b = Buf(), Buf()
        sq, Bsq = u2, Bu2
        ptr = ps("tptr", [128, 8, 128], BF16)
        Bptr = Buf()
        u2T = sb("u2T", [128, 8, 128], BF16)
        Bu2T = Buf()
        pGb = ps("pGb", [128, 512], F32)
        pG = [pGb[:, 256 * i:256 * i + 256].rearrange("p (a b) -> p a b", b=128) for i in range(2)]
        _bg = Buf()
        BpG = [_bg, _bg]
        sg = [sb(f"sg{i}", [128, 128], F32) for i in range(2)]
        Bsg = [Buf() for _ in range(2)]
        aT = sb("aT", [128, 22, 128], BF16)
        BaT = Buf()
        ob = xo
        Bob = Bxo
        Bout = Buf()
        Bm = Buf()
        for ti in range(32):
            k = ti % NB

            def ldo(e, k=k, ti=ti):
                return e.dma_start(out=oT[k][:], in_=oqv[:, :, ti * 128:(ti + 1) * 128])
            P.dma('sp', ldo, reads=[Boq], writes=[BoT[k]])
            P.dma('sp', lambda e, k=k, ti=ti: e.dma_start(out=xo[k][:], in_=I["x_own"][ti * 128:(ti + 1) * 128, :]), writes=[Bxo[k]])
            for nh in range(2):
                for fc in range(8):
                    P.op('pe', lambda e, k=k, nh=nh, fc=fc: e.matmul(pM[nh][:, :], lhsT=oT[k][:, fc, :], rhs=wo[:, fc, nh * 512:(nh + 1) * 512],
                                                                      start=(fc == 0), stop=(fc == 7)),
                         reads=[BoT[k], Bw], writes=[BpM[nh]])
                P.op('dve', lambda e, k=k, nh=nh: e.tensor_tensor(out=h1[k][:, nh * 512:(nh + 1) * 512], in0=pM[nh][:, :],
                                                                   in1=modbc[:, 0, nh * 512:(nh + 1) * 512], op=ALU.mult),
                     reads=[BpM[nh], Bm], writes=[Bh1[k]])
            P.op('pool', lambda e, k=k: e.tensor_tensor(out=h1[k][:], in0=h1[k][:], in1=xo[k][:], op=ALU.add),
                 reads=[Bh1[k], Bxo[k]], writes=[Bh1[k]])
            P.op('act', lambda e, k=k: e.activation(out=sq[:], in_=h1[k][:], func=AF.Square, accum_out=ss[:]),
                 reads=[Bh1[k]], writes=[Bsq, Bss])
            P.op('dve', lambda e: e.tensor_scalar(out=ss[:], in0=ss[:], scalar1=1.0 / D, scalar2=EPS, op0=ALU.mult, op1=ALU.add),
                 reads=[Bss], writes=[Bss])
            P.op('act', lambda e: e.activation(out=ss[:], in_=ss[:], func=AF.Sqrt), reads=[Bss], writes=[Bss])
            P.op('dve', lambda e: e.reciprocal(out=ss[:], in_=ss[:]), reads=[Bss], writes=[Bss])
            P.op('dve', lambda e, k=k: e.scalar_tensor_tensor(out=u2[:], in0=h1[k][:], scalar=ss[:, 0:1], in1=modbc[:, 2, :],
                                                              op0=ALU.mult, op1=ALU.mult), reads=[Bh1[k], Bss, Bm], writes=[Bu2])
            P.op('pool', lambda e: e.tensor_tensor(out=u2b[:], in0=u2[:], in1=modbc[:, 1, :], op=ALU.add), reads=[Bu2, Bm], writes=[Bu2b])
            for kc in range(8):
                P.op('pe', lambda e, kc=kc: e.transpose(out=ptr[:, kc, :], in_=u2b[:, kc * 128:(kc + 1) * 128], identity=identb[:]),
                     reads=[Bu2b], writes=[Bptr])
            P.op('act', lambda e: e.activation(out=u2T[:], in_=ptr[:], func=AF.Copy), reads=[Bptr], writes=[Bu2T])
            for fc in range(22):
                g2 = fc % 2
                for part in range(2):
                    c0_ = part * 2816 + fc * 128
                    for kc in range(8):
                        P.op('pe', lambda e, g2=g2, part=part, c0_=c0_, kc=kc: e.matmul(
                            pG[g2][:, part, :], lhsT=w1[:, kc, c0_:c0_ + 128], rhs=u2T[:, kc, :], start=(kc == 0), stop=(kc == 7)),
                            reads=[Bw, Bu2T], writes=[BpG[g2]])
                P.op('act', lambda e, g2=g2: e.activation(out=sg[g2][:], in_=pG[g2][:, 0, :], func=AF.Silu), reads=[BpG[g2]], writes=[Bsg[g2]])
                P.op('dve', lambda e, g2=g2, fc=fc: e.tensor_tensor(out=aT[:, fc, :], in0=sg[g2][:], in1=pG[g2][:, 1, :], op=ALU.mult),
                     reads=[Bsg[g2], BpG[g2]], writes=[BaT])
            for nh in range(2):
                for fc in range(22):
                    P.op('pe', lambda e, nh=nh, fc=fc: e.matmul(pM[nh][:, :], lhsT=aT[:, fc, :], rhs=w2[:, fc, nh * 512:(nh + 1) * 512],
                                                                start=(fc == 0), stop=(fc == 21)),
                         reads=[BaT, Bw], writes=[BpM[nh]])
                P.op('dve', lambda e, k=k, nh=nh: e.tensor_tensor(out=ob[k][:, nh * 512:(nh + 1) * 512], in0=pM[nh][:, :],
                                                                   in1=modbc[:, 3, nh * 512:(nh + 1) * 512], op=ALU.mult),
                     reads=[BpM[nh], Bm], writes=[Bob[k]])
            P.op('pool', lambda e, k=k: e.tensor_tensor(out=ob[k][:], in0=ob[k][:], in1=h1[k][:], op=ALU.add),
                 reads=[Bob[k], Bh1[k]], writes=[Bob[k]])
            P.dma('pool', lambda e, k=k, ti=ti: e.dma_start(out=out[ti * 128:(ti + 1) * 128, :], in_=ob[k][:]), reads=[Bob[k]], writes=[Bout])
        P.flush()


def _bias_tiles(rpb2):
    blocks = na_blocks()
    combos = {}
    for m in (2, 0, 1, 126, 127):
        for kt, slot in blocks[m]:
            combos[slot] = (m, kt)
    kr_l, kc_ = np.divmod(np.arange(128), 64)
    outb = np.full((2, 21, 128, 128), NEG, np.float32)
    for slot, (m, kt) in combos.items():
        qr = 2 * m + kr_l[None, :]
        qc = kc_[None, :]
        kr = 2 * kt + kr_l[:, None]
        kc = kc_[:, None]
        rs = np.clip(qr - 4, 0, 248)
        cs = np.clip(qc - 8, 0, 48)
        ok = (kr >= rs) & (kr < rs + 8) & (kc >= cs) & (kc < cs + 16)
        di = np.clip(kr - qr + 7, 0, 14)
        dj = np.clip(kc - qc + 15, 0, 30)
        for h in range(2):
            outb[h, slot] = np.where(ok, rpb2[h][di, dj], np.float32(NEG))
    return np.ascontiguousarray(outb.transpose(2, 0, 1, 3))


_NC_CACHE = {}


def kernel(**inp):
    f = lambda a: np.ascontiguousarray(np.asarray(a, dtype=np.float32))
    x, c, ctx, c_ctx = f(inp["x"]), f(inp["c"]), f(inp["ctx"]), f(inp["c_ctx"])
    w_in = f(inp["w_in"])[0]
    w_ada = f(inp["w_ada"])[0]
    b_ada = f(inp["b_ada"])[0]
    mu_p = f(inp["rw_mu_prev"])[0]
    mu_n = f(inp["rw_mu_next"])[0]
    ident = np.eye(128, dtype=np.float32)
    bones = np.kron(np.eye(2, dtype=np.float32), np.ones((64, 64), np.float32))
    s_i, t_i = np.meshgrid(np.arange(64), np.arange(64), indexing="ij")
    mk = np.zeros((128, 2, 2, 128), np.float32)
    mst = np.zeros((128, 2, 64), np.float32)
    for d in range(2):
        strict = (s_i < t_i) if d == 0 else (s_i > t_i)
        incl = (s_i <= t_i) if d == 0 else (s_i >= t_i)
        blk = np.concatenate([strict, incl], axis=1).astype(np.float32)
        for h in range(2):
            for j in range(2):
                mk[64 * h:64 * h + 64, d, j, :] = blk
            mst[64 * h:64 * h + 64, d, :] = strict.T.astype(np.float32)
    mkd = np.zeros((128, 2, 2, 64), np.float32)
    bd = (s_i // 32 == t_i // 32)
    for d in range(2):
        strict = (s_i < t_i) if d == 0 else (s_i > t_i)
        for h in range(2):
            mkd[64 * h:64 * h + 64, d, 0, :] = (strict & bd).astype(np.float32)
            mkd[64 * h:64 * h + 64, d, 1, :] = (strict & ~bd).astype(np.float32)
            mst[64 * h:64 * h + 64, d, :] = (strict & bd).T.astype(np.float32)
    istack = np.concatenate([np.eye(64, dtype=np.float32)] * 2, axis=0)
    reset = np.ones((128, 256), np.float32)
    reset[:, 0::64] = 0.0
    in_maps = []
    for core in range(8):
        b, g = divmod(core, 4)
        hc = slice(128 * g, 128 * g + 128)
        m = {}
        m["xcat"] = np.concatenate([ctx[b], x[b]], axis=0)
        m["x_own"] = x[b, 4096 * g:4096 * (g + 1)]
        c2 = np.stack([c[b], c_ctx], axis=0)
        m["c2T"] = c2.reshape(2, 8, 128).transpose(2, 1, 0)
        m["w_ada"] = w_ada
        m["b_adaT"] = b_ada.reshape(48, 128).T
        m["b_ada_bc"] = np.broadcast_to(b_ada[None, 2 * D:], (128, 4 * D))
        m["g1T"] = f(inp["norm1_g"])[0].reshape(8, 128).T
        m["g2_bc"] = np.broadcast_to(f(inp["norm2_g"])[0][None, :], (128, D))
        qc = np.arange(128 * g, 128 * g + 128)
        m["w_na"] = w_in[:, np.concatenate([qc, 512 + qc, 1024 + qc])]
        rwc = np.concatenate([1536 + qc, 2048 + qc, 2560 + qc, np.arange(3072, 3200), np.arange(3264, 3392), np.arange(3200, 3264)])
        m["w_rw"] = w_in[:, rwc]
        rc_ = rwc - 1536
        muT = np.zeros((128, 2, 6), np.float32)
        for j, mv in enumerate((mu_p, mu_n)):
            col = np.zeros(768, np.float32)
            col[:704] = mv[rc_]
            muT[:, j, :] = col.reshape(6, 128).T
        m["muT"] = muT
        gq = f(inp["na_q_g"])[0]
        gk = f(inp["na_k_g"])[0]
        m["gqk_bc"] = np.broadcast_to(np.stack([gq, gk], 0)[None], (128, 2, 64))
        m["bias"] = _bias_tiles(f(inp["na_rpb"])[0][2 * g:2 * g + 2])
        m["w_upT"] = f(inp["rw_w_up"])[0][:, :, hc].reshape(128, 128)
        m["a_upT"] = f(inp["rw_a_up"])[0][:, :, hc].transpose(1, 0, 2)
        m["g_up"] = f(inp["rw_g_up"])[0][:, hc]
        cvv = np.stack([f(inp["rw_w0"])[0][0, hc], f(inp["rw_w0"])[0][1, hc], f(inp["rw_a0"])[0][0, hc], f(inp["rw_a0"])[0][1, hc],
                        f(inp["rw_k_k"])[0][hc], f(inp["rw_k_a"])[0][hc], f(inp["rw_r_k"])[0].reshape(512)[hc],
                        f(inp["rw_ln_g"])[0][hc], f(inp["rw_ln_b"])[0][hc]], axis=1)
        m["cv"] = cvv
        perm = np.concatenate([np.concatenate([np.arange(128 * gg, 128 * gg + 128), 512 + np.arange(128 * gg, 128 * gg + 128)])
                               for gg in range(4)])
        m["w_out"] = f(inp["w_out"])[0][perm]
        m["ffn_w_in"] = f(inp["ffn_w_in"])[0]
        m["ffn_w_out"] = f(inp["ffn_w_out"])[0]
        m["qoff"] = np.array([[g]], np.int32)
        m["c_ident"] = ident
        m["c_bones"] = bones
        m["c_mk"] = np.broadcast_to(mk[:, :, None], (128, 2, 4, 2, 128))
        m["c_mst"] = np.broadcast_to(mst[:, :, None], (128, 2, 4, 64))
        m["c_mkd"] = np.broadcast_to(mkd[:, :, None], (128, 2, 4, 2, 64))
        m["c_istack"] = np.broadcast_to(istack[:, None], (128, 4, 64))
        m["c_reset"] = reset
        for k_, v_ in m.items():
            want = np.int32 if k_ == "qoff" else np.float32
            m[k_] = np.ascontiguousarray(v_, dtype=want)
            assert list(m[k_].shape) == IN_SPECS[k_][0], (k_, m[k_].shape)
        in_maps.append(m)
    if "nc" not in _NC_CACHE:
        _NC_CACHE["nc"] = build_nc()
    res = run_bass_kernel_spmd(_NC_CACHE["nc"], in_maps, core_ids=list(range(8)))
    outp = np.zeros((2, T, D), np.float32)
    for core in range(8):
        b, g = divmod(core, 4)
        outp[b, 4096 * g:4096 * (g + 1)] = res.results[core]["out"]
    return outp
```

```python
import contextlib
import numpy as np
import concourse.bass as bass
import concourse.mybir as mybir
from concourse.bass_utils import run_bass_kernel_spmd

F32 = mybir.dt.float32
BF16 = mybir.dt.bfloat16
I32 = mybir.dt.int32
AF = mybir.ActivationFunctionType
ALU = mybir.AluOpType
AX = mybir.AxisListType

EP = 30000
T = 16384
TC = 256
NT = T + TC
D = 1024
NEG = -30000.0
EPS = 1e-6


class Buf:
    def __init__(self, name=""):
        self.name = name
        self.w = None
        self.r = {}


class Prog:
    ENGS = ['pe', 'act', 'dve', 'pool', 'sp']

    _uid = [0]

    def __init__(self, nc, ndma=8):
        self.nc = nc
        Prog._uid[0] += 1
        self.uid = Prog._uid[0]
        self.q = {e: [] for e in self.ENGS}
        self.cnt = {e: 0 for e in self.ENGS}
        self.seen = {e: {} for e in self.ENGS}
        self.ndma = ndma
        self.dma_cnt = {}
        self.dma_eng = {}
        self.dma_next = {e: 0 for e in self.ENGS}

    def _deps(self, eng, reads, writes):
        deps = {}

        def add(tok):
            if tok is None:
                return
            k, v = tok
            if deps.get(k, 0) < v:
                deps[k] = v
        for b in reads:
            add(b.w)
        for b in writes:
            add(b.w)
            for k, v in b.r.items():
                if k == eng:
                    continue
                add((k, v))
        waits = []
        for k, v in deps.items():
            if self.seen[eng].get(k, 0) < v:
                self.seen[eng][k] = v
                waits.append((k, v))
        return waits

    def op(self, eng, fn, reads=(), writes=()):
        waits = self._deps(eng, reads, writes)
        self.cnt[eng] += 1
        idx = self.cnt[eng]
        self.q[eng].append(('op', waits, fn, idx))
        for b in reads:
            b.r[eng] = idx
        for b in writes:
            b.w = (eng, idx)
            b.r = {}

    def dma(self, eng, fn, reads=(), writes=(), inc=16):
        slot = self.dma_next[eng]
        self.dma_next[eng] = (slot + 1) % self.ndma
        key = ('dma', eng, slot)
        prev = self.dma_cnt.get(key, 0)
        waits = self._deps(eng, reads, writes)
        if prev > 0 and self.seen[eng].get(key, 0) < prev:
            self.seen[eng][key] = prev
            waits.append((key, prev))
        val = prev + inc
        self.dma_cnt[key] = val
        self.dma_eng[key] = eng
        self.q[eng].append(('dma', waits, fn, key, inc))
        for b in reads:
            b.r[key] = val
        for b in writes:
            b.w = (key, val)
            b.r = {}
        return (key, val)

    def flush(self):
        nc = self.nc
        for key, val in self.dma_cnt.items():
            self.q[self.dma_eng[key]].append(('wait', [(key, val)]))
        with contextlib.ExitStack() as st:
            sems = {}
            for e in self.ENGS:
                nep = self.cnt[e] // EP + 1
                for k in range(nep):
                    sems[(e, k)] = st.enter_context(nc.semaphore(f"s{self.uid}_{e}_{k}"))
            for key in self.dma_cnt:
                sems[key] = st.enter_context(nc.semaphore(f"d{self.uid}_{key[1]}_{key[2]}"))
            block = st.enter_context(nc.Block())

            def emit_wait(eng, k, v):
                if isinstance(k, tuple):
                    eng.wait_ge(sems[k], v)
                else:
                    ep = (v - 1) // EP
                    eng.wait_ge(sems[(k, ep)], v - ep * EP)

            def run(ename, eng):
                for it in self.q[ename]:
                    if it[0] == 'op':
                        _, waits, fn, idx = it
                        for k, v in waits:
                            emit_wait(eng, k, v)
                        ep = (idx - 1) // EP
                        fn(eng).then_inc(sems[(ename, ep)], 1)
                    elif it[0] == 'dma':
                        _, waits, fn, key, inc = it
                        for k, v in waits:
                            emit_wait(eng, k, v)
                        fn(eng).then_inc(sems[key], inc)
                    elif it[0] == 'raw':
                        it[1](eng)
                    else:
                        for k, v in it[1]:
                            emit_wait(eng, k, v)

            @block.tensor
            def _(e):
                run('pe', e)

            @block.scalar
            def _(e):
                run('act', e)

            @block.vector
            def _(e):
                run('dve', e)

            @block.gpsimd
            def _(e):
                run('pool', e)

            @block.sync
            def _(e):
                run('sp', e)


IN_SPECS = {
    "xcat": ([NT, D], F32), "x_own": ([4096, D], F32), "c2T": ([128, 8, 2], F32),
    "w_ada": ([D, 6 * D], F32), "b_adaT": ([128, 48], F32), "b_ada_bc": ([128, 4 * D], F32),
    "g1T": ([128, 8], F32), "g2_bc": ([128, D], F32),
    "w_na": ([D, 384], F32), "w_rw": ([D, 704], F32),
    "muT": ([128, 2, 6], F32), "gqk_bc": ([128, 2, 64], F32),
    "bias": ([128, 2, 21, 128], F32),
    "w_upT": ([128, 128], F32), "a_upT": ([64, 2, 128], F32), "g_up": ([128, 128], F32),
    "cv": ([128, 9], F32),
    "w_out": ([D, D], F32), "ffn_w_in": ([D, 5632], F32), "ffn_w_out": ([2816, D], F32),
    "qoff": ([1, 1], I32),
    "c_ident": ([128, 128], F32), "c_bones": ([128, 128], F32), "c_mk": ([128, 2, 4, 2, 128], F32),
    "c_mst": ([128, 2, 4, 64], F32), "c_mkd": ([128, 2, 4, 2, 64], F32), "c_istack": ([128, 4, 64], F32), "c_reset": ([128, 256], F32),
}


def build_nc():
    nc = bass.Bass("TRN2", target_bir_lowering=False)
    I = {k: nc.dram_tensor(k, s, d, kind="ExternalInput").ap() for k, (s, d) in IN_SPECS.items()}
    out = nc.dram_tensor("out", [4096, D], F32, kind="ExternalOutput").ap()
    U = nc.dram_tensor("U_scr", [D, NT], BF16).ap()
    gin = nc.dram_tensor("gin", [8, 256, 2048], BF16)
    gout = nc.dram_tensor("gout", [8, 1024, 2048], BF16)
    Uv = U.rearrange("(kc p) t -> p kc t", p=128)
    dbgU = dbgG = dbgW = dbgF = None

    outer = contextlib.ExitStack()
    with outer:
        def sbo(name, shape, dt):
            return outer.enter_context(nc.sbuf_tensor('S0_' + name, shape, dt))
        A1 = sbo("A1", [128, 2, 8], F32)
        SH1 = sbo("SH1", [128, 2, 8], F32)
        modbc = sbo("modbc", [128, 4, D], F32)
        identb = sbo("identb", [128, 128], BF16)
        identf = sbo("identf", [128, 128], F32)
        bones = sbo("bones", [128, 128], F32)

        with contextlib.ExitStack() as st:
            P = Prog(nc)

            def sb(name, shape, dt):
                return st.enter_context(nc.sbuf_tensor('S%d_' % P.uid + name, shape, dt))

            def ps(name, shape, dt):
                return st.enter_context(nc.psum_tensor('P%d_' % P.uid + name, shape, dt))
            c2 = sb("c2", [128, 8, 2], F32)
            sT = sb("sT", [128, 8, 2], F32)
            sbc = sb("sbc", [128, 8, 128], F32)
            onesf = sb("onesf", [128, 128], F32)
            bT = sb("bT", [128, 48], F32)
            bbc = sb("bbc", [128, 4 * D], F32)
            g1 = sb("g1", [128, 8], F32)
            g2bc = sb("g2bc", [128, D], F32)
            modT = sb("modT", [128, 2, 48], F32)
            wblk = [sb(f"wblk{i}", [128, 8, D], F32) for i in range(2)]
            pmod = ps("pmod", [128, 8, 2], F32)
            pbc = [ps(f"pbc{i}", [128, 512], F32) for i in range(2)]
            B = {n: Buf(n) for n in ["c2", "sT", "sbc", "onesf", "bT", "bbc", "g1", "g2bc", "modT", "wblk0", "wblk1",
                                     "pmod", "pbc0", "pbc1", "A1", "SH1", "modbc", "ident", "bones"]}
            P.dma('sp', lambda e: e.dma_start(out=c2[:], in_=I["c2T"]), writes=[B["c2"]])
            P.dma('sp', lambda e: e.dma_start(out=bT[:], in_=I["b_adaT"]), writes=[B["bT"]])
            P.dma('sp', lambda e: e.dma_start(out=bbc[:], in_=I["b_ada_bc"]), writes=[B["bbc"]])
            P.dma('sp', lambda e: e.dma_start(out=g1[:], in_=I["g1T"]), writes=[B["g1"]])
            P.dma('sp', lambda e: e.dma_start(out=g2bc[:], in_=I["g2_bc"]), writes=[B["g2bc"]])
            P.dma('sp', lambda e: e.dma_start(out=identf[:], in_=I["c_ident"]), writes=[B["ident"]])
            P.dma('sp', lambda e: e.dma_start(out=bones[:], in_=I["c_bones"]), writes=[B["bones"]])
            P.op('dve', lambda e: e.tensor_copy(out=identb[:], in_=identf[:]), reads=[B["ident"]], writes=[B["ident"]])
            P.op('act', lambda e: e.activation(out=sT[:], in_=c2[:], func=AF.Silu), reads=[B["c2"]], writes=[B["sT"]])
            P.op('dve', lambda e: e.memset(onesf[:], 1.0), writes=[B["onesf"]])
            for kc in range(8):
                P.op('dve', lambda e, kc=kc: e.tensor_scalar(out=sbc[:, kc, :], in0=onesf[:], scalar1=sT[:, kc, 0:1],
                                                             scalar2=None, op0=ALU.mult),
                     reads=[B["onesf"], B["sT"]], writes=[B["sbc"]])
            wv = I["w_ada"].rearrange("(kc p) n -> p kc n", p=128)
            for m in range(6):
                wb = wblk[m % 2]
                Bw = B[f"wblk{m % 2}"]
                for hh in range(2):
                    P.dma('sp', lambda e, m=m, wb=wb, hh=hh: e.dma_start(out=wb[:, 4 * hh:4 * hh + 4, :],
                                                                         in_=wv[:, 4 * hh:4 * hh + 4, m * D:(m + 1) * D]),
                          writes=[Bw])
                for jj in range(8):
                    for kc in range(8):
                        P.op('pe', lambda e, wb=wb, jj=jj, kc=kc: e.matmul(pmod[:, jj, :], lhsT=wb[:, kc, jj * 128:(jj + 1) * 128],
                                                                           rhs=sT[:, kc, :], start=(kc == 0), stop=(kc == 7)),
                             reads=[Bw, B["sT"]], writes=[B["pmod"]])
                for i in range(2):
                    P.op('dve', lambda e, m=m, i=i: e.tensor_tensor(out=modT[:, i, m * 8:(m + 1) * 8], in0=pmod[:, :, i],
                                                                   in1=bT[:, m * 8:(m + 1) * 8], op=ALU.add),
                         reads=[B["pmod"], B["bT"]], writes=[B["modT"]])
                if m >= 2:
                    for nh in range(2):
                        pb = pbc[nh]
                        for kc in range(8):
                            P.op('pe', lambda e, wb=wb, pb=pb, nh=nh, kc=kc: e.matmul(
                                pb[:, :], lhsT=sbc[:, kc, :], rhs=wb[:, kc, nh * 512:(nh + 1) * 512],
                                start=(kc == 0), stop=(kc == 7)),
                                reads=[Bw, B["sbc"]], writes=[B[f"pbc{nh}"]])
                        P.op('dve', lambda e, m=m, pb=pb, nh=nh: e.tensor_tensor(
                            out=modbc[:, m - 2, nh * 512:(nh + 1) * 512], in0=pb[:, :],
                            in1=bbc[:, (m - 2) * D + nh * 512:(m - 2) * D + (nh + 1) * 512], op=ALU.add),
                            reads=[B[f"pbc{nh}"], B["bbc"]], writes=[B["modbc"]])
            P.op('dve', lambda e: e.scalar_tensor_tensor(out=modbc[:, 2, :], in0=modbc[:, 2, :], scalar=1.0, in1=g2bc[:],
                                                         op0=ALU.add, op1=ALU.mult),
                 reads=[B["modbc"], B["g2bc"]], writes=[B["modbc"]])
            for i in range(2):
                P.op('dve', lambda e, i=i: e.scalar_tensor_tensor(out=A1[:, i, :], in0=modT[:, i, 8:16], scalar=1.0, in1=g1[:],
                                                                  op0=ALU.add, op1=ALU.mult),
                     reads=[B["modT"], B["g1"]], writes=[B["A1"]])
                P.op('dve', lambda e, i=i: e.tensor_copy(out=SH1[:, i, :], in_=modT[:, i, 0:8]),
                     reads=[B["modT"]], writes=[B["SH1"]])

            NB = 3
            xt = [sb(f"xt{i}", [128, D], F32) for i in range(NB)]
            sq = sb("sqscr", [128, D], F32)
            ss = [sb(f"ss{i}", [128, 1], F32) for i in range(NB)]
            rs = [sb(f"rs{i}", [128, 1], F32) for i in range(NB)]
            xn = [sb(f"xn{i}", [128, D], BF16) for i in range(NB)]
            uT = [sb(f"uT{i}", [128, 8, 128], BF16) for i in range(NB)]
            ptr = [ps(f"ptr{i}", [128, 8, 128], BF16) for i in range(2)]
            Bx = [Buf() for _ in range(NB)]
            Bss = [Buf() for _ in range(NB)]
            Bxn = [Buf() for _ in range(NB)]
            BuT = [Buf() for _ in range(NB)]
            Bpt = [Buf() for _ in range(2)]
            Bsq = Buf()
            BU = Buf()
            for ti in range(NT // 128):
                k = ti % NB
                i = 1 if ti < 2 else 0
                P.dma('sp', lambda e, ti=ti, k=k: e.dma_start(out=xt[k][:], in_=I["xcat"][ti * 128:(ti + 1) * 128, :]),
                      writes=[Bx[k]])
                P.op('act', lambda e, k=k: e.activation(out=sq[:], in_=xt[k][:], func=AF.Square, accum_out=ss[k][:]),
                     reads=[Bx[k]], writes=[Bsq, Bss[k]])
                P.op('dve', lambda e, k=k: e.tensor_scalar(out=rs[k][:], in0=ss[k][:], scalar1=1.0 / D, scalar2=EPS,
                                                          op0=ALU.mult, op1=ALU.add), reads=[Bss[k]], writes=[Bss[k]])
                P.op('act', lambda e, k=k: e.activation(out=rs[k][:], in_=rs[k][:], func=AF.Sqrt), reads=[Bss[k]], writes=[Bss[k]])
                P.op('dve', lambda e, k=k: e.reciprocal(out=rs[k][:], in_=rs[k][:]), reads=[Bss[k]], writes=[Bss[k]])
                P.op('dve', lambda e, k=k: e.tensor_scalar(out=xn[k][:], in0=xt[k][:], scalar1=rs[k][:, 0:1], scalar2=None,
                                                          op0=ALU.mult), reads=[Bx[k], Bss[k]], writes=[Bxn[k]])
                pk = ti % 2
                for kc in range(8):
                    P.op('pe', lambda e, k=k, pk=pk, kc=kc: e.transpose(out=ptr[pk][:, kc, :], in_=xn[k][:, kc * 128:(kc + 1) * 128],
                                                                        identity=identb[:]),
                         reads=[Bxn[k], B["ident"]], writes=[Bpt[pk]])
                for kc in range(8):
                    eng = 'act' if kc % 2 == 0 else 'dve'
                    if eng == 'act':
                        P.op('act', lambda e, k=k, pk=pk, kc=kc, i=i: e.activation(
                            out=uT[k][:, kc, :], in_=ptr[pk][:, kc, :], func=AF.Identity,
                            bias=SH1[:, i, kc:kc + 1], scale=A1[:, i, kc:kc + 1]),
                            reads=[Bpt[pk], B["A1"], B["SH1"]], writes=[BuT[k]])
                    else:
                        P.op('dve', lambda e, k=k, pk=pk, kc=kc, i=i: e.tensor_scalar(
                            out=uT[k][:, kc, :], in0=ptr[pk][:, kc, :], scalar1=A1[:, i, kc:kc + 1],
                            scalar2=SH1[:, i, kc:kc + 1], op0=ALU.mult, op1=ALU.add),
                            reads=[Bpt[pk], B["A1"], B["SH1"]], writes=[BuT[k]])
                P.dma('pool', lambda e, ti=ti, k=k: e.dma_start(out=Uv[:, :, ti * 128:(ti + 1) * 128], in_=uT[k][:]),
                      reads=[BuT[k]], writes=[BU])
            P.flush()
        nc.all_engine_barrier()
        phase_na(nc, I, Uv, gin, identb)
        nc.all_engine_barrier()
        phase_rw(nc, I, Uv, gin, identb, bones, dbgW, dbgF)
        nc.all_engine_barrier()
        phase_tail(nc, I, gin, gout, out, modbc, identb, U, dbgU, dbgG)
    return nc


def na_blocks():
    res = []
    for m in range(128):
        if m == 0:
            res.append([(kt, 5 + kt) for kt in range(4)])
        elif m == 1:
            res.append([(kt, 9 + kt) for kt in range(4)])
        elif m == 126:
            res.append([(124 + j, 13 + j) for j in range(4)])
        elif m == 127:
            res.append([(124 + j, 17 + j) for j in range(4)])
        else:
            res.append([(m + dl, dl + 2) for dl in range(-2, 3)])
    return res


def phase_na(nc, I, Uv, gin, identb):
    with contextlib.ExitStack() as st:
        P = Prog(nc)

        def sb(name, shape, dt):
            return st.enter_context(nc.sbuf_tensor('S%d_' % P.uid + name, shape, dt))

        def ps(name, shape, dt):
            return st.enter_context(nc.psum_tensor('P%d_' % P.uid + name, shape, dt))
        NTI = NT // 128
        qT = sb("qT", [128, NT], BF16)
        kT = sb("kT", [128, NT], BF16)
        vS = sb("vS", [128, NTI, 2, 65], BF16)
        biasS = sb("biasS", [128, 2, 21, 128], F32)
        wst = sb("wst", [128, 8, 384], F32)
        wb = sb("wnab", [128, 8, 384], BF16)
        gqk = sb("gqk", [128, 2, 64], F32)
        Bq, Bk, Bv, Bbias, Bw, Bg = Buf(), Buf(), Buf(), Buf(), Buf(), Buf()
        Bid = Buf()
        P.dma('sp', lambda e: e.dma_start(out=wst[:], in_=I["w_na"].rearrange("(kc p) n -> p kc n", p=128)), writes=[Bw])
        P.op('dve', lambda e: e.tensor_copy(out=wb[:], in_=wst[:]), reads=[Bw], writes=[Bw])
        P.dma('sp', lambda e: e.dma_start(out=biasS[:], in_=I["bias"]), writes=[Bbias])
        P.dma('sp', lambda e: e.dma_start(out=gqk[:], in_=I["gqk_bc"]), writes=[Bg])
        P.op('dve', lambda e: e.memset(vS[:], 1.0), writes=[Bv])
        NB = 3
        uT = [sb(f"nuT{i}", [128, 8, 128], BF16) for i in range(NB)]
        BuT = [Buf() for _ in range(NB)]
        pp = [ps(f"npp{i}", [128, 512], F32) for i in range(2)]
        nbf = ps("nbf", [128, 1024], BF16)
        Bpp = [Buf() for _ in range(2)]
        sq = sb("nsq", [128, 256], F32)
        ssq = sb("nssq", [128, 4], F32)
        qkn = [sb(f"qkn{i}", [128, 256], BF16) for i in range(2)]
        Bsq, Bssq = Buf(), Buf()
        Bqkn = [Buf() for _ in range(2)]
        ptq = [nbf[:, 256 * i:256 * i + 256].rearrange("p (a b) -> p a b", b=128) for i in range(2)]
        Bnbf = Buf()
        Bptq = [Bnbf, Bnbf]
        for ti in range(NTI):
            k = ti % NB
            k2 = ti % 2
            P.dma('sp', lambda e, ti=ti, k=k: e.dma_start(out=uT[k][:], in_=Uv[:, :, ti * 128:(ti + 1) * 128]), writes=[BuT[k]])
            for kc in range(8):
                P.op('pe', lambda e, k=k, k2=k2, kc=kc: e.matmul(pp[k2][:, 0:384], lhsT=uT[k][:, kc, :], rhs=wb[:, kc, :],
                                                                start=(kc == 0), stop=(kc == 7)),
                     reads=[BuT[k], Bw], writes=[Bpp[k2]])
            P.op('act', lambda e, k2=k2: e.activation(out=sq[:], in_=pp[k2][:, 0:256], func=AF.Square), reads=[Bpp[k2]], writes=[Bsq])
            P.op('dve', lambda e: e.tensor_reduce(out=ssq[:], in_=sq[:].rearrange("p (a b) -> p a b", b=64), axis=AX.X, op=ALU.add),
                 reads=[Bsq], writes=[Bssq])
            P.op('dve', lambda e: e.tensor_scalar(out=ssq[:, 0:2], in0=ssq[:, 0:2], scalar1=64 * EPS, scalar2=None,
                                                  op0=ALU.add), reads=[Bssq], writes=[Bssq])
            P.op('dve', lambda e: e.tensor_scalar(out=ssq[:, 2:4], in0=ssq[:, 2:4], scalar1=1.0 / 64, scalar2=EPS,
                                                  op0=ALU.mult, op1=ALU.add), reads=[Bssq], writes=[Bssq])
            P.op('act', lambda e: e.activation(out=ssq[:], in_=ssq[:], func=AF.Sqrt), reads=[Bssq], writes=[Bssq])
            P.op('dve', lambda e: e.reciprocal(out=ssq[:], in_=ssq[:]), reads=[Bssq], writes=[Bssq])
            for j in range(4):
                P.op('dve', lambda e, k2=k2, j=j: e.scalar_tensor_tensor(
                    out=qkn[k2][:, j * 64:(j + 1) * 64], in0=pp[k2][:, j * 64:(j + 1) * 64], scalar=ssq[:, j:j + 1],
                    in1=gqk[:, j // 2, :], op0=ALU.mult, op1=ALU.mult),
                    reads=[Bpp[k2], Bssq, Bg], writes=[Bqkn[k2]])
            P.op('act', lambda e, k2=k2, ti=ti: e.activation(out=vS[:, ti, :, 0:64],
                                                             in_=pp[k2][:, 256:384].rearrange("p (h d) -> p h d", d=64),
                                                             func=AF.Copy), reads=[Bpp[k2]], writes=[Bv])
            for j in range(2):
                P.op('pe', lambda e, k2=k2, j=j: e.transpose(out=ptq[k2][:, j, :], in_=qkn[k2][:, j * 128:(j + 1) * 128],
                                                             identity=identb[:]), reads=[Bqkn[k2], Bid], writes=[Bptq[k2]])
            P.op('act', lambda e, k2=k2, ti=ti: e.activation(out=qT[:, ti * 128:(ti + 1) * 128], in_=ptq[k2][:, 0, :], func=AF.Copy),
                 reads=[Bptq[k2]], writes=[Bq])
            P.op('dve', lambda e, k2=k2, ti=ti: e.tensor_copy(out=kT[:, ti * 128:(ti + 1) * 128], in_=ptq[k2][:, 1, :]),
                 reads=[Bptq[k2]], writes=[Bk])
        pS = [ps(f"pS{i}", [128, 8, 128], F32) for i in range(2)]
        BpS = [Buf() for _ in range(2)]
        sS = [sb(f"sS{i}", [128, 5, 128], F32) for i in range(2)]
        BsS = [Buf() for _ in range(2)]
        pT = [sb(f"pT{i}", [128, 7, 128], BF16) for i in range(2)]
        BpT = [Buf() for _ in range(2)]
        pOb = ps("pOb", [128, 512], F32)
        pO = [pOb[:, 256 * i:256 * i + 130].rearrange("p (a b) -> p a b", b=65) for i in range(2)]
        _b = Buf()
        BpO = [_b, _b]
        rc = [sb(f"rc{i}", [128, 2], F32) for i in range(2)]
        Brc = [Buf() for _ in range(2)]
        oS = [sb(f"oS{i}", [128, 128], BF16) for i in range(2)]
        BoS = [Buf() for _ in range(2)]
        pOT = [nbf[:, 512:640]]
        BpOT = [Bnbf]
        oT = [sb(f"oT{i}", [128, 128], BF16) for i in range(2)]
        BoT = [Buf() for _ in range(2)]
        Bgin = Buf()
        blocks = na_blocks()
        it = 0
        for m in range(128):
            kl = blocks[m]
            nk = len(kl)
            qc0 = (m + 2) * 128
            mb = m % 2
            for h in range(2):
                x2 = it % 2
                it += 1
                hs = slice(64 * h, 64 * h + 64)
                tiles = [kt + 2 for kt, _ in kl] + [0, 1]
                for j, tt in enumerate(tiles):
                    P.op('pe', lambda e, x2=x2, j=j, tt=tt, hs=hs, qc0=qc0: e.matmul(
                        pS[x2][:, j, :], lhsT=kT[hs, tt * 128:(tt + 1) * 128], rhs=qT[hs, qc0:qc0 + 128], start=True, stop=True),
                        reads=[Bq, Bk], writes=[BpS[x2]])
                s0 = kl[0][1]
                P.op('dve', lambda e, x2=x2, nk=nk, h=h, s0=s0: e.tensor_tensor(
                    out=sS[x2][:, 0:nk, :], in0=pS[x2][:, 0:nk, :], in1=biasS[:, h, s0:s0 + nk, :], op=ALU.add),
                    reads=[BpS[x2], Bbias], writes=[BsS[x2]])
                P.op('act', lambda e, x2=x2, nk=nk: e.activation(out=pT[x2][:, 0:nk, :], in_=sS[x2][:, 0:nk, :], func=AF.Exp),
                     reads=[BsS[x2]], writes=[BpT[x2]])
                P.op('act', lambda e, x2=x2, nk=nk: e.activation(out=pT[x2][:, nk:nk + 2, :], in_=pS[x2][:, nk:nk + 2, :], func=AF.Exp),
                     reads=[BpS[x2]], writes=[BpT[x2]])
                for j, tt in enumerate(tiles):
                    P.op('pe', lambda e, x2=x2, j=j, tt=tt, h=h, mb=mb, n=len(tiles): e.matmul(
                        pO[mb][:, h, :], lhsT=pT[x2][:, j, :], rhs=vS[:, tt, h, :], start=(j == 0), stop=(j == n - 1)),
                        reads=[BpT[x2], Bv], writes=[BpO[mb]])
            P.op('dve', lambda e, mb=mb: e.reciprocal(out=rc[mb][:], in_=pO[mb][:, :, 64]), reads=[BpO[mb]], writes=[Brc[mb]])
            for h in range(2):
                P.op('dve', lambda e, mb=mb, h=h: e.tensor_scalar(out=oS[mb][:, h * 64:(h + 1) * 64], in0=pO[mb][:, h, 0:64],
                                                                   scalar1=rc[mb][:, h:h + 1], scalar2=None, op0=ALU.mult),
                     reads=[BpO[mb], Brc[mb]], writes=[BoS[mb]])
            P.op('pe', lambda e, mb=mb: e.transpose(out=pOT[0][:, :], in_=oS[mb][:, :], identity=identb[:]),
                 reads=[BoS[mb]], writes=[BpOT[0]])
            P.op('act', lambda e, mb=mb: e.activation(out=oT[mb][:], in_=pOT[0][:, :], func=AF.Copy),
                 reads=[BpOT[0]], writes=[BoT[mb]])
            P.dma('pool', lambda e, mb=mb, m=m: e.dma_start(out=gin.ap()[m // 16, 0:128, (m % 16) * 128:(m % 16 + 1) * 128], in_=oT[mb][:]),
                  reads=[BoT[mb]], writes=[Bgin])
        P.flush()


def phase_rw(nc, I, Uv, gin, identb, bones, dbgW=None, dbgF=None):
    with contextlib.ExitStack() as st:
        P = Prog(nc)

        def sb(name, shape, dt):
            return st.enter_context(nc.sbuf_tensor('S%d_' % P.uid + name, shape, dt))

        def ps(name, shape, dt):
            return st.enter_context(nc.psum_tensor('P%d_' % P.uid + name, shape, dt))
        wb = sb("rwb", [128, 8, 704], BF16)
        mu = sb("mu", [128, 2, 6], F32)
        c0 = sb("c0", [128, 6], F32)
        cv = sb("cv", [128, 9], F32)
        omka = sb("omka", [128, 1], F32)
        omka2 = sb("omka2", [128, 1], F32)
        wupf = sb("wupf", [128, 128], F32)
        wupb = sb("wupb", [128, 128], BF16)
        aupf = sb("aupf", [64, 2, 128], F32)
        aupb = sb("aupb", [64, 2, 128], BF16)
        gupf = sb("gupf", [128, 128], F32)
        gupb = sb("gupb", [128, 128], BF16)
        mk = sb("mk", [128, 2, 4, 2, 128], F32)
        mst = sb("mst", [128, 2, 4, 64], F32)
        mkd = sb("mkd", [128, 2, 4, 2, 64], F32)
        istack = sb("istack", [128, 4, 64], F32)
        reset = sb("reset", [128, 256], F32)
        bavg = sb("bavg", [128, 128], F32)
        wkvT = sb("wkvT", [128, T], F32)
        wst = wkvT[:, 0:8 * 704].rearrange("p (a b) -> p a b", b=704)
        Bc = Buf("consts")
        Bwkv = Buf("wkv")
        for dst, src in [(None, I["w_rw"].rearrange("(kc p) n -> p kc n", p=128)), (mu, I["muT"]), (cv, I["cv"]), (wupf, I["w_upT"]),
                         (aupf, I["a_upT"]), (gupf, I["g_up"]), (mk, I["c_mk"]), (mst, I["c_mst"]), (mkd, I["c_mkd"]), (istack, I["c_istack"]),
                         (reset, I["c_reset"])]:
            if dst is None:
                P.dma('sp', lambda e, src=src: e.dma_start(out=wst, in_=src), writes=[Bc, Bwkv])
            else:
                P.dma('sp', lambda e, dst=dst, src=src: e.dma_start(out=dst[:], in_=src), writes=[Bc])
        P.op('dve', lambda e: e.tensor_copy(out=wb[:], in_=wst), reads=[Bc, Bwkv], writes=[Bc])
        P.op('dve', lambda e: e.tensor_copy(out=wupb[:], in_=wupf[:]), reads=[Bc], writes=[Bc])
        P.op('dve', lambda e: e.tensor_copy(out=aupb[:], in_=aupf[:]), reads=[Bc], writes=[Bc])
        P.op('dve', lambda e: e.tensor_copy(out=gupb[:], in_=gupf[:]), reads=[Bc], writes=[Bc])
        P.op('dve', lambda e: e.tensor_tensor(out=c0[:], in0=mu[:, 0, :], in1=mu[:, 1, :], op=ALU.add), reads=[Bc], writes=[Bc])
        P.op('dve', lambda e: e.tensor_scalar(out=c0[:], in0=c0[:], scalar1=-1.0, scalar2=1.0, op0=ALU.mult, op1=ALU.add),
             reads=[Bc], writes=[Bc])
        P.op('dve', lambda e: e.tensor_scalar(out=omka[:], in0=cv[:, 5:6], scalar1=-1.0, scalar2=1.0, op0=ALU.mult, op1=ALU.add),
             reads=[Bc], writes=[Bc])
        P.op('dve', lambda e: e.tensor_scalar(out=omka2[:], in0=omka[:], scalar1=2.0, scalar2=None, op0=ALU.mult),
             reads=[Bc], writes=[Bc])
        P.op('dve', lambda e: e.tensor_scalar(out=bavg[:], in0=bones[:], scalar1=1.0 / 64, scalar2=None, op0=ALU.mult),
             reads=[Bc], writes=[Bc])
        NBU = 2
        uT = [sb(f"ruT{i}", [128, 8, 258], BF16) for i in range(NBU)]
        BuT = [Buf() for _ in range(NBU)]
        _ppj = ps("rpp", [128, 512], F32)
        ppj = [_ppj, _ppj]
        _bj = Buf()
        Bppj = [_bj, _bj]
        pmisc = ps("rpmisc", [128, 512], F32)
        pax = [pmisc[:, 256:512], pmisc[:, 256:512]]
        _bp = Buf()
        Bpax = [_bp, _bp]
        FT = {}

        def feat(name, dt=F32, n=2, w=256):
            FT[name] = ([sb(f"f_{name}{i}", [128, w], dt) for i in range(n)], [Buf() for _ in range(n)])
        for nm in ["t1", "r", "k", "v", "lw", "rate", "kk", "kdir", "bb", "cum", "epos", "eneg", "eprev", "tmp", "tmp2"]:
            feat(nm)
        feat("vb", BF16)
        feat("wd", BF16)
        feat("gd", BF16)
        feat("ad", BF16)
        feat("ptot", F32, 2, 4)
        FM = [sb(f"FM{i}", [128, 4, 4, 64], BF16) for i in range(2)]
        BFM = [Buf() for _ in range(2)]
        pTM = ps("pTM", [128, 4, 4, 64], BF16)
        pSM = ps("pSM", [128, 4, 2, 128], F32)
        pPOW = ps("pPOW", [128, 4, 128], F32)
        pW0 = ps("pW0", [128, 4, 128], F32)
        pWS = ps("pWS", [128, 4, 128], F32)
        pY = [pmisc[:, 0:256]]
        BpTM, BpSM, BpPOW, BpW0, BpWS = Buf(), Buf(), Buf(), Buf(), Buf()
        BpY = [_bp]
        TM = sb("TM", [128, 4, 4, 64], BF16)
        SM = sb("SM", [128, 4, 2, 128], BF16)
        L1 = sb("L1", [128, 4, 64], F32)
        LTd = sb("LTd", [128, 4, 2, 64], F32)
        POW = sb("POW", [128, 4, 4, 128], F32)
        Wf = sb("Wf", [128, 4, 128], F32)
        Cf = sb("Cf", [128, 4, 128], F32)
        Zf = sb("Zf", [128, 4, 128], F32)
        TTf = sb("TTf", [128, 4, 64], F32)
        BZf, BTT = Buf(), Buf()
        Wb = sb("Wb", [128, 4, 128], BF16)
        QT = sb("QT", [128, 4, 64], BF16)
        GT = sb("GT", [128, 4, 64], BF16)
        BTM, BSM, BL1, BLTd, BWf, BCf, BWb, BQT, BGT = [Buf() for _ in range(9)]
        BPOW = [Buf() for _ in range(4)]
        Hf = sb("Hf", [128, 64], F32)
        Hb = [sb(f"Hb{i}", [128, 64], BF16) for i in range(2)]
        BHf = Buf()
        BHb = [Buf() for _ in range(2)]
        yS = sb("yS", [128, 256], F32)
        cen = sb("cen", [128, 256], F32)
        sq2 = sb("sq2", [128, 256], F32)
        rstd = sb("rstd", [128, 256], F32)
        bon = sb("bon", [128, 256], F32)
        gS = sb("gS", [128, 256], F32)
        oR = [sb(f"oR{i}", [128, 256], BF16) for i in range(2)]
        ByS, Bcen, Bsq2, Brstd, Bbon, BgS = Buf(), Buf(), Buf(), Buf(), Buf(), Buf()
        BoR = [Buf() for _ in range(2)]
        Bgin = Buf()
        hsl = [slice(0, 64), slice(64, 128)]
        state = {"hcur": 0, "tile_it": 0, "lane_it": 0}

        def F(name, k):
            a, b = FT[name]
            return a[k], b[k]

        def proj(cc, k, ku, ncol=128):
            pk = state.setdefault("pk", 0)
            state["pk"] = 1 - pk
            for kc in range(8):
                P.op('pe', lambda e, pk=pk, kc=kc, cc=cc, ku=ku, ncol=ncol: e.matmul(
                    ppj[pk][0:ncol, 0:258], lhsT=wb[:, kc, cc * 128:cc * 128 + ncol], rhs=uT[ku][:, kc, :],
                    start=(kc == 0), stop=(kc == 7)), reads=[BuT[ku], Bc], writes=[Bppj[pk]])
            return pk

        def shift(cc, pk, k, dst, Bdst, ncol=128):
            t1, Bt1 = F("t1", k)
            P.op('act', lambda e: e.activation(out=t1[0:ncol, :], in_=ppj[pk][0:ncol, 1:257], func=AF.Identity, scale=c0[0:ncol, cc:cc + 1]),
                 reads=[Bppj[pk], Bc], writes=[Bt1])
            P.op('dve', lambda e: e.scalar_tensor_tensor(out=t1[0:ncol, :], in0=ppj[pk][0:ncol, 0:256], scalar=mu[0:ncol, 0, cc:cc + 1],
                                                         in1=t1[0:ncol, :], op0=ALU.mult, op1=ALU.add),
                 reads=[Bppj[pk], Bt1, Bc], writes=[Bt1])
            P.op('dve', lambda e: e.scalar_tensor_tensor(out=dst[0:ncol, :], in0=ppj[pk][0:ncol, 2:258], scalar=mu[0:ncol, 1, cc:cc + 1],
                                                         in1=t1[0:ncol, :], op0=ALU.mult, op1=ALU.add),
                 reads=[Bppj[pk], Bt1, Bc], writes=[Bdst])

        def do_tile(d, tok0, is_ctx, first, last):
            k = state["tile_it"] % 2
            state["tile_it"] += 1
            ku = k
            lo = 0 if first else -1
            hi = 256 if last else 257
            if first:
                P.op('pool', lambda e: e.memset(uT[ku][:, :, 0:1], 0.0), writes=[BuT[ku]])
            if last:
                P.op('pool', lambda e: e.memset(uT[ku][:, :, 257:258], 0.0), writes=[BuT[ku]])
            P.dma('sp', lambda e: e.dma_start(out=uT[ku][:, :, 1 + lo:1 + hi], in_=Uv[:, :, tok0 + lo:tok0 + hi]), writes=[BuT[ku]])
            r, Br = F("r", k)
            kf, Bk = F("k", k)
            v, Bv = F("v", k)
            vb, Bvb = F("vb", k)
            wd, Bwd = F("wd", k)
            gd, Bgd = F("gd", k)
            ad, Bad = F("ad", k)
            tmp, Btmp = F("tmp", k)
            tmp2, Btmp2 = F("tmp2", k)
            lw, Blw = F("lw", k)
            rate, Brate = F("rate", k)
            kk, Bkk = F("kk", k)
            kdir, Bkdir = F("kdir", k)
            bb, Bbb = F("bb", k)
            cum, Bcum = F("cum", k)
            epos, Bepos = F("epos", k)
            eneg, Beneg = F("eneg", k)
            eprev, Beprev = F("eprev", k)
            ptot, Bptot = F("ptot", k)
            for cc, dst, Bd in [(0, r, Br), (1, kf, Bk), (2, v, Bv), (3, tmp, Btmp)]:
                pk = proj(cc, k, ku)
                shift(cc, pk, k, dst, Bd)
                if cc == 3:
                    P.op('act', lambda e: e.activation(out=wd[:], in_=tmp[:], func=AF.Tanh), reads=[Btmp], writes=[Bwd])
            pk = proj(5, k, ku, 64)
            shift(5, pk, k, tmp2, Btmp2, 64)
            P.op('act', lambda e: e.activation(out=ad[0:64, :], in_=tmp2[0:64, :], func=AF.Copy), reads=[Btmp2], writes=[Bad])
            P.op('pool', lambda e: e.tensor_copy(out=vb[:], in_=v[:]), reads=[Bv], writes=[Bvb])
            ds_ = slice(64 * d, 64 * d + 64)
            P.op('pe', lambda e: e.matmul(pax[0][:, :], lhsT=wupb[ds_, :], rhs=wd[ds_, :], start=True, stop=True),
                 reads=[Bwd, Bc], writes=[Bpax[0]])
            P.op('act', lambda e: e.activation(out=lw[:], in_=pax[0][:, :], func=AF.Sigmoid, bias=cv[:, d:d + 1]),
                 reads=[Bpax[0], Bc], writes=[Blw])
            P.op('pool', lambda e: e.tensor_scalar(out=lw[:], in0=lw[:], scalar1=-0.6065306597126334, scalar2=None, op0=ALU.mult),
                 reads=[Blw], writes=[Blw])
            P.op('pe', lambda e: e.matmul(pax[1][:, :], lhsT=aupb[:, d, :], rhs=ad[0:64, :], start=True, stop=True),
                 reads=[Bad, Bc], writes=[Bpax[1]])
            P.op('act', lambda e: e.activation(out=rate[:], in_=pax[1][:, :], func=AF.Sigmoid, bias=cv[:, 2 + d:3 + d]),
                 reads=[Bpax[1], Bc], writes=[Brate])
            P.op('dve', lambda e: e.tensor_scalar(out=kk[:], in0=kf[:], scalar1=cv[:, 4:5], scalar2=None, op0=ALU.mult),
                 reads=[Bk, Bc], writes=[Bkk])
            P.op('pool', lambda e: e.tensor_tensor(out=tmp[:], in0=kk[:], in1=kk[:], op=ALU.mult), reads=[Bkk], writes=[Btmp])
            P.op('pe', lambda e: e.matmul(pax[0][:, :], lhsT=bones[:], rhs=tmp[:], start=True, stop=True),
                 reads=[Btmp, Bc], writes=[Bpax[0]])
            P.op('dve', lambda e: e.tensor_scalar(out=tmp2[:], in0=pax[0][:, :], scalar1=1e-24, scalar2=None, op0=ALU.max),
                 reads=[Bpax[0]], writes=[Btmp2])
            P.op('act', lambda e: e.activation(out=tmp2[:], in_=tmp2[:], func=AF.Sqrt), reads=[Btmp2], writes=[Btmp2])
            P.op('dve', lambda e: e.reciprocal(out=tmp2[:], in_=tmp2[:]), reads=[Btmp2], writes=[Btmp2])
            P.op('dve', lambda e: e.tensor_tensor(out=kk[:], in0=kk[:], in1=tmp2[:], op=ALU.mult), reads=[Bkk, Btmp2], writes=[Bkk])
            P.op('dve', lambda e: e.tensor_scalar(out=kdir[:], in0=rate[:], scalar1=cv[:, 5:6], scalar2=omka[:, 0:1],
                                                  op0=ALU.mult, op1=ALU.add), reads=[Brate, Bc], writes=[Bkdir])
            P.op('pool', lambda e: e.tensor_tensor(out=kdir[:], in0=kdir[:], in1=kf[:], op=ALU.mult), reads=[Bkdir, Bk], writes=[Bkdir])
            P.op('pool', lambda e: e.tensor_tensor(out=bb[:], in0=kk[:], in1=rate[:], op=ALU.mult), reads=[Bkk, Brate], writes=[Bbb])
            P.op('dve', lambda e: e.tensor_tensor_scan(out=cum[:], data0=reset[:], data1=lw[:], initial=0.0, op0=ALU.mult, op1=ALU.add),
                 reads=[Blw, Bc], writes=[Bcum])
            if d == 1:
                for c in range(4):
                    cs = slice(64 * c, 64 * c + 64)
                    P.op('dve', lambda e, c=c, cs=cs: e.tensor_scalar(out=tmp[:, cs], in0=cum[:, cs], scalar1=-1.0,
                                                                      scalar2=cum[:, 64 * c + 63:64 * c + 64],
                                                                      op0=ALU.mult, op1=ALU.add), reads=[Bcum], writes=[Btmp])
                P.op('dve', lambda e: e.tensor_tensor(out=cum[:], in0=tmp[:], in1=lw[:], op=ALU.add), reads=[Btmp, Blw], writes=[Bcum])
            P.op('act', lambda e: e.activation(out=epos[:], in_=cum[:], func=AF.Exp), reads=[Bcum], writes=[Bepos])
            P.op('act', lambda e: e.activation(out=eneg[:], in_=cum[:], func=AF.Exp, scale=-1.0), reads=[Bcum], writes=[Beneg])
            P.op('pool', lambda e: e.tensor_tensor(out=tmp2[:], in0=cum[:], in1=lw[:], op=ALU.subtract), reads=[Bcum, Blw], writes=[Btmp2])
            P.op('act', lambda e: e.activation(out=eprev[:], in_=tmp2[:], func=AF.Exp), reads=[Btmp2], writes=[Beprev])
            last_col = 63 if d == 0 else 0
            P.op('pool', lambda e: e.tensor_copy(out=ptot[:, 0:4], in_=epos[:].rearrange("p (c t) -> p c t", t=64)[:, :, last_col]),
                 reads=[Bepos], writes=[Bptot])
            fm = FM[k]
            v3 = lambda a: a[:].rearrange("p (c t) -> p c t", t=64)
            P.op('dve', lambda e: e.tensor_tensor(out=fm[:, :, 0, :], in0=v3(bb), in1=v3(eneg), op=ALU.mult),
                 reads=[Bbb, Beneg], writes=[BFM[k]])
            P.op('pool', lambda e: e.tensor_tensor(out=fm[:, :, 1, :], in0=v3(kdir), in1=v3(eneg), op=ALU.mult),
                 reads=[Bkdir, Beneg], writes=[BFM[k]])
            P.op('dve', lambda e: e.scalar_tensor_tensor(out=fm[:, :, 2, :], in0=v3(kk), scalar=-1.0, in1=v3(eprev),
                                                         op0=ALU.mult, op1=ALU.mult), reads=[Bkk, Beprev], writes=[BFM[k]])
            P.op('pool', lambda e: e.tensor_tensor(out=fm[:, :, 3, :], in0=v3(r), in1=v3(epos), op=ALU.mult),
                 reads=[Br, Bepos], writes=[BFM[k]])
            if dbgF is not None and d == 0 and tok0 == TC:
                for j, (a_, b_) in enumerate([(r, Br), (kf, Bk), (v, Bv), (lw, Blw), (rate, Brate), (kk, Bkk), (kdir, Bkdir), (bb, Bbb),
                                              (cum, Bcum), (epos, Bepos), (eneg, Beneg), (eprev, Beprev)]):
                    P.dma('sp', lambda e, j=j, a_=a_: e.dma_start(out=dbgF[j], in_=a_[:]), reads=[b_])
            do_chunks(d, k, is_ctx, vb, Bvb, ptot, Bptot)
            if is_ctx:
                return
            lt0 = tok0 - TC
            if d == 0:
                P.op('act', lambda e: e.activation(out=wkvT[:, lt0:lt0 + 256], in_=pY[0][:, :], func=AF.Copy),
                     reads=[BpY[0]], writes=[Bwkv])
                return
            P.op('dve', lambda e: e.tensor_tensor(out=yS[:], in0=pY[0][:, :], in1=wkvT[:, lt0:lt0 + 256], op=ALU.add),
                 reads=[BpY[0], Bwkv], writes=[ByS])
            P.op('pe', lambda e: e.matmul(pax[0][:, :], lhsT=bavg[:], rhs=yS[:], start=True, stop=True),
                 reads=[ByS, Bc], writes=[Bpax[0]])
            P.op('dve', lambda e: e.tensor_tensor(out=cen[:], in0=yS[:], in1=pax[0][:, :], op=ALU.subtract),
                 reads=[ByS, Bpax[0]], writes=[Bcen])
            P.op('pool', lambda e: e.tensor_tensor(out=sq2[:], in0=cen[:], in1=cen[:], op=ALU.mult), reads=[Bcen], writes=[Bsq2])
            P.op('pe', lambda e: e.matmul(pax[1][:, :], lhsT=bavg[:], rhs=sq2[:], start=True, stop=True),
                 reads=[Bsq2, Bc], writes=[Bpax[1]])
            P.op('dve', lambda e: e.tensor_scalar(out=rstd[:], in0=pax[1][:, :], scalar1=64e-5, scalar2=None, op0=ALU.add),
                 reads=[Bpax[1]], writes=[Brstd])
            P.op('act', lambda e: e.activation(out=rstd[:], in_=rstd[:], func=AF.Sqrt), reads=[Brstd], writes=[Brstd])
            P.op('dve', lambda e: e.reciprocal(out=rstd[:], in_=rstd[:]), reads=[Brstd], writes=[Brstd])
            P.op('dve', lambda e: e.tensor_tensor(out=cen[:], in0=cen[:], in1=rstd[:], op=ALU.mult), reads=[Bcen, Brstd], writes=[Bcen])
            P.op('dve', lambda e: e.tensor_scalar(out=cen[:], in0=cen[:], scalar1=cv[:, 7:8], scalar2=cv[:, 8:9], op0=ALU.mult, op1=ALU.add),
                 reads=[Bcen, Bc], writes=[Bcen])
            P.op('pe', lambda e: e.matmul(pax[0][:, :], lhsT=aupb[:, 0, :], rhs=ad[0:64, :], start=True, stop=True),
                 reads=[Bad, Bc, Bcen], writes=[Bpax[0]])
            P.op('act', lambda e: e.activation(out=tmp[:], in_=pax[0][:, :], func=AF.Sigmoid, bias=cv[:, 2:3]),
                 reads=[Bpax[0], Bc], writes=[Btmp])
            P.op('dve', lambda e: e.tensor_tensor(out=tmp[:], in0=tmp[:], in1=rate[:], op=ALU.add), reads=[Btmp, Brate], writes=[Btmp])
            P.op('dve', lambda e: e.tensor_scalar(out=tmp[:], in0=tmp[:], scalar1=cv[:, 5:6], scalar2=None, op0=ALU.mult),
                 reads=[Btmp, Bc], writes=[Btmp])
            P.op('dve', lambda e: e.tensor_scalar(out=tmp[:], in0=tmp[:], scalar1=omka2[:, 0:1], scalar2=None, op0=ALU.add),
                 reads=[Btmp, Bc], writes=[Btmp])
            P.op('dve', lambda e: e.tensor_tensor(out=tmp[:], in0=tmp[:], in1=kf[:], op=ALU.mult), reads=[Btmp, Bk], writes=[Btmp])
            P.op('dve', lambda e: e.scalar_tensor_tensor(out=tmp[:], in0=tmp[:], scalar=cv[:, 6:7], in1=r[:], op0=ALU.mult, op1=ALU.mult),
                 reads=[Btmp, Br, Bc], writes=[Btmp])
            P.op('pe', lambda e: e.matmul(pax[1][:, :], lhsT=bones[:], rhs=tmp[:], start=True, stop=True),
                 reads=[Btmp, Bc, Brstd], writes=[Bpax[1]])
            P.op('dve', lambda e: e.tensor_tensor(out=bon[:], in0=pax[1][:, :], in1=v[:], op=ALU.mult), reads=[Bpax[1], Bv], writes=[Bbon])
            P.op('dve', lambda e: e.tensor_tensor(out=cen[:], in0=cen[:], in1=bon[:], op=ALU.add), reads=[Bcen, Bbon], writes=[Bcen])
            pk = proj(4, k, ku)
            shift(4, pk, k, tmp2, Btmp2)
            P.op('act', lambda e: e.activation(out=gd[:], in_=tmp2[:], func=AF.Sigmoid), reads=[Btmp2], writes=[Bgd])
            P.op('pe', lambda e: e.matmul(pax[0][:, :], lhsT=gupb[:], rhs=gd[:], start=True, stop=True),
                 reads=[Bgd, Bc], writes=[Bpax[0]])
            P.op('dve', lambda e: e.tensor_tensor(out=oR[k][:], in0=cen[:], in1=pax[0][:, :], op=ALU.mult),
                 reads=[Bcen, Bpax[0]], writes=[BoR[k]])
            P.dma('pool', lambda e: e.dma_start(out=gin.ap()[lt0 // 2048, 128:256, lt0 % 2048:lt0 % 2048 + 256], in_=oR[k][:]), reads=[BoR[k]], writes=[Bgin])

        def do_chunks(d, k, is_ctx, vb, Bvb, ptot, Bptot):
            fm = FM[k]
            Bfm = BFM[k]
            CH = [(c, h, hsl[h]) for c in range(4) for h in range(2)]
            for c, h, hs in CH:
                for j, src in enumerate([fm[hs, c, 2, :], fm[hs, c, 0, :], fm[hs, c, 1, :], vb[hs, 64 * c:64 * c + 64]]):
                    P.op('pe', lambda e, hs=hs, c=c, j=j, src=src: e.transpose(out=pTM[hs, c, j, :], in_=src, identity=identb[hs, hs]),
                         reads=[Bfm, Bvb], writes=[BpTM])
            P.op('act', lambda e: e.activation(out=TM[:], in_=pTM[:], func=AF.Copy), reads=[BpTM], writes=[BTM])
            for c, h, hs in CH:
                for j in range(2):
                    P.op('pe', lambda e, hs=hs, c=c, j=j: e.matmul(pSM[hs, c, j, :], lhsT=fm[hs, c, j, :],
                                                                   rhs=fm[hs, c, 2:4, :], start=True, stop=True),
                         reads=[Bfm], writes=[BpSM])
            P.op('dve', lambda e: e.tensor_tensor(out=SM[:], in0=pSM[:], in1=mk[:, d], op=ALU.mult),
                 reads=[BpSM, Bc], writes=[BSM])
            for jj in range(2):
                P.op('dve', lambda e, jj=jj: e.tensor_tensor(out=LTd[:, :, jj, :], in0=pSM[:, :, 0, 0:64],
                                                             in1=mkd[:, d, :, jj, :], op=ALU.mult),
                     reads=[BpSM, Bc], writes=[BLTd])
            for c, h, hs in CH:
                P.op('pe', lambda e, hs=hs, c=c: e.matmul(pWS[hs, c, 0:64], lhsT=fm[hs, c, 2, :], rhs=fm[hs, c, 0, :], start=True, stop=True),
                     reads=[Bfm], writes=[BpWS])
            P.op('dve', lambda e: e.tensor_tensor(out=L1[:], in0=pWS[:, :, 0:64], in1=mst[:, d], op=ALU.mult),
                 reads=[BpWS, Bc], writes=[BL1])
            P.op('dve', lambda e: e.tensor_tensor(out=TTf[:], in0=LTd[:, :, 0, :], in1=istack[:], op=ALU.add),
                 reads=[BLTd, Bc], writes=[BTT])
            for lvl in range(4):
                if lvl == 0:
                    LTp = lambda hs, c: LTd[hs, c, 0, :]
                    Lp = lambda hs, c: L1[hs, c, :]
                    rd = [BLTd, BL1]
                else:
                    LTp = lambda hs, c, lvl=lvl: POW[hs, lvl - 1, c, 0:64]
                    Lp = lambda hs, c, lvl=lvl: POW[hs, lvl - 1, c, 64:128]
                    rd = [BPOW[lvl - 1]]
                for c, h, hs in CH:
                    if lvl < 3:
                        P.op('pe', lambda e, hs=hs, c=c, LTp=LTp, Lp=Lp: e.matmul(pPOW[hs, c, 0:64], lhsT=Lp(hs, c), rhs=LTp(hs, c), start=True, stop=True),
                             reads=rd, writes=[BpPOW])
                    P.op('pe', lambda e, hs=hs, c=c, LTp=LTp, Lp=Lp: e.matmul(pPOW[hs, c, 64:128], lhsT=LTp(hs, c), rhs=Lp(hs, c), start=True, stop=True),
                         reads=rd, writes=[BpPOW])
                lo_ = 0 if lvl < 3 else 64
                if lvl % 2 == 0:
                    P.op('act', lambda e, lvl=lvl, lo_=lo_: e.activation(out=POW[:, lvl, :, lo_:128], in_=pPOW[:, :, lo_:128], func=AF.Copy),
                         reads=[BpPOW], writes=[BPOW[lvl]])
                else:
                    P.op('dve', lambda e, lvl=lvl, lo_=lo_: e.tensor_copy(out=POW[:, lvl, :, lo_:128], in_=pPOW[:, :, lo_:128]),
                         reads=[BpPOW], writes=[BPOW[lvl]])
                for c, h, hs in CH:
                    P.op('pe', lambda e, hs=hs, c=c, lvl=lvl: e.matmul(pWS[hs, c, 0:64], lhsT=POW[hs, lvl, c, 64:128], rhs=TTf[hs, c, :], start=True, stop=True),
                         reads=[BPOW[lvl], BTT], writes=[BpWS])
                P.op('dve', lambda e: e.tensor_tensor(out=TTf[:], in0=TTf[:], in1=pWS[:, :, 0:64], op=ALU.add),
                     reads=[BTT, BpWS], writes=[BTT])
            for c, h, hs in CH:
                P.op('pe', lambda e, hs=hs, c=c: e.matmul(pW0[hs, c, 64:128], lhsT=SM[hs, c, 1, 0:64], rhs=TM[hs, c, 3, :], start=True, stop=True),
                     reads=[BSM, BTM], writes=[BpW0])
            P.op('pool', lambda e: e.tensor_copy(out=Wf[:, :, 0:64], in_=TM[:, :, 0, :]), reads=[BTM], writes=[BWf])
            P.op('dve', lambda e: e.tensor_copy(out=Wf[:, :, 64:128], in_=pW0[:, :, 64:128]), reads=[BpW0], writes=[BWf])
            for c, h, hs in CH:
                P.op('pe', lambda e, hs=hs, c=c: e.matmul(pWS[hs, c, :], lhsT=TTf[hs, c, :], rhs=Wf[hs, c, :], start=True, stop=True),
                     reads=[BTT, BWf], writes=[BpWS])
            P.op('act', lambda e: e.activation(out=Zf[:], in_=pWS[:], func=AF.Copy), reads=[BpWS], writes=[BZf])
            for c, h, hs in CH:
                P.op('pe', lambda e, hs=hs, c=c: e.matmul(pW0[hs, c, :], lhsT=LTd[hs, c, 1, :], rhs=Zf[hs, c, :], start=True, stop=True),
                     reads=[BLTd, BZf], writes=[BpW0])
            P.op('act', lambda e: e.activation(out=Cf[:], in_=pW0[:], func=AF.Copy), reads=[BpW0], writes=[BCf])
            for c, h, hs in CH:
                P.op('pe', lambda e, hs=hs, c=c: e.matmul(pWS[hs, c, :], lhsT=TTf[hs, c, :], rhs=Cf[hs, c, :], start=True, stop=True),
                     reads=[BTT, BCf], writes=[BpWS])
            P.op('dve', lambda e: e.tensor_tensor(out=Wb[:], in0=Zf[:], in1=pWS[:], op=ALU.add),
                 reads=[BZf, BpWS], writes=[BWb])
            for c, h, hs in CH:
                if not is_ctx:
                    P.op('pe', lambda e, hs=hs, c=c: e.matmul(pPOW[hs, c, 0:64], lhsT=Wb[hs, c, 0:64], rhs=SM[hs, c, 0, 64:128], start=True, stop=True),
                         reads=[BWb, BSM], writes=[BpPOW])
                P.op('pe', lambda e, hs=hs, c=c: e.matmul(pPOW[hs, c, 64:128], lhsT=Wb[hs, c, 0:64], rhs=TM[hs, c, 1, :], start=True, stop=True),
                     reads=[BWb, BTM], writes=[BpPOW])
            if not is_ctx:
                P.op('dve', lambda e: e.tensor_tensor(out=QT[:], in0=pPOW[:, :, 0:64], in1=fm[:, :, 3, :], op=ALU.add),
                     reads=[BpPOW, Bfm], writes=[BQT])
            P.op('dve', lambda e: e.tensor_tensor(out=GT[:], in0=pPOW[:, :, 64:128], in1=istack[:], op=ALU.add),
                 reads=[BpPOW, Bc], writes=[BGT])
            corder = range(4) if d == 0 else range(3, -1, -1)
            for c in corder:
                hc = state["hcur"]
                hn = 1 - hc
                if not is_ctx:
                    for h in range(2):
                        hs = hsl[h]
                        ysl = slice(64 * c, 64 * c + 64)
                        P.op('pe', lambda e, hs=hs, ysl=ysl, c=c: e.matmul(pY[0][hs, ysl], lhsT=Wb[hs, c, 64:128], rhs=SM[hs, c, 0, 64:128], start=True, stop=False),
                             reads=[BWb, BSM], writes=[BpY[0]])
                        P.op('pe', lambda e, hs=hs, ysl=ysl, c=c: e.matmul(pY[0][hs, ysl], lhsT=TM[hs, c, 3, :], rhs=SM[hs, c, 1, 64:128], start=False, stop=False),
                             reads=[BTM, BSM], writes=[BpY[0]])
                        P.op('pe', lambda e, hs=hs, ysl=ysl, c=c, hc=hc: e.matmul(pY[0][hs, ysl], lhsT=Hb[hc][hs, :], rhs=QT[hs, c, :], start=False, stop=True),
                             reads=[BHb[hc], BQT], writes=[BpY[0]])
                for h in range(2):
                    hs = hsl[h]
                    P.op('pe', lambda e, hs=hs, c=c: e.matmul(pW0[hs, c, 0:64], lhsT=TM[hs, c, 1, :], rhs=Wb[hs, c, 64:128], start=True, stop=False),
                         reads=[BTM, BWb], writes=[BpW0])
                    P.op('pe', lambda e, hs=hs, c=c: e.matmul(pW0[hs, c, 0:64], lhsT=TM[hs, c, 2, :], rhs=TM[hs, c, 3, :], start=False, stop=False),
                         reads=[BTM], writes=[BpW0])
                    P.op('pe', lambda e, hs=hs, c=c, hc=hc: e.matmul(pW0[hs, c, 0:64], lhsT=GT[hs, c, :], rhs=Hb[hc][hs, :], start=False, stop=True),
                         reads=[BGT, BHb[hc]], writes=[BpW0])
                P.op('act', lambda e, c=c: e.activation(out=Hf[:], in_=pW0[:, c, 0:64], func=AF.Identity, scale=ptot[:, c:c + 1]),
                     reads=[BpW0, Bptot], writes=[BHf])
                P.op('dve', lambda e, hn=hn: e.tensor_copy(out=Hb[hn][:], in_=Hf[:]), reads=[BHf], writes=[BHb[hn]])
                state["hcur"] = hn

        for d in range(2):
            if d == 1 and dbgW is not None:
                for j in range(8):
                    P.dma('sp', lambda e, j=j: e.dma_start(out=dbgW[:, j * 2048:(j + 1) * 2048], in_=wkvT[:, j * 2048:(j + 1) * 2048]),
                          reads=[Bwkv])
            P.op('dve', lambda e: e.memset(Hb[state["hcur"]][:], 0.0), writes=[BHb[state["hcur"]]])
            do_tile(d, 0, True, True, True)
            order = range(64) if d == 0 else range(63, -1, -1)
            for ti in order:
                do_tile(d, TC + ti * 256, False, ti == 0, ti == 63)
        P.flush()


def phase_tail(nc, I, gin, gout, out, modbc, identb, U=None, dbgU=None, dbgG=None):
    with contextlib.ExitStack() as st:
        P = Prog(nc)

        def sb(name, shape, dt):
            return st.enter_context(nc.sbuf_tensor('S%d_' % P.uid + name, shape, dt))

        def ps(name, shape, dt):
            return st.enter_context(nc.psum_tensor('P%d_' % P.uid + name, shape, dt))
        Bg = Buf()
        if dbgU is not None:
            for j in range(8):
                P.dma('sp', lambda e, j=j: e.dma_start(out=dbgU[j * 128:(j + 1) * 128, :], in_=U[j * 128:(j + 1) * 128, :]))
                P.dma('sp', lambda e, j=j: e.dma_start(out=dbgG[j], in_=gin.ap()[j]))
        for gi in range(8):
            P.dma('pool', lambda e, gi=gi: e.collective_compute("AllGather", ALU.bypass, replica_groups=[[0, 1, 2, 3], [4, 5, 6, 7]],
                                                                 ins=[gin.ap()[gi].opt()], outs=[gout.ap()[gi].opt()]), writes=[Bg], inc=1)
        wo = sb("wo", [128, 8, D], BF16)
        w1 = sb("w1", [128, 8, 5632], BF16)
        w2 = sb("w2", [128, 22, D], BF16)
        stg = [sb(f"stg{i}", [128, 1024], F32) for i in range(2)]
        Bstg = [Buf() for _ in range(2)]
        Bw = Buf()
        si = 0
        jobs = []
        wov = I["w_out"].rearrange("(kc p) n -> p kc n", p=128)
        w1v = I["ffn_w_in"].rearrange("(kc p) n -> p kc n", p=128)
        w2v = I["ffn_w_out"].rearrange("(kc p) n -> p kc n", p=128)
        for kc in range(0, 8, 2):
            pass
        for kc in range(8):
            jobs.append((wov[:, kc:kc + 1, :], wo[:, kc:kc + 1, :], 1, D))
        for kc in range(8):
            for n0 in range(0, 5632, 1024):
                n1 = min(5632, n0 + 1024)
                jobs.append((w1v[:, kc:kc + 1, n0:n1], w1[:, kc:kc + 1, n0:n1], 1, n1 - n0))
        for kc in range(22):
            jobs.append((w2v[:, kc:kc + 1, :], w2[:, kc:kc + 1, :], 1, D))
        for ji, (src, dst, a, b) in enumerate(jobs):
            k = ji % 2
            sv = stg[k][:, 0:a * b].rearrange("p (a b) -> p a b", b=b)
            P.dma('sp', lambda e, sv=sv, src=src: e.dma_start(out=sv, in_=src), writes=[Bstg[k]])
            eng = ['dve', 'pool', 'act'][ji % 3]
            if eng == 'act':
                P.op('act', lambda e, sv=sv, dst=dst: e.activation(out=dst, in_=sv, func=AF.Copy), reads=[Bstg[k]], writes=[Bw])
            else:
                P.op(eng, lambda e, sv=sv, dst=dst: e.tensor_copy(out=dst, in_=sv), reads=[Bstg[k]], writes=[Bw])
        reg = st.enter_context(nc.sync.register("qoffr"))
        stt = {"val": None}

        def ldreg(e):
            ins = e.reg_load(reg, I["qoff"][0:1, 0:1])
            stt["val"] = e.snap(reg)
            return ins
        P.q['sp'].append(('raw', ldreg))
        goutq = gout.ap().rearrange("(q a) f t -> q a f t", a=2)
        oq = nc.dram_tensor("oq_scr", [1024, 4096], BF16).ap()
        oqv = oq.rearrange("(fc p) t -> p fc t", p=128)
        Boq = Buf()
        for fc in range(8):
            for a in range(2):
                def cpq(e, fc=fc, a=a):
                    return e.dma_start(out=oq[fc * 128:(fc + 1) * 128, a * 2048:(a + 1) * 2048],
                                       in_=goutq[bass.ds(stt["val"], 1), a, fc * 128:(fc + 1) * 128, :].squeeze(0))
                P.dma('sp', cpq, reads=[Bg], writes=[Boq])
        NB = 2
        oT = [sb(f"toT{i}", [128, 8, 128], BF16) for i in range(NB)]
        xo = [sb(f"xo{i}", [128, D], F32) for i in range(1)] * 2
        BoT = [Buf() for _ in range(NB)]
        Bxo = [Buf()] * 2
        pM = [ps(f"pM{i}", [128, 512], F32) for i in range(2)]
        BpM = [Buf() for _ in range(2)]
        h1 = [sb(f"h1{i}", [128, D], F32) for i in range(1)] * 2
        Bh1 = [Buf()] * 2
        ss = sb("tss", [128, 1], F32)
        Bss = Buf()
        u2 = sb("u2", [128, D], F32)
        u2b = sb("u2b", [128, D], BF16)
        Bu2, Bu2b = Buf(), Buf()
        sq, Bsq = u2, Bu2
        ptr = ps("tptr", [128, 8, 128], BF16)
        Bptr = Buf()
        u2T = sb("u2T", [128, 8, 128], BF16)
        Bu2T = Buf()
        pGb = ps("pGb", [128, 512], F32)
        pG = [pGb[:, 256 * i:256 * i + 256].rearrange("p (a b) -> p a b", b=128) for i in range(2)]
        _bg = Buf()
        BpG = [_bg, _bg]
        sg = [sb(f"sg{i}", [128, 128], F32) for i in range(2)]
        Bsg = [Buf() for _ in range(2)]
        aT = sb("aT", [128, 22, 128], BF16)
        BaT = Buf()
        ob = xo
        Bob = Bxo
        Bout = Buf()
        Bm = Buf()
        for ti in range(32):
            k = ti % NB

            def ldo(e, k=k, ti=ti):
                return e.dma_start(out=oT[k][:], in_=oqv[:, :, ti * 128:(ti + 1) * 128])
            P.dma('sp', ldo, reads=[Boq], writes=[BoT[k]])
            P.dma('sp', lambda e, k=k, ti=ti: e.dma_start(out=xo[k][:], in_=I["x_own"][ti * 128:(ti + 1) * 128, :]), writes=[Bxo[k]])
            for nh in range(2):
                for fc in range(8):
                    P.op('pe', lambda e, k=k, nh=nh, fc=fc: e.matmul(pM[nh][:, :], lhsT=oT[k][:, fc, :], rhs=wo[:, fc, nh * 512:(nh + 1) * 512],
                                                                      start=(fc == 0), stop=(fc == 7)),
                         reads=[BoT[k], Bw], writes=[BpM[nh]])
                P.op('dve', lambda e, k=k, nh=nh: e.tensor_tensor(out=h1[k][:, nh * 512:(nh + 1) * 512], in0=pM[nh][:, :],
                                                                   in1=modbc[:, 0, nh * 512:(nh + 1) * 512], op=ALU.mult),
                     reads=[BpM[nh], Bm], writes=[Bh1[k]])
            P.op('pool', lambda e, k=k: e.tensor_tensor(out=h1[k][:], in0=h1[k][:], in1=xo[k][:], op=ALU.add),
                 reads=[Bh1[k], Bxo[k]], writes=[Bh1[k]])
            P.op('act', lambda e, k=k: e.activation(out=sq[:], in_=h1[k][:], func=AF.Square, accum_out=ss[:]),
                 reads=[Bh1[k]], writes=[Bsq, Bss])
            P.op('dve', lambda e: e.tensor_scalar(out=ss[:], in0=ss[:], scalar1=1.0 / D, scalar2=EPS, op0=ALU.mult, op1=ALU.add),
                 reads=[Bss], writes=[Bss])
            P.op('act', lambda e: e.activation(out=ss[:], in_=ss[:], func=AF.Sqrt), reads=[Bss], writes=[Bss])
            P.op('dve', lambda e: e.reciprocal(out=ss[:], in_=ss[:]), reads=[Bss], writes=[Bss])
            P.op('dve', lambda e, k=k: e.scalar_tensor_tensor(out=u2[:], in0=h1[k][:], scalar=ss[:, 0:1], in1=modbc[:, 2, :],
                                                              op0=ALU.mult, op1=ALU.mult), reads=[Bh1[k], Bss, Bm], writes=[Bu2])
            P.op('pool', lambda e: e.tensor_tensor(out=u2b[:], in0=u2[:], in1=modbc[:, 1, :], op=ALU.add), reads=[Bu2, Bm], writes=[Bu2b])
            for kc in range(8):
                P.op('pe', lambda e, kc=kc: e.transpose(out=ptr[:, kc, :], in_=u2b[:, kc * 128:(kc + 1) * 128], identity=identb[:]),
                     reads=[Bu2b], writes=[Bptr])
            P.op('act', lambda e: e.activation(out=u2T[:], in_=ptr[:], func=AF.Copy), reads=[Bptr], writes=[Bu2T])
            for fc in range(22):
                g2 = fc % 2
                for part in range(2):
                    c0_ = part * 2816 + fc * 128
                    for kc in range(8):
                        P.op('pe', lambda e, g2=g2, part=part, c0_=c0_, kc=kc: e.matmul(
                            pG[g2][:, part, :], lhsT=w1[:, kc, c0_:c0_ + 128], rhs=u2T[:, kc, :], start=(kc == 0), stop=(kc == 7)),
                            reads=[Bw, Bu2T], writes=[BpG[g2]])
                P.op('act', lambda e, g2=g2: e.activation(out=sg[g2][:], in_=pG[g2][:, 0, :], func=AF.Silu), reads=[BpG[g2]], writes=[Bsg[g2]])
                P.op('dve', lambda e, g2=g2, fc=fc: e.tensor_tensor(out=aT[:, fc, :], in0=sg[g2][:], in1=pG[g2][:, 1, :], op=ALU.mult),
                     reads=[Bsg[g2], BpG[g2]], writes=[BaT])
            for nh in range(2):
                for fc in range(22):
                    P.op('pe', lambda e, nh=nh, fc=fc: e.matmul(pM[nh][:, :], lhsT=aT[:, fc, :], rhs=w2[:, fc, nh * 512:(nh + 1) * 512],
                                                                start=(fc == 0), stop=(fc == 21)),
                         reads=[BaT, Bw], writes=[BpM[nh]])
                P.op('dve', lambda e, k=k, nh=nh: e.tensor_tensor(out=ob[k][:, nh * 512:(nh + 1) * 512], in0=pM[nh][:, :],
                                                                   in1=modbc[:, 3, nh * 512:(nh + 1) * 512], op=ALU.mult),
                     reads=[BpM[nh], Bm], writes=[Bob[k]])
            P.op('pool', lambda e, k=k: e.tensor_tensor(out=ob[k][:], in0=ob[k][:], in1=h1[k][:], op=ALU.add),
                 reads=[Bob[k], Bh1[k]], writes=[Bob[k]])
            P.dma('pool', lambda e, k=k, ti=ti: e.dma_start(out=out[ti * 128:(ti + 1) * 128, :], in_=ob[k][:]), reads=[Bob[k]], writes=[Bout])
        P.flush()


def _bias_tiles(rpb2):
    blocks = na_blocks()
    combos = {}
    for m in (2, 0, 1, 126, 127):
        for kt, slot in blocks[m]:
            combos[slot] = (m, kt)
    kr_l, kc_ = np.divmod(np.arange(128), 64)
    outb = np.full((2, 21, 128, 128), NEG, np.float32)
    for slot, (m, kt) in combos.items():
        qr = 2 * m + kr_l[None, :]
        qc = kc_[None, :]
        kr = 2 * kt + kr_l[:, None]
        kc = kc_[:, None]
        rs = np.clip(qr - 4, 0, 248)
        cs = np.clip(qc - 8, 0, 48)
        ok = (kr >= rs) & (kr < rs + 8) & (kc >= cs) & (kc < cs + 16)
        di = np.clip(kr - qr + 7, 0, 14)
        dj = np.clip(kc - qc + 15, 0, 30)
        for h in range(2):
            outb[h, slot] = np.where(ok, rpb2[h][di, dj], np.float32(NEG))
    return np.ascontiguousarray(outb.transpose(2, 0, 1, 3))


_NC_CACHE = {}


def kernel(**inp):
    f = lambda a: np.ascontiguousarray(np.asarray(a, dtype=np.float32))
    x, c, ctx, c_ctx = f(inp["x"]), f(inp["c"]), f(inp["ctx"]), f(inp["c_ctx"])
    w_in = f(inp["w_in"])[0]
    w_ada = f(inp["w_ada"])[0]
    b_ada = f(inp["b_ada"])[0]
    mu_p = f(inp["rw_mu_prev"])[0]
    mu_n = f(inp["rw_mu_next"])[0]
    ident = np.eye(128, dtype=np.float32)
    bones = np.kron(np.eye(2, dtype=np.float32), np.ones((64, 64), np.float32))
    s_i, t_i = np.meshgrid(np.arange(64), np.arange(64), indexing="ij")
    mk = np.zeros((128, 2, 2, 128), np.float32)
    mst = np.zeros((128, 2, 64), np.float32)
    for d in range(2):
        strict = (s_i < t_i) if d == 0 else (s_i > t_i)
        incl = (s_i <= t_i) if d == 0 else (s_i >= t_i)
        blk = np.concatenate([strict, incl], axis=1).astype(np.float32)
        for h in range(2):
            for j in range(2):
                mk[64 * h:64 * h + 64, d, j, :] = blk
            mst[64 * h:64 * h + 64, d, :] = strict.T.astype(np.float32)
    mkd = np.zeros((128, 2, 2, 64), np.float32)
    bd = (s_i // 32 == t_i // 32)
    for d in range(2):
        strict = (s_i < t_i) if d == 0 else (s_i > t_i)
        for h in range(2):
            mkd[64 * h:64 * h + 64, d, 0, :] = (strict & bd).astype(np.float32)
            mkd[64 * h:64 * h + 64, d, 1, :] = (strict & ~bd).astype(np.float32)
            mst[64 * h:64 * h + 64, d, :] = (strict & bd).T.astype(np.float32)
    istack = np.concatenate([np.eye(64, dtype=np.float32)] * 2, axis=0)
    reset = np.ones((128, 256), np.float32)
    reset[:, 0::64] = 0.0
    in_maps = []
    for core in range(8):
        b, g = divmod(core, 4)
        hc = slice(128 * g, 128 * g + 128)
        m = {}
        m["xcat"] = np.concatenate([ctx[b], x[b]], axis=0)
        m["x_own"] = x[b, 4096 * g:4096 * (g + 1)]
        c2 = np.stack([c[b], c_ctx], axis=0)
        m["c2T"] = c2.reshape(2, 8, 128).transpose(2, 1, 0)
        m["w_ada"] = w_ada
        m["b_adaT"] = b_ada.reshape(48, 128).T
        m["b_ada_bc"] = np.broadcast_to(b_ada[None, 2 * D:], (128, 4 * D))
        m["g1T"] = f(inp["norm1_g"])[0].reshape(8, 128).T
        m["g2_bc"] = np.broadcast_to(f(inp["norm2_g"])[0][None, :], (128, D))
        qc = np.arange(128 * g, 128 * g + 128)
        m["w_na"] = w_in[:, np.concatenate([qc, 512 + qc, 1024 + qc])]
        rwc = np.concatenate([1536 + qc, 2048 + qc, 2560 + qc, np.arange(3072, 3200), np.arange(3264, 3392), np.arange(3200, 3264)])
        m["w_rw"] = w_in[:, rwc]
        rc_ = rwc - 1536
        muT = np.zeros((128, 2, 6), np.float32)
        for j, mv in enumerate((mu_p, mu_n)):
            col = np.zeros(768, np.float32)
            col[:704] = mv[rc_]
            muT[:, j, :] = col.reshape(6, 128).T
        m["muT"] = muT
        gq = f(inp["na_q_g"])[0]
        gk = f(inp["na_k_g"])[0]
        m["gqk_bc"] = np.broadcast_to(np.stack([gq, gk], 0)[None], (128, 2, 64))
        m["bias"] = _bias_tiles(f(inp["na_rpb"])[0][2 * g:2 * g + 2])
        m["w_upT"] = f(inp["rw_w_up"])[0][:, :, hc].reshape(128, 128)
        m["a_upT"] = f(inp["rw_a_up"])[0][:, :, hc].transpose(1, 0, 2)
        m["g_up"] = f(inp["rw_g_up"])[0][:, hc]
        cvv = np.stack([f(inp["rw_w0"])[0][0, hc], f(inp["rw_w0"])[0][1, hc], f(inp["rw_a0"])[0][0, hc], f(inp["rw_a0"])[0][1, hc],
                        f(inp["rw_k_k"])[0][hc], f(inp["rw_k_a"])[0][hc], f(inp["rw_r_k"])[0].reshape(512)[hc],
                        f(inp["rw_ln_g"])[0][hc], f(inp["rw_ln_b"])[0][hc]], axis=1)
        m["cv"] = cvv
        perm = np.concatenate([np.concatenate([np.arange(128 * gg, 128 * gg + 128), 512 + np.arange(128 * gg, 128 * gg + 128)])
                               for gg in range(4)])
        m["w_out"] = f(inp["w_out"])[0][perm]
        m["ffn_w_in"] = f(inp["ffn_w_in"])[0]
        m["ffn_w_out"] = f(inp["ffn_w_out"])[0]
        m["qoff"] = np.array([[g]], np.int32)
        m["c_ident"] = ident
        m["c_bones"] = bones
        m["c_mk"] = np.broadcast_to(mk[:, :, None], (128, 2, 4, 2, 128))
        m["c_mst"] = np.broadcast_to(mst[:, :, None], (128, 2, 4, 64))
        m["c_mkd"] = np.broadcast_to(mkd[:, :, None], (128, 2, 4, 2, 64))
        m["c_istack"] = np.broadcast_to(istack[:, None], (128, 4, 64))
        m["c_reset"] = reset
        for k_, v_ in m.items():
            want = np.int32 if k_ == "qoff" else np.float32
            m[k_] = np.ascontiguousarray(v_, dtype=want)
            assert list(m[k_].shape) == IN_SPECS[k_][0], (k_, m[k_].shape)
        in_maps.append(m)
    if "nc" not in _NC_CACHE:
        _NC_CACHE["nc"] = build_nc()
    res = run_bass_kernel_spmd(_NC_CACHE["nc"], in_maps, core_ids=list(range(8)))
    outp = np.zeros((2, T, D), np.float32)
    for core in range(8):
        b, g = divmod(core, 4)
        outp[b, 4096 * g:4096 * (g + 1)] = res.results[core]["out"]
    return outp
```

```python
import contextlib
import numpy as np
import concourse.bass as bass
import concourse.mybir as mybir
from concourse.bass_utils import run_bass_kernel_spmd

F32 = mybir.dt.float32
BF16 = mybir.dt.bfloat16
I32 = mybir.dt.int32
AF = mybir.ActivationFunctionType
ALU = mybir.AluOpType
AX = mybir.AxisListType

EP = 30000
T = 16384
TC = 256
NT = T + TC
D = 1024
NEG = -30000.0
EPS = 1e-6


class Buf:
    def __init__(self, name=""):
        self.name = name
        self.w = None
        self.r = {}


class Prog:
    ENGS = ['pe', 'act', 'dve', 'pool', 'sp']
    _uid = [0]

    def __init__(self, nc, ndma=8):
        self.nc = nc
        Prog._uid[0] += 1
        self.uid = Prog._uid[0]
        self.q = {e: [] for e in self.ENGS}
        self.ops = {e: [None] for e in self.ENGS}
        self.seen = {e: {} for e in self.ENGS}
        self.ndma = ndma
        self.dma_cnt = {}
        self.dma_eng = {}
        self.dma_next = {e: 0 for e in self.ENGS}

    def _deps(self, eng, reads, writes):
        deps = {}

        def add(tok):
            if tok is None:
                return
            k, v = tok
            if deps.get(k, 0) < v:
                deps[k] = v
        for b in reads:
            add(b.w)
        for b in writes:
            add(b.w)
            for k, v in b.r.items():
                if k == eng:
                    continue
                add((k, v))
        waits = []
        for k, v in deps.items():
            if k == eng == 'pe':
                continue
            if self.seen[eng].get(k, 0) < v:
                self.seen[eng][k] = v
                waits.append((k, v))
                if not isinstance(k, tuple):
                    self.ops[k][v][4] = True
        return waits

    def op(self, eng, fn, reads=(), writes=()):
        waits = self._deps(eng, reads, writes)
        idx = len(self.ops[eng])
        item = ['op', waits, fn, idx, False]
        self.ops[eng].append(item)
        self.q[eng].append(item)
        for b in reads:
            b.r[eng] = idx
        for b in writes:
            b.w = (eng, idx)
            b.r = {}

    def dma(self, eng, fn, reads=(), writes=(), inc=16):
        slot = self.dma_next[eng]
        self.dma_next[eng] = (slot + 1) % self.ndma
        key = ('dma', eng, slot)
        prev = self.dma_cnt.get(key, 0)
        waits = self._deps(eng, reads, writes)
        if prev > 0 and self.seen[eng].get(key, 0) < prev:
            self.seen[eng][key] = prev
            waits.append((key, prev))
        val = prev + inc
        self.dma_cnt[key] = val
        self.dma_eng[key] = eng
        self.q[eng].append(['dma', waits, fn, key, inc])
        for b in reads:
            b.r[key] = val
        for b in writes:
            b.w = (key, val)
            b.r = {}
        return (key, val)

    def flush(self):
        nc = self.nc
        for e in self.ENGS:
            n = len(self.ops[e]) - 1
            if n >= 1:
                self.ops[e][n][4] = True
                self.q[e].append(['wait', [(e, n)]])
        for key, val in self.dma_cnt.items():
            self.q[self.dma_eng[key]].append(['wait', [(key, val)]])
        sig = {}
        for e in self.ENGS:
            cnt = 0
            arr = [0]
            for it in self.ops[e][1:]:
                if it[4]:
                    cnt += 1
                arr.append(cnt)
            sig[e] = arr
        with contextlib.ExitStack() as st:
            sems = {}
            for e in self.ENGS:
                nep = sig[e][-1] // EP + 1
                for k in range(nep):
                    sems[(e, k)] = st.enter_context(nc.semaphore(f"s{self.uid}_{e}_{k}"))
            for key in self.dma_cnt:
                sems[key] = st.enter_context(nc.semaphore(f"d{self.uid}_{key[1]}_{key[2]}"))
            block = st.enter_context(nc.Block())

            def emit_wait(eng, k, v):
                if isinstance(k, tuple):
                    eng.wait_ge(sems[k], v)
                else:
                    assert self.ops[k][v][4]
                    c = sig[k][v]
                    ep = (c - 1) // EP
                    eng.wait_ge(sems[(k, ep)], c - ep * EP)

            def run(ename, eng):
                for it in self.q[ename]:
                    if it[0] == 'op':
                        _, waits, fn, idx, marked = it
                        for k, v in waits:
                            emit_wait(eng, k, v)
                        ins = fn(eng)
                        if marked:
                            c = sig[ename][idx]
                            ins.then_inc(sems[(ename, (c - 1) // EP)], 1)
                    elif it[0] == 'dma':
                        _, waits, fn, key, inc = it
                        for k, v in waits:
                            emit_wait(eng, k, v)
                        fn(eng).then_inc(sems[key], inc)
                    elif it[0] == 'raw':
                        it[1](eng)
                    else:
                        for k, v in it[1]:
                            emit_wait(eng, k, v)

            @block.tensor
            def _(e):
                run('pe', e)

            @block.scalar
            def _(e):
                run('act', e)

            @block.vector
            def _(e):
                run('dve', e)

            @block.gpsimd
            def _(e):
                run('pool', e)

            @block.sync
            def _(e):
                run('sp', e)


IN_SPECS = {
    "xcat": ([NT, D], F32), "x_own": ([4096, D], F32), "c2T": ([128, 8, 2], F32),
    "w_ada": ([D, 6 * D], F32), "b_adaT": ([128, 48], F32), "b_ada_bc": ([128, 4 * D], F32),
    "g1T": ([128, 8], F32), "g2_bc": ([128, D], F32),
    "w_na": ([D, 384], F32), "w_rw": ([D, 704], F32),
    "muT": ([128, 2, 6], F32), "gqk_bc": ([128, 2, 64], F32),
    "bias": ([128, 2, 21, 128], F32),
    "w_upT": ([128, 128], F32), "a_upT": ([64, 2, 128], F32), "g_up": ([128, 128], F32),
    "cv": ([128, 9], F32),
    "w_out": ([D, D], F32), "ffn_w_in": ([D, 5632], F32), "ffn_w_out": ([2816, D], F32),
    "qoff": ([1, 1], I32),
    "c_ident": ([128, 128], F32), "c_bones": ([128, 128], F32), "c_mk": ([128, 2, 4, 2, 128], F32),
    "c_mst": ([128, 2, 4, 64], F32), "c_mkd": ([128, 2, 4, 2, 64], F32), "c_istack": ([128, 4, 64], F32), "c_reset": ([128, 256], F32),
}


def build_nc():
    nc = bass.Bass("TRN2", target_bir_lowering=False)
    I = {k: nc.dram_tensor(k, s, d, kind="ExternalInput").ap() for k, (s, d) in IN_SPECS.items()}
    out = nc.dram_tensor("out", [4096, D], F32, kind="ExternalOutput").ap()
    U = nc.dram_tensor("U_scr", [D, NT], BF16).ap()
    gin = nc.dram_tensor("gin", [8, 256, 2048], BF16)
    gout = nc.dram_tensor("gout", [8, 1024, 2048], BF16)
    Uv = U.rearrange("(kc p) t -> p kc t", p=128)
    dbgU = dbgG = dbgW = dbgF = None

    outer = contextlib.ExitStack()
    with outer:
        def sbo(name, shape, dt):
            return outer.enter_context(nc.sbuf_tensor('S0_' + name, shape, dt))
        A1 = sbo("A1", [128, 2, 8], F32)
        SH1 = sbo("SH1", [128, 2, 8], F32)
        modbc = sbo("modbc", [128, 4, D], F32)
        identb = sbo("identb", [128, 128], BF16)
        identf = sbo("identf", [128, 128], F32)
        bones = sbo("bones", [128, 128], F32)

        with contextlib.ExitStack() as st:
            P = Prog(nc)

            def sb(name, shape, dt):
                return st.enter_context(nc.sbuf_tensor('S%d_' % P.uid + name, shape, dt))

            def ps(name, shape, dt):
                return st.enter_context(nc.psum_tensor('P%d_' % P.uid + name, shape, dt))
            c2 = sb("c2", [128, 8, 2], F32)
            sT = sb("sT", [128, 8, 2], F32)
            sbc = sb("sbc", [128, 8, 128], F32)
            onesf = sb("onesf", [128, 128], F32)
            bT = sb("bT", [128, 48], F32)
            bbc = sb("bbc", [128, 4 * D], F32)
            g1 = sb("g1", [128, 8], F32)
            g2bc = sb("g2bc", [128, D], F32)
            modT = sb("modT", [128, 2, 48], F32)
            wblk = [sb(f"wblk{i}", [128, 8, D], F32) for i in range(2)]
            pmod = ps("pmod", [128, 8, 2], F32)
            pbc = [ps(f"pbc{i}", [128, 512], F32) for i in range(2)]
            B = {n: Buf(n) for n in ["c2", "sT", "sbc", "onesf", "bT", "bbc", "g1", "g2bc", "modT", "wblk0", "wblk1",
                                     "pmod", "pbc0", "pbc1", "A1", "SH1", "modbc", "ident", "bones"]}
            P.dma('sp', lambda e: e.dma_start(out=c2[:], in_=I["c2T"]), writes=[B["c2"]])
            P.dma('sp', lambda e: e.dma_start(out=bT[:], in_=I["b_adaT"]), writes=[B["bT"]])
            P.dma('sp', lambda e: e.dma_start(out=bbc[:], in_=I["b_ada_bc"]), writes=[B["bbc"]])
            P.dma('sp', lambda e: e.dma_start(out=g1[:], in_=I["g1T"]), writes=[B["g1"]])
            P.dma('sp', lambda e: e.dma_start(out=g2bc[:], in_=I["g2_bc"]), writes=[B["g2bc"]])
            P.dma('sp', lambda e: e.dma_start(out=identf[:], in_=I["c_ident"]), writes=[B["ident"]])
            P.dma('sp', lambda e: e.dma_start(out=bones[:], in_=I["c_bones"]), writes=[B["bones"]])
            P.op('dve', lambda e: e.tensor_copy(out=identb[:], in_=identf[:]), reads=[B["ident"]], writes=[B["ident"]])
            P.op('act', lambda e: e.activation(out=sT[:], in_=c2[:], func=AF.Silu), reads=[B["c2"]], writes=[B["sT"]])
            P.op('dve', lambda e: e.memset(onesf[:], 1.0), writes=[B["onesf"]])
            for kc in range(8):
                P.op('dve', lambda e, kc=kc: e.tensor_scalar(out=sbc[:, kc, :], in0=onesf[:], scalar1=sT[:, kc, 0:1],
                                                             scalar2=None, op0=ALU.mult),
                     reads=[B["onesf"], B["sT"]], writes=[B["sbc"]])
            wv = I["w_ada"].rearrange("(kc p) n -> p kc n", p=128)
            for m in range(6):
                wb = wblk[m % 2]
                Bw = B[f"wblk{m % 2}"]
                for hh in range(2):
                    P.dma('sp', lambda e, m=m, wb=wb, hh=hh: e.dma_start(out=wb[:, 4 * hh:4 * hh + 4, :],
                                                                         in_=wv[:, 4 * hh:4 * hh + 4, m * D:(m + 1) * D]),
                          writes=[Bw])
                for jj in range(8):
                    for kc in range(8):
                        P.op('pe', lambda e, wb=wb, jj=jj, kc=kc: e.matmul(pmod[:, jj, :], lhsT=wb[:, kc, jj * 128:(jj + 1) * 128],
                                                                           rhs=sT[:, kc, :], start=(kc == 0), stop=(kc == 7)),
                             reads=[Bw, B["sT"]], writes=[B["pmod"]])
                for i in range(2):
                    P.op('dve', lambda e, m=m, i=i: e.tensor_tensor(out=modT[:, i, m * 8:(m + 1) * 8], in0=pmod[:, :, i],
                                                                   in1=bT[:, m * 8:(m + 1) * 8], op=ALU.add),
                         reads=[B["pmod"], B["bT"]], writes=[B["modT"]])
                if m >= 2:
                    for nh in range(2):
                        pb = pbc[nh]
                        for kc in range(8):
                            P.op('pe', lambda e, wb=wb, pb=pb, nh=nh, kc=kc: e.matmul(
                                pb[:, :], lhsT=sbc[:, kc, :], rhs=wb[:, kc, nh * 512:(nh + 1) * 512],
                                start=(kc == 0), stop=(kc == 7)),
                                reads=[Bw, B["sbc"]], writes=[B[f"pbc{nh}"]])
                        P.op('dve', lambda e, m=m, pb=pb, nh=nh: e.tensor_tensor(
                            out=modbc[:, m - 2, nh * 512:(nh + 1) * 512], in0=pb[:, :],
                            in1=bbc[:, (m - 2) * D + nh * 512:(m - 2) * D + (nh + 1) * 512], op=ALU.add),
                            reads=[B[f"pbc{nh}"], B["bbc"]], writes=[B["modbc"]])
            P.op('dve', lambda e: e.scalar_tensor_tensor(out=modbc[:, 2, :], in0=modbc[:, 2, :], scalar=1.0, in1=g2bc[:],
                                                         op0=ALU.add, op1=ALU.mult),
                 reads=[B["modbc"], B["g2bc"]], writes=[B["modbc"]])
            for i in range(2):
                P.op('dve', lambda e, i=i: e.scalar_tensor_tensor(out=A1[:, i, :], in0=modT[:, i, 8:16], scalar=1.0, in1=g1[:],
                                                                  op0=ALU.add, op1=ALU.mult),
                     reads=[B["modT"], B["g1"]], writes=[B["A1"]])
                P.op('dve', lambda e, i=i: e.tensor_copy(out=SH1[:, i, :], in_=modT[:, i, 0:8]),
                     reads=[B["modT"]], writes=[B["SH1"]])

            NB = 3
            xt = [sb(f"xt{i}", [128, D], F32) for i in range(NB)]
            sq = sb("sqscr", [128, D], F32)
            ss = [sb(f"ss{i}", [128, 1], F32) for i in range(NB)]
            rs = [sb(f"rs{i}", [128, 1], F32) for i in range(NB)]
            xn = [sb(f"xn{i}", [128, D], BF16) for i in range(NB)]
            uT = [sb(f"uT{i}", [128, 8, 128], BF16) for i in range(NB)]
            ptr = [ps(f"ptr{i}", [128, 8, 128], BF16) for i in range(2)]
            Bx = [Buf() for _ in range(NB)]
            Bss = [Buf() for _ in range(NB)]
            Bxn = [Buf() for _ in range(NB)]
            BuT = [Buf() for _ in range(NB)]
            Bpt = [Buf() for _ in range(2)]
            Bsq = Buf()
            BU = Buf()
            for ti in range(NT // 128):
                k = ti % NB
                i = 1 if ti < 2 else 0
                P.dma('sp', lambda e, ti=ti, k=k: e.dma_start(out=xt[k][:], in_=I["xcat"][ti * 128:(ti + 1) * 128, :]),
                      writes=[Bx[k]])
                P.op('act', lambda e, k=k: e.activation(out=sq[:], in_=xt[k][:], func=AF.Square, accum_out=ss[k][:]),
                     reads=[Bx[k]], writes=[Bsq, Bss[k]])
                P.op('dve', lambda e, k=k: e.tensor_scalar(out=rs[k][:], in0=ss[k][:], scalar1=1.0 / D, scalar2=EPS,
                                                          op0=ALU.mult, op1=ALU.add), reads=[Bss[k]], writes=[Bss[k]])
                P.op('act', lambda e, k=k: e.activation(out=rs[k][:], in_=rs[k][:], func=AF.Sqrt), reads=[Bss[k]], writes=[Bss[k]])
                P.op('dve', lambda e, k=k: e.reciprocal(out=rs[k][:], in_=rs[k][:]), reads=[Bss[k]], writes=[Bss[k]])
                P.op('dve', lambda e, k=k: e.tensor_scalar(out=xn[k][:], in0=xt[k][:], scalar1=rs[k][:, 0:1], scalar2=None,
                                                          op0=ALU.mult), reads=[Bx[k], Bss[k]], writes=[Bxn[k]])
                pk = ti % 2
                for kc in range(8):
                    P.op('pe', lambda e, k=k, pk=pk, kc=kc: e.transpose(out=ptr[pk][:, kc, :], in_=xn[k][:, kc * 128:(kc + 1) * 128],
                                                                        identity=identb[:]),
                         reads=[Bxn[k], B["ident"]], writes=[Bpt[pk]])
                for kc in range(8):
                    eng = 'act' if kc % 2 == 0 else 'dve'
                    if eng == 'act':
                        P.op('act', lambda e, k=k, pk=pk, kc=kc, i=i: e.activation(
                            out=uT[k][:, kc, :], in_=ptr[pk][:, kc, :], func=AF.Identity,
                            bias=SH1[:, i, kc:kc + 1], scale=A1[:, i, kc:kc + 1]),
                            reads=[Bpt[pk], B["A1"], B["SH1"]], writes=[BuT[k]])
                    else:
                        P.op('dve', lambda e, k=k, pk=pk, kc=kc, i=i: e.tensor_scalar(
                            out=uT[k][:, kc, :], in0=ptr[pk][:, kc, :], scalar1=A1[:, i, kc:kc + 1],
                            scalar2=SH1[:, i, kc:kc + 1], op0=ALU.mult, op1=ALU.add),
                            reads=[Bpt[pk], B["A1"], B["SH1"]], writes=[BuT[k]])
                P.dma('pool', lambda e, ti=ti, k=k: e.dma_start(out=Uv[:, :, ti * 128:(ti + 1) * 128], in_=uT[k][:]),
                      reads=[BuT[k]], writes=[BU])
            P.flush()
        nc.all_engine_barrier()
        phase_na(nc, I, Uv, gin, identb)
        nc.all_engine_barrier()
        phase_rw(nc, I, Uv, gin, identb, bones, dbgW, dbgF)
        nc.all_engine_barrier()
        phase_tail(nc, I, gin, gout, out, modbc, identb, U, dbgU, dbgG)
    return nc


def na_blocks():
    res = []
    for m in range(128):
        if m == 0:
            res.append([(kt, 5 + kt) for kt in range(4)])
        elif m == 1:
            res.append([(kt, 9 + kt) for kt in range(4)])
        elif m == 126:
            res.append([(124 + j, 13 + j) for j in range(4)])
        elif m == 127:
            res.append([(124 + j, 17 + j) for j in range(4)])
        else:
            res.append([(m + dl, dl + 2) for dl in range(-2, 3)])
    return res


def phase_na(nc, I, Uv, gin, identb):
    with contextlib.ExitStack() as st:
        P = Prog(nc)

        def sb(name, shape, dt):
            return st.enter_context(nc.sbuf_tensor('S%d_' % P.uid + name, shape, dt))

        def ps(name, shape, dt):
            return st.enter_context(nc.psum_tensor('P%d_' % P.uid + name, shape, dt))
        NTI = NT // 128
        qT = sb("qT", [128, NT], BF16)
        kT = sb("kT", [128, NT], BF16)
        vS = sb("vS", [128, NTI, 2, 65], BF16)
        biasS = sb("biasS", [128, 2, 21, 128], F32)
        wst = sb("wst", [128, 8, 384], F32)
        wb = sb("wnab", [128, 8, 384], BF16)
        gqk = sb("gqk", [128, 2, 64], F32)
        Bq, Bk, Bv, Bbias, Bw, Bg = Buf(), Buf(), Buf(), Buf(), Buf(), Buf()
        Bid = Buf()
        P.dma('sp', lambda e: e.dma_start(out=wst[:], in_=I["w_na"].rearrange("(kc p) n -> p kc n", p=128)), writes=[Bw])
        P.op('dve', lambda e: e.tensor_copy(out=wb[:], in_=wst[:]), reads=[Bw], writes=[Bw])
        P.dma('sp', lambda e: e.dma_start(out=biasS[:], in_=I["bias"]), writes=[Bbias])
        P.dma('sp', lambda e: e.dma_start(out=gqk[:], in_=I["gqk_bc"]), writes=[Bg])
        P.op('dve', lambda e: e.memset(vS[:], 1.0), writes=[Bv])
        NB = 3
        uT = [sb(f"nuT{i}", [128, 8, 128], BF16) for i in range(NB)]
        BuT = [Buf() for _ in range(NB)]
        pp = [ps(f"npp{i}", [128, 512], F32) for i in range(2)]
        nbf = ps("nbf", [128, 1024], BF16)
        Bpp = [Buf() for _ in range(2)]
        sq = sb("nsq", [128, 256], F32)
        ssq = sb("nssq", [128, 4], F32)
        qkn = [sb(f"qkn{i}", [128, 256], BF16) for i in range(2)]
        Bsq, Bssq = Buf(), Buf()
        Bqkn = [Buf() for _ in range(2)]
        ptq = [nbf[:, 256 * i:256 * i + 256].rearrange("p (a b) -> p a b", b=128) for i in range(2)]
        Bnbf = Buf()
        Bptq = [Bnbf, Bnbf]
        for ti in range(NTI):
            k = ti % NB
            k2 = ti % 2
            P.dma('sp', lambda e, ti=ti, k=k: e.dma_start(out=uT[k][:], in_=Uv[:, :, ti * 128:(ti + 1) * 128]), writes=[BuT[k]])
            for kc in range(8):
                P.op('pe', lambda e, k=k, k2=k2, kc=kc: e.matmul(pp[k2][:, 0:384], lhsT=uT[k][:, kc, :], rhs=wb[:, kc, :],
                                                                start=(kc == 0), stop=(kc == 7)),
                     reads=[BuT[k], Bw], writes=[Bpp[k2]])
            P.op('act', lambda e, k2=k2: e.activation(out=sq[:], in_=pp[k2][:, 0:256], func=AF.Square), reads=[Bpp[k2]], writes=[Bsq])
            P.op('dve', lambda e: e.tensor_reduce(out=ssq[:], in_=sq[:].rearrange("p (a b) -> p a b", b=64), axis=AX.X, op=ALU.add),
                 reads=[Bsq], writes=[Bssq])
            P.op('dve', lambda e: e.tensor_scalar(out=ssq[:, 0:2], in0=ssq[:, 0:2], scalar1=64 * EPS, scalar2=None,
                                                  op0=ALU.add), reads=[Bssq], writes=[Bssq])
            P.op('dve', lambda e: e.tensor_scalar(out=ssq[:, 2:4], in0=ssq[:, 2:4], scalar1=1.0 / 64, scalar2=EPS,
                                                  op0=ALU.mult, op1=ALU.add), reads=[Bssq], writes=[Bssq])
            P.op('act', lambda e: e.activation(out=ssq[:], in_=ssq[:], func=AF.Sqrt), reads=[Bssq], writes=[Bssq])
            P.op('dve', lambda e: e.reciprocal(out=ssq[:], in_=ssq[:]), reads=[Bssq], writes=[Bssq])
            for j in range(4):
                P.op('dve', lambda e, k2=k2, j=j: e.scalar_tensor_tensor(
                    out=qkn[k2][:, j * 64:(j + 1) * 64], in0=pp[k2][:, j * 64:(j + 1) * 64], scalar=ssq[:, j:j + 1],
                    in1=gqk[:, j // 2, :], op0=ALU.mult, op1=ALU.mult),
                    reads=[Bpp[k2], Bssq, Bg], writes=[Bqkn[k2]])
            P.op('act', lambda e, k2=k2, ti=ti: e.activation(out=vS[:, ti, :, 0:64],
                                                             in_=pp[k2][:, 256:384].rearrange("p (h d) -> p h d", d=64),
                                                             func=AF.Copy), reads=[Bpp[k2]], writes=[Bv])
            for j in range(2):
                P.op('pe', lambda e, k2=k2, j=j: e.transpose(out=ptq[k2][:, j, :], in_=qkn[k2][:, j * 128:(j + 1) * 128],
                                                             identity=identb[:]), reads=[Bqkn[k2], Bid], writes=[Bptq[k2]])
            P.op('act', lambda e, k2=k2, ti=ti: e.activation(out=qT[:, ti * 128:(ti + 1) * 128], in_=ptq[k2][:, 0, :], func=AF.Copy),
                 reads=[Bptq[k2]], writes=[Bq])
            P.op('dve', lambda e, k2=k2, ti=ti: e.tensor_copy(out=kT[:, ti * 128:(ti + 1) * 128], in_=ptq[k2][:, 1, :]),
                 reads=[Bptq[k2]], writes=[Bk])
        pS = [ps(f"pS{i}", [128, 8, 128], F32) for i in range(2)]
        BpS = [Buf() for _ in range(2)]
        sS = [sb(f"sS{i}", [128, 5, 128], F32) for i in range(2)]
        BsS = [Buf() for _ in range(2)]
        pT = [sb(f"pT{i}", [128, 7, 128], BF16) for i in range(2)]
        BpT = [Buf() for _ in range(2)]
        pOb = ps("pOb", [128, 512], F32)
        pO = [pOb[:, 256 * i:256 * i + 130].rearrange("p (a b) -> p a b", b=65) for i in range(2)]
        _b = Buf()
        BpO = [_b, _b]
        rc = [sb(f"rc{i}", [128, 2], F32) for i in range(2)]
        Brc = [Buf() for _ in range(2)]
        oS = [sb(f"oS{i}", [128, 128], BF16) for i in range(2)]
        BoS = [Buf() for _ in range(2)]
        pOT = [nbf[:, 512:640]]
        BpOT = [Bnbf]
        oT = [sb(f"oT{i}", [128, 128], BF16) for i in range(2)]
        BoT = [Buf() for _ in range(2)]
        Bgin = Buf()
        blocks = na_blocks()
        it = 0
        for m in range(128):
            kl = blocks[m]
            nk = len(kl)
            qc0 = (m + 2) * 128
            mb = m % 2
            for h in range(2):
                x2 = it % 2
                it += 1
                hs = slice(64 * h, 64 * h + 64)
                tiles = [kt + 2 for kt, _ in kl] + [0, 1]
                for j, tt in enumerate(tiles):
                    P.op('pe', lambda e, x2=x2, j=j, tt=tt, hs=hs, qc0=qc0: e.matmul(
                        pS[x2][:, j, :], lhsT=kT[hs, tt * 128:(tt + 1) * 128], rhs=qT[hs, qc0:qc0 + 128], start=True, stop=True),
                        reads=[Bq, Bk], writes=[BpS[x2]])
                s0 = kl[0][1]
                P.op('dve', lambda e, x2=x2, nk=nk, h=h, s0=s0: e.tensor_tensor(
                    out=sS[x2][:, 0:nk, :], in0=pS[x2][:, 0:nk, :], in1=biasS[:, h, s0:s0 + nk, :], op=ALU.add),
                    reads=[BpS[x2], Bbias], writes=[BsS[x2]])
                P.op('act', lambda e, x2=x2, nk=nk: e.activation(out=pT[x2][:, 0:nk, :], in_=sS[x2][:, 0:nk, :], func=AF.Exp),
                     reads=[BsS[x2]], writes=[BpT[x2]])
                P.op('act', lambda e, x2=x2, nk=nk: e.activation(out=pT[x2][:, nk:nk + 2, :], in_=pS[x2][:, nk:nk + 2, :], func=AF.Exp),
                     reads=[BpS[x2]], writes=[BpT[x2]])
                for j, tt in enumerate(tiles):
                    P.op('pe', lambda e, x2=x2, j=j, tt=tt, h=h, mb=mb, n=len(tiles): e.matmul(
                        pO[mb][:, h, :], lhsT=pT[x2][:, j, :], rhs=vS[:, tt, h, :], start=(j == 0), stop=(j == n - 1)),
                        reads=[BpT[x2], Bv], writes=[BpO[mb]])
            P.op('dve', lambda e, mb=mb: e.reciprocal(out=rc[mb][:], in_=pO[mb][:, :, 64]), reads=[BpO[mb]], writes=[Brc[mb]])
            for h in range(2):
                P.op('dve', lambda e, mb=mb, h=h: e.tensor_scalar(out=oS[mb][:, h * 64:(h + 1) * 64], in0=pO[mb][:, h, 0:64],
                                                                   scalar1=rc[mb][:, h:h + 1], scalar2=None, op0=ALU.mult),
                     reads=[BpO[mb], Brc[mb]], writes=[BoS[mb]])
            P.op('pe', lambda e, mb=mb: e.transpose(out=pOT[0][:, :], in_=oS[mb][:, :], identity=identb[:]),
                 reads=[BoS[mb]], writes=[BpOT[0]])
            P.op('act', lambda e, mb=mb: e.activation(out=oT[mb][:], in_=pOT[0][:, :], func=AF.Copy),
                 reads=[BpOT[0]], writes=[BoT[mb]])
            P.dma('pool', lambda e, mb=mb, m=m: e.dma_start(out=gin.ap()[m // 16, 0:128, (m % 16) * 128:(m % 16 + 1) * 128], in_=oT[mb][:]),
                  reads=[BoT[mb]], writes=[Bgin])
        P.flush()


def phase_rw(nc, I, Uv, gin, identb, bones, dbgW=None, dbgF=None):
    with contextlib.ExitStack() as st:
        P = Prog(nc)

        def sb(name, shape, dt):
            return st.enter_context(nc.sbuf_tensor('S%d_' % P.uid + name, shape, dt))

        def ps(name, shape, dt):
            return st.enter_context(nc.psum_tensor('P%d_' % P.uid + name, shape, dt))
        wb = sb("rwb", [128, 8, 704], BF16)
        mu = sb("mu", [128, 2, 6], F32)
        c0 = sb("c0", [128, 6], F32)
        cv = sb("cv", [128, 9], F32)
        omka = sb("omka", [128, 1], F32)
        omka2 = sb("omka2", [128, 1], F32)
        wupf = sb("wupf", [128, 128], F32)
        wupb = sb("wupb", [128, 128], BF16)
        aupf = sb("aupf", [64, 2, 128], F32)
        aupb = sb("aupb", [64, 2, 128], BF16)
        gupf = sb("gupf", [128, 128], F32)
        gupb = sb("gupb", [128, 128], BF16)
        mk = sb("mk", [128, 2, 4, 2, 128], F32)
        mst = sb("mst", [128, 2, 4, 64], F32)
        mkd = sb("mkd", [128, 2, 4, 2, 64], F32)
        istack = sb("istack", [128, 4, 64], F32)
        reset = sb("reset", [128, 256], F32)
        bavg = sb("bavg", [128, 128], F32)
        wkvT = sb("wkvT", [128, T], F32)
        wst = wkvT[:, 0:8 * 704].rearrange("p (a b) -> p a b", b=704)
        Bc = Buf("consts")
        Bwkv = Buf("wkv")
        for dst, src in [(None, I["w_rw"].rearrange("(kc p) n -> p kc n", p=128)), (mu, I["muT"]), (cv, I["cv"]), (wupf, I["w_upT"]),
                         (aupf, I["a_upT"]), (gupf, I["g_up"]), (mk, I["c_mk"]), (mst, I["c_mst"]), (mkd, I["c_mkd"]), (istack, I["c_istack"]),
                         (reset, I["c_reset"])]:
            if dst is None:
                P.dma('sp', lambda e, src=src: e.dma_start(out=wst, in_=src), writes=[Bc, Bwkv])
            else:
                P.dma('sp', lambda e, dst=dst, src=src: e.dma_start(out=dst[:], in_=src), writes=[Bc])
        P.op('dve', lambda e: e.tensor_copy(out=wb[:], in_=wst), reads=[Bc, Bwkv], writes=[Bc])
        P.op('dve', lambda e: e.tensor_copy(out=wupb[:], in_=wupf[:]), reads=[Bc], writes=[Bc])
        P.op('dve', lambda e: e.tensor_copy(out=aupb[:], in_=aupf[:]), reads=[Bc], writes=[Bc])
        P.op('dve', lambda e: e.tensor_copy(out=gupb[:], in_=gupf[:]), reads=[Bc], writes=[Bc])
        P.op('dve', lambda e: e.tensor_tensor(out=c0[:], in0=mu[:, 0, :], in1=mu[:, 1, :], op=ALU.add), reads=[Bc], writes=[Bc])
        P.op('dve', lambda e: e.tensor_scalar(out=c0[:], in0=c0[:], scalar1=-1.0, scalar2=1.0, op0=ALU.mult, op1=ALU.add),
             reads=[Bc], writes=[Bc])
        P.op('dve', lambda e: e.tensor_scalar(out=omka[:], in0=cv[:, 5:6], scalar1=-1.0, scalar2=1.0, op0=ALU.mult, op1=ALU.add),
             reads=[Bc], writes=[Bc])
        P.op('dve', lambda e: e.tensor_scalar(out=omka2[:], in0=omka[:], scalar1=2.0, scalar2=None, op0=ALU.mult),
             reads=[Bc], writes=[Bc])
        P.op('dve', lambda e: e.tensor_scalar(out=bavg[:], in0=bones[:], scalar1=1.0 / 64, scalar2=None, op0=ALU.mult),
             reads=[Bc], writes=[Bc])
        NBU = 2
        uT = [sb(f"ruT{i}", [128, 8, 258], BF16) for i in range(NBU)]
        BuT = [Buf() for _ in range(NBU)]
        _ppj = ps("rpp", [128, 512], F32)
        ppj = [_ppj, _ppj]
        _bj = Buf()
        Bppj = [_bj, _bj]
        pmisc = ps("rpmisc", [128, 512], F32)
        pax = [pmisc[:, 256:512], pmisc[:, 256:512]]
        _bp = Buf()
        Bpax = [_bp, _bp]
        FT = {}

        def feat(name, dt=F32, n=2, w=256):
            FT[name] = ([sb(f"f_{name}{i}", [128, w], dt) for i in range(n)], [Buf() for _ in range(n)])
        for nm in ["t1", "r", "k", "v", "lw", "rate", "kk", "kdir", "bb", "cum", "epos", "eneg", "eprev", "tmp", "tmp2"]:
            feat(nm)
        feat("vb", BF16)
        feat("wd", BF16)
        feat("gd", BF16)
        feat("ad", BF16)
        feat("ptot", F32, 2, 4)
        FM = [sb(f"FM{i}", [128, 4, 4, 64], BF16) for i in range(2)]
        BFM = [Buf() for _ in range(2)]
        pTM = ps("pTM", [128, 4, 4, 64], BF16)
        pSM = ps("pSM", [128, 4, 2, 128], F32)
        pPOW = ps("pPOW", [128, 4, 128], F32)
        pW0 = ps("pW0", [128, 4, 128], F32)
        pWS = ps("pWS", [128, 4, 128], F32)
        pY = [pmisc[:, 0:256]]
        BpTM, BpSM, BpPOW, BpW0, BpWS = Buf(), Buf(), Buf(), Buf(), Buf()
        BpY = [_bp]
        TM = sb("TM", [128, 4, 4, 64], BF16)
        SM = sb("SM", [128, 4, 2, 128], BF16)
        L1 = sb("L1", [128, 4, 64], F32)
        LTd = sb("LTd", [128, 4, 2, 64], F32)
        POW = sb("POW", [128, 4, 4, 128], F32)
        Wf = sb("Wf", [128, 4, 128], F32)
        Cf = sb("Cf", [128, 4, 128], F32)
        Zf = sb("Zf", [128, 4, 128], F32)
        TTf = sb("TTf", [128, 4, 64], F32)
        BZf, BTT = Buf(), Buf()
        Wb = sb("Wb", [128, 4, 128], BF16)
        QT = sb("QT", [128, 4, 64], BF16)
        GT = sb("GT", [128, 4, 64], BF16)
        BTM, BSM, BL1, BLTd, BWf, BCf, BWb, BQT, BGT = [Buf() for _ in range(9)]
        BPOW = [Buf() for _ in range(4)]
        Hf = sb("Hf", [128, 64], F32)
        Hb = [sb(f"Hb{i}", [128, 64], BF16) for i in range(2)]
        BHf = Buf()
        BHb = [Buf() for _ in range(2)]
        yS = sb("yS", [128, 256], F32)
        cen = sb("cen", [128, 256], F32)
        sq2 = sb("sq2", [128, 256], F32)
        rstd = sb("rstd", [128, 256], F32)
        bon = sb("bon", [128, 256], F32)
        gS = sb("gS", [128, 256], F32)
        oR = [sb(f"oR{i}", [128, 256], BF16) for i in range(2)]
        ByS, Bcen, Bsq2, Brstd, Bbon, BgS = Buf(), Buf(), Buf(), Buf(), Buf(), Buf()
        BoR = [Buf() for _ in range(2)]
        Bgin = Buf()
        hsl = [slice(0, 64), slice(64, 128)]
        state = {"hcur": 0, "tile_it": 0, "lane_it": 0}

        def F(name, k):
            a, b = FT[name]
            return a[k], b[k]

        def proj(cc, k, ku, ncol=128):
            pk = state.setdefault("pk", 0)
            state["pk"] = 1 - pk
            for kc in range(8):
                P.op('pe', lambda e, pk=pk, kc=kc, cc=cc, ku=ku, ncol=ncol: e.matmul(
                    ppj[pk][0:ncol, 0:258], lhsT=wb[:, kc, cc * 128:cc * 128 + ncol], rhs=uT[ku][:, kc, :],
                    start=(kc == 0), stop=(kc == 7)), reads=[BuT[ku], Bc], writes=[Bppj[pk]])
            return pk

        def shift(cc, pk, k, dst, Bdst, ncol=128):
            t1, Bt1 = F("t1", k)
            P.op('act', lambda e: e.activation(out=t1[0:ncol, :], in_=ppj[pk][0:ncol, 1:257], func=AF.Identity, scale=c0[0:ncol, cc:cc + 1]),
                 reads=[Bppj[pk], Bc], writes=[Bt1])
            P.op('dve', lambda e: e.scalar_tensor_tensor(out=t1[0:ncol, :], in0=ppj[pk][0:ncol, 0:256], scalar=mu[0:ncol, 0, cc:cc + 1],
                                                         in1=t1[0:ncol, :], op0=ALU.mult, op1=ALU.add),
                 reads=[Bppj[pk], Bt1, Bc], writes=[Bt1])
            P.op('dve', lambda e: e.scalar_tensor_tensor(out=dst[0:ncol, :], in0=ppj[pk][0:ncol, 2:258], scalar=mu[0:ncol, 1, cc:cc + 1],
                                                         in1=t1[0:ncol, :], op0=ALU.mult, op1=ALU.add),
                 reads=[Bppj[pk], Bt1, Bc], writes=[Bdst])

        def do_tile(d, tok0, is_ctx, first, last):
            k = state["tile_it"] % 2
            state["tile_it"] += 1
            ku = k
            lo = 0 if first else -1
            hi = 256 if last else 257
            if first:
                P.op('pool', lambda e: e.memset(uT[ku][:, :, 0:1], 0.0), writes=[BuT[ku]])
            if last:
                P.op('pool', lambda e: e.memset(uT[ku][:, :, 257:258], 0.0), writes=[BuT[ku]])
            P.dma('sp', lambda e: e.dma_start(out=uT[ku][:, :, 1 + lo:1 + hi], in_=Uv[:, :, tok0 + lo:tok0 + hi]), writes=[BuT[ku]])
            r, Br = F("r", k)
            kf, Bk = F("k", k)
            v, Bv = F("v", k)
            vb, Bvb = F("vb", k)
            wd, Bwd = F("wd", k)
            gd, Bgd = F("gd", k)
            ad, Bad = F("ad", k)
            tmp, Btmp = F("tmp", k)
            tmp2, Btmp2 = F("tmp2", k)
            lw, Blw = F("lw", k)
            rate, Brate = F("rate", k)
            kk, Bkk = F("kk", k)
            kdir, Bkdir = F("kdir", k)
            bb, Bbb = F("bb", k)
            cum, Bcum = F("cum", k)
            epos, Bepos = F("epos", k)
            eneg, Beneg = F("eneg", k)
            eprev, Beprev = F("eprev", k)
            ptot, Bptot = F("ptot", k)
            for cc, dst, Bd in [(0, r, Br), (1, kf, Bk), (2, v, Bv), (3, tmp, Btmp)]:
                pk = proj(cc, k, ku)
                shift(cc, pk, k, dst, Bd)
                if cc == 3:
                    P.op('act', lambda e: e.activation(out=wd[:], in_=tmp[:], func=AF.Tanh), reads=[Btmp], writes=[Bwd])
            pk = proj(5, k, ku, 64)
            shift(5, pk, k, tmp2, Btmp2, 64)
            P.op('act', lambda e: e.activation(out=ad[0:64, :], in_=tmp2[0:64, :], func=AF.Copy), reads=[Btmp2], writes=[Bad])
            P.op('pool', lambda e: e.tensor_copy(out=vb[:], in_=v[:]), reads=[Bv], writes=[Bvb])
            ds_ = slice(64 * d, 64 * d + 64)
            P.op('pe', lambda e: e.matmul(pax[0][:, :], lhsT=wupb[ds_, :], rhs=wd[ds_, :], start=True, stop=True),
                 reads=[Bwd, Bc], writes=[Bpax[0]])
            P.op('act', lambda e: e.activation(out=lw[:], in_=pax[0][:, :], func=AF.Sigmoid, bias=cv[:, d:d + 1]),
                 reads=[Bpax[0], Bc], writes=[Blw])
            P.op('pool', lambda e: e.tensor_scalar(out=lw[:], in0=lw[:], scalar1=-0.6065306597126334, scalar2=None, op0=ALU.mult),
                 reads=[Blw], writes=[Blw])
            P.op('pe', lambda e: e.matmul(pax[1][:, :], lhsT=aupb[:, d, :], rhs=ad[0:64, :], start=True, stop=True),
                 reads=[Bad, Bc], writes=[Bpax[1]])
            P.op('act', lambda e: e.activation(out=rate[:], in_=pax[1][:, :], func=AF.Sigmoid, bias=cv[:, 2 + d:3 + d]),
                 reads=[Bpax[1], Bc], writes=[Brate])
            P.op('dve', lambda e: e.tensor_scalar(out=kk[:], in0=kf[:], scalar1=cv[:, 4:5], scalar2=None, op0=ALU.mult),
                 reads=[Bk, Bc], writes=[Bkk])
            P.op('pool', lambda e: e.tensor_tensor(out=tmp[:], in0=kk[:], in1=kk[:], op=ALU.mult), reads=[Bkk], writes=[Btmp])
            P.op('pe', lambda e: e.matmul(pax[0][:, :], lhsT=bones[:], rhs=tmp[:], start=True, stop=True),
                 reads=[Btmp, Bc], writes=[Bpax[0]])
            P.op('dve', lambda e: e.tensor_scalar(out=tmp2[:], in0=pax[0][:, :], scalar1=1e-24, scalar2=None, op0=ALU.max),
                 reads=[Bpax[0]], writes=[Btmp2])
            P.op('act', lambda e: e.activation(out=tmp2[:], in_=tmp2[:], func=AF.Sqrt), reads=[Btmp2], writes=[Btmp2])
            P.op('dve', lambda e: e.reciprocal(out=tmp2[:], in_=tmp2[:]), reads=[Btmp2], writes=[Btmp2])
            P.op('dve', lambda e: e.tensor_tensor(out=kk[:], in0=kk[:], in1=tmp2[:], op=ALU.mult), reads=[Bkk, Btmp2], writes=[Bkk])
            P.op('dve', lambda e: e.tensor_scalar(out=kdir[:], in0=rate[:], scalar1=cv[:, 5:6], scalar2=omka[:, 0:1],
                                                  op0=ALU.mult, op1=ALU.add), reads=[Brate, Bc], writes=[Bkdir])
            P.op('pool', lambda e: e.tensor_tensor(out=kdir[:], in0=kdir[:], in1=kf[:], op=ALU.mult), reads=[Bkdir, Bk], writes=[Bkdir])
            P.op('pool', lambda e: e.tensor_tensor(out=bb[:], in0=kk[:], in1=rate[:], op=ALU.mult), reads=[Bkk, Brate], writes=[Bbb])
            P.op('dve', lambda e: e.tensor_tensor_scan(out=cum[:], data0=reset[:], data1=lw[:], initial=0.0, op0=ALU.mult, op1=ALU.add),
                 reads=[Blw, Bc], writes=[Bcum])
            if d == 1:
                for c in range(4):
                    cs = slice(64 * c, 64 * c + 64)
                    P.op('dve', lambda e, c=c, cs=cs: e.tensor_scalar(out=tmp[:, cs], in0=cum[:, cs], scalar1=-1.0,
                                                                      scalar2=cum[:, 64 * c + 63:64 * c + 64],
                                                                      op0=ALU.mult, op1=ALU.add), reads=[Bcum], writes=[Btmp])
                P.op('dve', lambda e: e.tensor_tensor(out=cum[:], in0=tmp[:], in1=lw[:], op=ALU.add), reads=[Btmp, Blw], writes=[Bcum])
            P.op('act', lambda e: e.activation(out=epos[:], in_=cum[:], func=AF.Exp), reads=[Bcum], writes=[Bepos])
            P.op('act', lambda e: e.activation(out=eneg[:], in_=cum[:], func=AF.Exp, scale=-1.0), reads=[Bcum], writes=[Beneg])
            P.op('pool', lambda e: e.tensor_tensor(out=tmp2[:], in0=cum[:], in1=lw[:], op=ALU.subtract), reads=[Bcum, Blw], writes=[Btmp2])
            P.op('act', lambda e: e.activation(out=eprev[:], in_=tmp2[:], func=AF.Exp), reads=[Btmp2], writes=[Beprev])
            last_col = 63 if d == 0 else 0
            P.op('pool', lambda e: e.tensor_copy(out=ptot[:, 0:4], in_=epos[:].rearrange("p (c t) -> p c t", t=64)[:, :, last_col]),
                 reads=[Bepos], writes=[Bptot])
            fm = FM[k]
            v3 = lambda a: a[:].rearrange("p (c t) -> p c t", t=64)
            P.op('dve', lambda e: e.tensor_tensor(out=fm[:, :, 0, :], in0=v3(bb), in1=v3(eneg), op=ALU.mult),
                 reads=[Bbb, Beneg], writes=[BFM[k]])
            P.op('pool', lambda e: e.tensor_tensor(out=fm[:, :, 1, :], in0=v3(kdir), in1=v3(eneg), op=ALU.mult),
                 reads=[Bkdir, Beneg], writes=[BFM[k]])
            P.op('dve', lambda e: e.scalar_tensor_tensor(out=fm[:, :, 2, :], in0=v3(kk), scalar=-1.0, in1=v3(eprev),
                                                         op0=ALU.mult, op1=ALU.mult), reads=[Bkk, Beprev], writes=[BFM[k]])
            P.op('pool', lambda e: e.tensor_tensor(out=fm[:, :, 3, :], in0=v3(r), in1=v3(epos), op=ALU.mult),
                 reads=[Br, Bepos], writes=[BFM[k]])
            if dbgF is not None and d == 0 and tok0 == TC:
                for j, (a_, b_) in enumerate([(r, Br), (kf, Bk), (v, Bv), (lw, Blw), (rate, Brate), (kk, Bkk), (kdir, Bkdir), (bb, Bbb),
                                              (cum, Bcum), (epos, Bepos), (eneg, Beneg), (eprev, Beprev)]):
                    P.dma('sp', lambda e, j=j, a_=a_: e.dma_start(out=dbgF[j], in_=a_[:]), reads=[b_])
            do_chunks(d, k, is_ctx, vb, Bvb, ptot, Bptot)
            if is_ctx:
                return
            lt0 = tok0 - TC
            if d == 0:
                P.op('act', lambda e: e.activation(out=wkvT[:, lt0:lt0 + 256], in_=pY[0][:, :], func=AF.Copy),
                     reads=[BpY[0]], writes=[Bwkv])
                return
            P.op('dve', lambda e: e.tensor_tensor(out=yS[:], in0=pY[0][:, :], in1=wkvT[:, lt0:lt0 + 256], op=ALU.add),
                 reads=[BpY[0], Bwkv], writes=[ByS])
            P.op('pe', lambda e: e.matmul(pax[0][:, :], lhsT=bavg[:], rhs=yS[:], start=True, stop=True),
                 reads=[ByS, Bc], writes=[Bpax[0]])
            P.op('dve', lambda e: e.tensor_tensor(out=cen[:], in0=yS[:], in1=pax[0][:, :], op=ALU.subtract),
                 reads=[ByS, Bpax[0]], writes=[Bcen])
            P.op('pool', lambda e: e.tensor_tensor(out=sq2[:], in0=cen[:], in1=cen[:], op=ALU.mult), reads=[Bcen], writes=[Bsq2])
            P.op('pe', lambda e: e.matmul(pax[1][:, :], lhsT=bavg[:], rhs=sq2[:], start=True, stop=True),
                 reads=[Bsq2, Bc], writes=[Bpax[1]])
            P.op('dve', lambda e: e.tensor_scalar(out=rstd[:], in0=pax[1][:, :], scalar1=64e-5, scalar2=None, op0=ALU.add),
                 reads=[Bpax[1]], writes=[Brstd])
            P.op('act', lambda e: e.activation(out=rstd[:], in_=rstd[:], func=AF.Sqrt), reads=[Brstd], writes=[Brstd])
            P.op('dve', lambda e: e.reciprocal(out=rstd[:], in_=rstd[:]), reads=[Brstd], writes=[Brstd])
            P.op('dve', lambda e: e.tensor_tensor(out=cen[:], in0=cen[:], in1=rstd[:], op=ALU.mult), reads=[Bcen, Brstd], writes=[Bcen])
            P.op('dve', lambda e: e.tensor_scalar(out=cen[:], in0=cen[:], scalar1=cv[:, 7:8], scalar2=cv[:, 8:9], op0=ALU.mult, op1=ALU.add),
                 reads=[Bcen, Bc], writes=[Bcen])
            P.op('pe', lambda e: e.matmul(pax[0][:, :], lhsT=aupb[:, 0, :], rhs=ad[0:64, :], start=True, stop=True),
                 reads=[Bad, Bc, Bcen], writes=[Bpax[0]])
            P.op('act', lambda e: e.activation(out=tmp[:], in_=pax[0][:, :], func=AF.Sigmoid, bias=cv[:, 2:3]),
                 reads=[Bpax[0], Bc], writes=[Btmp])
            P.op('dve', lambda e: e.tensor_tensor(out=tmp[:], in0=tmp[:], in1=rate[:], op=ALU.add), reads=[Btmp, Brate], writes=[Btmp])
            P.op('dve', lambda e: e.tensor_scalar(out=tmp[:], in0=tmp[:], scalar1=cv[:, 5:6], scalar2=None, op0=ALU.mult),
                 reads=[Btmp, Bc], writes=[Btmp])
            P.op('dve', lambda e: e.tensor_scalar(out=tmp[:], in0=tmp[:], scalar1=omka2[:, 0:1], scalar2=None, op0=ALU.add),
                 reads=[Btmp, Bc], writes=[Btmp])
            P.op('dve', lambda e: e.tensor_tensor(out=tmp[:], in0=tmp[:], in1=kf[:], op=ALU.mult), reads=[Btmp, Bk], writes=[Btmp])
            P.op('dve', lambda e: e.scalar_tensor_tensor(out=tmp[:], in0=tmp[:], scalar=cv[:, 6:7], in1=r[:], op0=ALU.mult, op1=ALU.mult),
                 reads=[Btmp, Br, Bc], writes=[Btmp])
            P.op('pe', lambda e: e.matmul(pax[1][:, :], lhsT=bones[:], rhs=tmp[:], start=True, stop=True),
                 reads=[Btmp, Bc, Brstd], writes=[Bpax[1]])
            P.op('dve', lambda e: e.tensor_tensor(out=bon[:], in0=pax[1][:, :], in1=v[:], op=ALU.mult), reads=[Bpax[1], Bv], writes=[Bbon])
            P.op('dve', lambda e: e.tensor_tensor(out=cen[:], in0=cen[:], in1=bon[:], op=ALU.add), reads=[Bcen, Bbon], writes=[Bcen])
            pk = proj(4, k, ku)
            shift(4, pk, k, tmp2, Btmp2)
            P.op('act', lambda e: e.activation(out=gd[:], in_=tmp2[:], func=AF.Sigmoid), reads=[Btmp2], writes=[Bgd])
            P.op('pe', lambda e: e.matmul(pax[0][:, :], lhsT=gupb[:], rhs=gd[:], start=True, stop=True),
                 reads=[Bgd, Bc], writes=[Bpax[0]])
            P.op('dve', lambda e: e.tensor_tensor(out=oR[k][:], in0=cen[:], in1=pax[0][:, :], op=ALU.mult),
                 reads=[Bcen, Bpax[0]], writes=[BoR[k]])
            P.dma('pool', lambda e: e.dma_start(out=gin.ap()[lt0 // 2048, 128:256, lt0 % 2048:lt0 % 2048 + 256], in_=oR[k][:]), reads=[BoR[k]], writes=[Bgin])

        def do_chunks(d, k, is_ctx, vb, Bvb, ptot, Bptot):
            fm = FM[k]
            Bfm = BFM[k]
            CH = [(c, h, hsl[h]) for c in range(4) for h in range(2)]
            for c, h, hs in CH:
                for j, src in enumerate([fm[hs, c, 2, :], fm[hs, c, 0, :], fm[hs, c, 1, :], vb[hs, 64 * c:64 * c + 64]]):
                    P.op('pe', lambda e, hs=hs, c=c, j=j, src=src: e.transpose(out=pTM[hs, c, j, :], in_=src, identity=identb[hs, hs]),
                         reads=[Bfm, Bvb], writes=[BpTM])
            P.op('act', lambda e: e.activation(out=TM[:], in_=pTM[:], func=AF.Copy), reads=[BpTM], writes=[BTM])
            for c, h, hs in CH:
                for j in range(2):
                    P.op('pe', lambda e, hs=hs, c=c, j=j: e.matmul(pSM[hs, c, j, :], lhsT=fm[hs, c, j, :],
                                                                   rhs=fm[hs, c, 2:4, :], start=True, stop=True),
                         reads=[Bfm], writes=[BpSM])
            P.op('dve', lambda e: e.tensor_tensor(out=SM[:], in0=pSM[:], in1=mk[:, d], op=ALU.mult),
                 reads=[BpSM, Bc], writes=[BSM])
            for jj in range(2):
                P.op('dve', lambda e, jj=jj: e.tensor_tensor(out=LTd[:, :, jj, :], in0=pSM[:, :, 0, 0:64],
                                                             in1=mkd[:, d, :, jj, :], op=ALU.mult),
                     reads=[BpSM, Bc], writes=[BLTd])
            for c, h, hs in CH:
                P.op('pe', lambda e, hs=hs, c=c: e.matmul(pWS[hs, c, 0:64], lhsT=fm[hs, c, 2, :], rhs=fm[hs, c, 0, :], start=True, stop=True),
                     reads=[Bfm], writes=[BpWS])
            P.op('dve', lambda e: e.tensor_tensor(out=L1[:], in0=pWS[:, :, 0:64], in1=mst[:, d], op=ALU.mult),
                 reads=[BpWS, Bc], writes=[BL1])
            P.op('dve', lambda e: e.tensor_tensor(out=TTf[:], in0=LTd[:, :, 0, :], in1=istack[:], op=ALU.add),
                 reads=[BLTd, Bc], writes=[BTT])
            for lvl in range(4):
                if lvl == 0:
                    LTp = lambda hs, c: LTd[hs, c, 0, :]
                    Lp = lambda hs, c: L1[hs, c, :]
                    rd = [BLTd, BL1]
                else:
                    LTp = lambda hs, c, lvl=lvl: POW[hs, lvl - 1, c, 0:64]
                    Lp = lambda hs, c, lvl=lvl: POW[hs, lvl - 1, c, 64:128]
                    rd = [BPOW[lvl - 1]]
                for c, h, hs in CH:
                    if lvl < 3:
                        P.op('pe', lambda e, hs=hs, c=c, LTp=LTp, Lp=Lp: e.matmul(pPOW[hs, c, 0:64], lhsT=Lp(hs, c), rhs=LTp(hs, c), start=True, stop=True),
                             reads=rd, writes=[BpPOW])
                    P.op('pe', lambda e, hs=hs, c=c, LTp=LTp, Lp=Lp: e.matmul(pPOW[hs, c, 64:128], lhsT=LTp(hs, c), rhs=Lp(hs, c), start=True, stop=True),
                         reads=rd, writes=[BpPOW])
                lo_ = 0 if lvl < 3 else 64
                if lvl % 2 == 0:
                    P.op('act', lambda e, lvl=lvl, lo_=lo_: e.activation(out=POW[:, lvl, :, lo_:128], in_=pPOW[:, :, lo_:128], func=AF.Copy),
                         reads=[BpPOW], writes=[BPOW[lvl]])
                else:
                    P.op('dve', lambda e, lvl=lvl, lo_=lo_: e.tensor_copy(out=POW[:, lvl, :, lo_:128], in_=pPOW[:, :, lo_:128]),
                         reads=[BpPOW], writes=[BPOW[lvl]])
                for c, h, hs in CH:
                    P.op('pe', lambda e, hs=hs, c=c, lvl=lvl: e.matmul(pWS[hs, c, 0:64], lhsT=POW[hs, lvl, c, 64:128], rhs=TTf[hs, c, :], start=True, stop=True),
                         reads=[BPOW[lvl], BTT], writes=[BpWS])
                P.op('dve', lambda e: e.tensor_tensor(out=TTf[:], in0=TTf[:], in1=pWS[:, :, 0:64], op=ALU.add),
                     reads=[BTT, BpWS], writes=[BTT])
            for c, h, hs in CH:
                P.op('pe', lambda e, hs=hs, c=c: e.matmul(pW0[hs, c, 64:128], lhsT=SM[hs, c, 1, 0:64], rhs=TM[hs, c, 3, :], start=True, stop=True),
                     reads=[BSM, BTM], writes=[BpW0])
            P.op('pool', lambda e: e.tensor_copy(out=Wf[:, :, 0:64], in_=TM[:, :, 0, :]), reads=[BTM], writes=[BWf])
            P.op('dve', lambda e: e.tensor_copy(out=Wf[:, :, 64:128], in_=pW0[:, :, 64:128]), reads=[BpW0], writes=[BWf])
            for c, h, hs in CH:
                P.op('pe', lambda e, hs=hs, c=c: e.matmul(pWS[hs, c, :], lhsT=TTf[hs, c, :], rhs=Wf[hs, c, :], start=True, stop=True),
                     reads=[BTT, BWf], writes=[BpWS])
            P.op('act', lambda e: e.activation(out=Zf[:], in_=pWS[:], func=AF.Copy), reads=[BpWS], writes=[BZf])
            for c, h, hs in CH:
                P.op('pe', lambda e, hs=hs, c=c: e.matmul(pW0[hs, c, :], lhsT=LTd[hs, c, 1, :], rhs=Zf[hs, c, :], start=True, stop=True),
                     reads=[BLTd, BZf], writes=[BpW0])
            P.op('act', lambda e: e.activation(out=Cf[:], in_=pW0[:], func=AF.Copy), reads=[BpW0], writes=[BCf])
            for c, h, hs in CH:
                P.op('pe', lambda e, hs=hs, c=c: e.matmul(pWS[hs, c, :], lhsT=TTf[hs, c, :], rhs=Cf[hs, c, :], start=True, stop=True),
                     reads=[BTT, BCf], writes=[BpWS])
            P.op('dve', lambda e: e.tensor_tensor(out=Wb[:], in0=Zf[:], in1=pWS[:], op=ALU.add),
                 reads=[BZf, BpWS], writes=[BWb])
            for c, h, hs in CH:
                if not is_ctx:
                    P.op('pe', lambda e, hs=hs, c=c: e.matmul(pPOW[hs, c, 0:64], lhsT=Wb[hs, c, 0:64], rhs=SM[hs, c, 0, 64:128], start=True, stop=True),
                         reads=[BWb, BSM], writes=[BpPOW])
                P.op('pe', lambda e, hs=hs, c=c: e.matmul(pPOW[hs, c, 64:128], lhsT=Wb[hs, c, 0:64], rhs=TM[hs, c, 1, :], start=True, stop=True),
                     reads=[BWb, BTM], writes=[BpPOW])
            if not is_ctx:
                P.op('dve', lambda e: e.tensor_tensor(out=QT[:], in0=pPOW[:, :, 0:64], in1=fm[:, :, 3, :], op=ALU.add),
                     reads=[BpPOW, Bfm], writes=[BQT])
            P.op('dve', lambda e: e.tensor_tensor(out=GT[:], in0=pPOW[:, :, 64:128], in1=istack[:], op=ALU.add),
                 reads=[BpPOW, Bc], writes=[BGT])
            corder = range(4) if d == 0 else range(3, -1, -1)
            for c in corder:
                hc = state["hcur"]
                hn = 1 - hc
                if not is_ctx:
                    for h in range(2):
                        hs = hsl[h]
                        ysl = slice(64 * c, 64 * c + 64)
                        P.op('pe', lambda e, hs=hs, ysl=ysl, c=c: e.matmul(pY[0][hs, ysl], lhsT=Wb[hs, c, 64:128], rhs=SM[hs, c, 0, 64:128], start=True, stop=False),
                             reads=[BWb, BSM], writes=[BpY[0]])
                        P.op('pe', lambda e, hs=hs, ysl=ysl, c=c: e.matmul(pY[0][hs, ysl], lhsT=TM[hs, c, 3, :], rhs=SM[hs, c, 1, 64:128], start=False, stop=False),
                             reads=[BTM, BSM], writes=[BpY[0]])
                        P.op('pe', lambda e, hs=hs, ysl=ysl, c=c, hc=hc: e.matmul(pY[0][hs, ysl], lhsT=Hb[hc][hs, :], rhs=QT[hs, c, :], start=False, stop=True),
                             reads=[BHb[hc], BQT], writes=[BpY[0]])
                for h in range(2):
                    hs = hsl[h]
                    P.op('pe', lambda e, hs=hs, c=c: e.matmul(pW0[hs, c, 0:64], lhsT=TM[hs, c, 1, :], rhs=Wb[hs, c, 64:128], start=True, stop=False),
                         reads=[BTM, BWb], writes=[BpW0])
                    P.op('pe', lambda e, hs=hs, c=c: e.matmul(pW0[hs, c, 0:64], lhsT=TM[hs, c, 2, :], rhs=TM[hs, c, 3, :], start=False, stop=False),
                         reads=[BTM], writes=[BpW0])
                    P.op('pe', lambda e, hs=hs, c=c, hc=hc: e.matmul(pW0[hs, c, 0:64], lhsT=GT[hs, c, :], rhs=Hb[hc][hs, :], start=False, stop=True),
                         reads=[BGT, BHb[hc]], writes=[BpW0])
                P.op('act', lambda e, c=c: e.activation(out=Hf[:], in_=pW0[:, c, 0:64], func=AF.Identity, scale=ptot[:, c:c + 1]),
                     reads=[BpW0, Bptot], writes=[BHf])
                P.op('dve', lambda e, hn=hn: e.tensor_copy(out=Hb[hn][:], in_=Hf[:]), reads=[BHf], writes=[BHb[hn]])
                state["hcur"] = hn

        for d in range(2):
            if d == 1 and dbgW is not None:
                for j in range(8):
                    P.dma('sp', lambda e, j=j: e.dma_start(out=dbgW[:, j * 2048:(j + 1) * 2048], in_=wkvT[:, j * 2048:(j + 1) * 2048]),
                          reads=[Bwkv])
            P.op('dve', lambda e: e.memset(Hb[state["hcur"]][:], 0.0), writes=[BHb[state["hcur"]]])
            do_tile(d, 0, True, True, True)
            order = range(64) if d == 0 else range(63, -1, -1)
            for ti in order:
                do_tile(d, TC + ti * 256, False, ti == 0, ti == 63)
        P.flush()


def phase_tail(nc, I, gin, gout, out, modbc, identb, U=None, dbgU=None, dbgG=None):
    with contextlib.ExitStack() as st:
        P = Prog(nc)

        def sb(name, shape, dt):
            return st.enter_context(nc.sbuf_tensor('S%d_' % P.uid + name, shape, dt))

        def ps(name, shape, dt):
            return st.enter_context(nc.psum_tensor('P%d_' % P.uid + name, shape, dt))
        Bg = Buf()
        if dbgU is not None:
            for j in range(8):
                P.dma('sp', lambda e, j=j: e.dma_start(out=dbgU[j * 128:(j + 1) * 128, :], in_=U[j * 128:(j + 1) * 128, :]))
                P.dma('sp', lambda e, j=j: e.dma_start(out=dbgG[j], in_=gin.ap()[j]))
        for gi in range(8):
            P.dma('pool', lambda e, gi=gi: e.collective_compute("AllGather", ALU.bypass, replica_groups=[[0, 1, 2, 3], [4, 5, 6, 7]],
                                                                 ins=[gin.ap()[gi].opt()], outs=[gout.ap()[gi].opt()]), writes=[Bg], inc=1)
        wo = sb("wo", [128, 8, D], BF16)
        w1 = sb("w1", [128, 8, 5632], BF16)
        w2 = sb("w2", [128, 22, D], BF16)
        stg = [sb(f"stg{i}", [128, 512], F32) for i in range(2)]
        Bstg = [Buf() for _ in range(2)]
        Bw = Buf()
        si = 0
        jobs = []
        wov = I["w_out"].rearrange("(kc p) n -> p kc n", p=128)
        w1v = I["ffn_w_in"].rearrange("(kc p) n -> p kc n", p=128)
        w2v = I["ffn_w_out"].rearrange("(kc p) n -> p kc n", p=128)
        for kc in range(0, 8, 2):
            pass
        for kc in range(8):
            for n0 in range(0, D, 512):
                jobs.append((wov[:, kc:kc + 1, n0:n0 + 512], wo[:, kc:kc + 1, n0:n0 + 512], 1, 512))
        for kc in range(8):
            for n0 in range(0, 5632, 512):
                jobs.append((w1v[:, kc:kc + 1, n0:n0 + 512], w1[:, kc:kc + 1, n0:n0 + 512], 1, 512))
        for kc in range(22):
            for n0 in range(0, D, 512):
                jobs.append((w2v[:, kc:kc + 1, n0:n0 + 512], w2[:, kc:kc + 1, n0:n0 + 512], 1, 512))
        for ji, (src, dst, a, b) in enumerate(jobs):
            k = ji % 2
            sv = stg[k][:, 0:a * b].rearrange("p (a b) -> p a b", b=b)
            P.dma('sp', lambda e, sv=sv, src=src: e.dma_start(out=sv, in_=src), writes=[Bstg[k]])
            eng = ['dve', 'pool', 'act'][ji % 3]
            if eng == 'act':
                P.op('act', lambda e, sv=sv, dst=dst: e.activation(out=dst, in_=sv, func=AF.Copy), reads=[Bstg[k]], writes=[Bw])
            else:
                P.op(eng, lambda e, sv=sv, dst=dst: e.tensor_copy(out=dst, in_=sv), reads=[Bstg[k]], writes=[Bw])
        reg = st.enter_context(nc.sync.register("qoffr"))
        stt = {"val": None}

        def ldreg(e):
            ins = e.reg_load(reg, I["qoff"][0:1, 0:1])
            stt["val"] = e.snap(reg)
            return ins
        P.q['sp'].append(['raw', ldreg])
        goutq = gout.ap().rearrange("(q a) f t -> q a f t", a=2)
        oq = nc.dram_tensor("oq_scr", [1024, 4096], BF16).ap()
        oqv = oq.rearrange("(fc p) t -> p fc t", p=128)
        Boq = Buf()
        for fc in range(8):
            for a in range(2):
                def cpq(e, fc=fc, a=a):
                    return e.dma_start(out=oq[fc * 128:(fc + 1) * 128, a * 2048:(a + 1) * 2048],
                                       in_=goutq[bass.ds(stt["val"], 1), a, fc * 128:(fc + 1) * 128, :].squeeze(0))
                P.dma('sp', cpq, reads=[Bg], writes=[Boq])
        oT = [sb(f"toT{i}", [128, 8, 128], BF16) for i in range(2)]
        BoT = [Buf() for _ in range(2)]
        xo = sb("xo", [128, D], F32)
        Bxo = Buf()
        pM = [ps(f"pM{i}", [128, 512], F32) for i in range(2)]
        BpM = [Buf() for _ in range(2)]
        h1 = [sb(f"h1{i}", [128, D], F32) for i in range(2)]
        Bh1 = [Buf() for _ in range(2)]
        ss = sb("tss", [128, 1], F32)
        Bss = Buf()
        u2b = sb("u2b", [128, D], BF16)
        Bu2b = Buf()
        ptr = ps("tptr", [128, 8, 128], BF16)
        Bptr = Buf()
        u2T = sb("u2T", [128, 8, 256], BF16)
        Bu2T = Buf()
        pG = [ps(f"pG{i}", [128, 2, 256], F32) for i in range(2)]
        BpG = [Buf() for _ in range(2)]
        sg = [sb(f"sg{i}", [128, 256], F32) for i in range(2)]
        Bsg = [Buf() for _ in range(2)]
        aT = sb("aT", [128, 22, 256], BF16)
        BaT = Buf()
        Bout = Buf()
        Bm = Buf()
        for tp_ in range(16):
            for a in range(2):
                ti = 2 * tp_ + a
                P.dma('sp', lambda e, a=a, ti=ti: e.dma_start(out=oT[a][:], in_=oqv[:, :, ti * 128:(ti + 1) * 128]), reads=[Boq], writes=[BoT[a]])
                P.dma('sp', lambda e, ti=ti: e.dma_start(out=xo[:], in_=I["x_own"][ti * 128:(ti + 1) * 128, :]), writes=[Bxo])
                for nh in range(2):
                    for fc in range(8):
                        P.op('pe', lambda e, a=a, nh=nh, fc=fc: e.matmul(pM[nh][:, :], lhsT=oT[a][:, fc, :], rhs=wo[:, fc, nh * 512:(nh + 1) * 512],
                                                                          start=(fc == 0), stop=(fc == 7)),
                             reads=[BoT[a], Bw], writes=[BpM[nh]])
                    P.op('dve', lambda e, a=a, nh=nh: e.tensor_tensor(out=h1[a][:, nh * 512:(nh + 1) * 512], in0=pM[nh][:, :],
                                                                       in1=modbc[:, 0, nh * 512:(nh + 1) * 512], op=ALU.mult),
                         reads=[BpM[nh], Bm], writes=[Bh1[a]])
                P.op('pool', lambda e, a=a: e.tensor_tensor(out=h1[a][:], in0=h1[a][:], in1=xo[:], op=ALU.add),
                     reads=[Bh1[a], Bxo], writes=[Bh1[a]])
                P.op('act', lambda e, a=a: e.activation(out=xo[:], in_=h1[a][:], func=AF.Square, accum_out=ss[:]),
                     reads=[Bh1[a]], writes=[Bxo, Bss])
                P.op('dve', lambda e: e.tensor_scalar(out=ss[:], in0=ss[:], scalar1=1.0 / D, scalar2=EPS, op0=ALU.mult, op1=ALU.add),
                     reads=[Bss], writes=[Bss])
                P.op('act', lambda e: e.activation(out=ss[:], in_=ss[:], func=AF.Sqrt), reads=[Bss], writes=[Bss])
                P.op('dve', lambda e: e.reciprocal(out=ss[:], in_=ss[:]), reads=[Bss], writes=[Bss])
                P.op('dve', lambda e, a=a: e.scalar_tensor_tensor(out=xo[:], in0=h1[a][:], scalar=ss[:, 0:1], in1=modbc[:, 2, :],
                                                                  op0=ALU.mult, op1=ALU.mult), reads=[Bh1[a], Bss, Bm], writes=[Bxo])
                P.op('pool', lambda e: e.tensor_tensor(out=u2b[:], in0=xo[:], in1=modbc[:, 1, :], op=ALU.add), reads=[Bxo, Bm], writes=[Bu2b])
                for kc in range(8):
                    P.op('pe', lambda e, kc=kc: e.transpose(out=ptr[:, kc, :], in_=u2b[:, kc * 128:(kc + 1) * 128], identity=identb[:]),
                         reads=[Bu2b], writes=[Bptr])
                P.op('act', lambda e, a=a: e.activation(out=u2T[:, :, a * 128:(a + 1) * 128], in_=ptr[:], func=AF.Copy), reads=[Bptr], writes=[Bu2T])
            for fc in range(22):
                g2 = fc % 2
                for part in range(2):
                    c0_ = part * 2816 + fc * 128
                    for kc in range(8):
                        P.op('pe', lambda e, g2=g2, part=part, c0_=c0_, kc=kc: e.matmul(
                            pG[g2][:, part, :], lhsT=w1[:, kc, c0_:c0_ + 128], rhs=u2T[:, kc, :], start=(kc == 0), stop=(kc == 7)),
                            reads=[Bw, Bu2T], writes=[BpG[g2]])
                P.op('act', lambda e, g2=g2: e.activation(out=sg[g2][:], in_=pG[g2][:, 0, :], func=AF.Silu), reads=[BpG[g2]], writes=[Bsg[g2]])
                P.op('dve', lambda e, g2=g2, fc=fc: e.tensor_tensor(out=aT[:, fc, :], in0=sg[g2][:], in1=pG[g2][:, 1, :], op=ALU.mult),
                     reads=[Bsg[g2], BpG[g2]], writes=[BaT])
            for a in range(2):
                ti = 2 * tp_ + a
                for nh in range(2):
                    for fc in range(22):
                        P.op('pe', lambda e, a=a, nh=nh, fc=fc: e.matmul(pM[nh][:, :], lhsT=aT[:, fc, a * 128:(a + 1) * 128], rhs=w2[:, fc, nh * 512:(nh + 1) * 512],
                                                                          start=(fc == 0), stop=(fc == 21)),
                             reads=[BaT, Bw], writes=[BpM[nh]])
                    P.op('dve', lambda e, nh=nh: e.tensor_tensor(out=xo[:, nh * 512:(nh + 1) * 512], in0=pM[nh][:, :],
                                                                 in1=modbc[:, 3, nh * 512:(nh + 1) * 512], op=ALU.mult),
                         reads=[BpM[nh], Bm], writes=[Bxo])
                P.op('pool', lambda e, a=a: e.tensor_tensor(out=xo[:], in0=xo[:], in1=h1[a][:], op=ALU.add),
                     reads=[Bxo, Bh1[a]], writes=[Bxo])
                P.dma('pool', lambda e, ti=ti: e.dma_start(out=out[ti * 128:(ti + 1) * 128, :], in_=xo[:]), reads=[Bxo], writes=[Bout])
        P.flush()


def _bias_tiles(rpb2):
    blocks = na_blocks()
    combos = {}
    for m in (2, 0, 1, 126, 127):
        for kt, slot in blocks[m]:
            combos[slot] = (m, kt)
    kr_l, kc_ = np.divmod(np.arange(128), 64)
    outb = np.full((2, 21, 128, 128), NEG, np.float32)
    for slot, (m, kt) in combos.items():
        qr = 2 * m + kr_l[None, :]
        qc = kc_[None, :]
        kr = 2 * kt + kr_l[:, None]
        kc = kc_[:, None]
        rs = np.clip(qr - 4, 0, 248)
        cs = np.clip(qc - 8, 0, 48)
        ok = (kr >= rs) & (kr < rs + 8) & (kc >= cs) & (kc < cs + 16)
        di = np.clip(kr - qr + 7, 0, 14)
        dj = np.clip(kc - qc + 15, 0, 30)
        for h in range(2):
            outb[h, slot] = np.where(ok, rpb2[h][di, dj], np.float32(NEG))
    return np.ascontiguousarray(outb.transpose(2, 0, 1, 3))


_NC_CACHE = {}


def kernel(**inp):
    f = lambda a: np.ascontiguousarray(np.asarray(a, dtype=np.float32))
    x, c, ctx, c_ctx = f(inp["x"]), f(inp["c"]), f(inp["ctx"]), f(inp["c_ctx"])
    w_in = f(inp["w_in"])[0]
    w_ada = f(inp["w_ada"])[0]
    b_ada = f(inp["b_ada"])[0]
    mu_p = f(inp["rw_mu_prev"])[0]
    mu_n = f(inp["rw_mu_next"])[0]
    ident = np.eye(128, dtype=np.float32)
    bones = np.kron(np.eye(2, dtype=np.float32), np.ones((64, 64), np.float32))
    s_i, t_i = np.meshgrid(np.arange(64), np.arange(64), indexing="ij")
    mk = np.zeros((128, 2, 2, 128), np.float32)
    mst = np.zeros((128, 2, 64), np.float32)
    for d in range(2):
        strict = (s_i < t_i) if d == 0 else (s_i > t_i)
        incl = (s_i <= t_i) if d == 0 else (s_i >= t_i)
        blk = np.concatenate([strict, incl], axis=1).astype(np.float32)
        for h in range(2):
            for j in range(2):
                mk[64 * h:64 * h + 64, d, j, :] = blk
            mst[64 * h:64 * h + 64, d, :] = strict.T.astype(np.float32)
    mkd = np.zeros((128, 2, 2, 64), np.float32)
    bd = (s_i // 32 == t_i // 32)
    for d in range(2):
        strict = (s_i < t_i) if d == 0 else (s_i > t_i)
        for h in range(2):
            mkd[64 * h:64 * h + 64, d, 0, :] = (strict & bd).astype(np.float32)
            mkd[64 * h:64 * h + 64, d, 1, :] = (strict & ~bd).astype(np.float32)
            mst[64 * h:64 * h + 64, d, :] = (strict & bd).T.astype(np.float32)
    istack = np.concatenate([np.eye(64, dtype=np.float32)] * 2, axis=0)
    reset = np.ones((128, 256), np.float32)
    reset[:, 0::64] = 0.0
    in_maps = []
    for core in range(8):
        b, g = divmod(core, 4)
        hc = slice(128 * g, 128 * g + 128)
        m = {}
        m["xcat"] = np.concatenate([ctx[b], x[b]], axis=0)
        m["x_own"] = x[b, 4096 * g:4096 * (g + 1)]
        c2 = np.stack([c[b], c_ctx], axis=0)
        m["c2T"] = c2.reshape(2, 8, 128).transpose(2, 1, 0)
        m["w_ada"] = w_ada
        m["b_adaT"] = b_ada.reshape(48, 128).T
        m["b_ada_bc"] = np.broadcast_to(b_ada[None, 2 * D:], (128, 4 * D))
        m["g1T"] = f(inp["norm1_g"])[0].reshape(8, 128).T
        m["g2_bc"] = np.broadcast_to(f(inp["norm2_g"])[0][None, :], (128, D))
        qc = np.arange(128 * g, 128 * g + 128)
        m["w_na"] = w_in[:, np.concatenate([qc, 512 + qc, 1024 + qc])]
        rwc = np.concatenate([1536 + qc, 2048 + qc, 2560 + qc, np.arange(3072, 3200), np.arange(3264, 3392), np.arange(3200, 3264)])
        m["w_rw"] = w_in[:, rwc]
        rc_ = rwc - 1536
        muT = np.zeros((128, 2, 6), np.float32)
        for j, mv in enumerate((mu_p, mu_n)):
            col = np.zeros(768, np.float32)
            col[:704] = mv[rc_]
            muT[:, j, :] = col.reshape(6, 128).T
        m["muT"] = muT
        gq = f(inp["na_q_g"])[0]
        gk = f(inp["na_k_g"])[0]
        m["gqk_bc"] = np.broadcast_to(np.stack([gq, gk], 0)[None], (128, 2, 64))
        m["bias"] = _bias_tiles(f(inp["na_rpb"])[0][2 * g:2 * g + 2])
        m["w_upT"] = f(inp["rw_w_up"])[0][:, :, hc].reshape(128, 128)
        m["a_upT"] = f(inp["rw_a_up"])[0][:, :, hc].transpose(1, 0, 2)
        m["g_up"] = f(inp["rw_g_up"])[0][:, hc]
        cvv = np.stack([f(inp["rw_w0"])[0][0, hc], f(inp["rw_w0"])[0][1, hc], f(inp["rw_a0"])[0][0, hc], f(inp["rw_a0"])[0][1, hc],
                        f(inp["rw_k_k"])[0][hc], f(inp["rw_k_a"])[0][hc], f(inp["rw_r_k"])[0].reshape(512)[hc],
                        f(inp["rw_ln_g"])[0][hc], f(inp["rw_ln_b"])[0][hc]], axis=1)
        m["cv"] = cvv
        perm = np.concatenate([np.concatenate([np.arange(128 * gg, 128 * gg + 128), 512 + np.arange(128 * gg, 128 * gg + 128)])
                               for gg in range(4)])
        m["w_out"] = f(inp["w_out"])[0][perm]
        m["ffn_w_in"] = f(inp["ffn_w_in"])[0]
        m["ffn_w_out"] = f(inp["ffn_w_out"])[0]
        m["qoff"] = np.array([[g]], np.int32)
        m["c_ident"] = ident
        m["c_bones"] = bones
        m["c_mk"] = np.broadcast_to(mk[:, :, None], (128, 2, 4, 2, 128))
        m["c_mst"] = np.broadcast_to(mst[:, :, None], (128, 2, 4, 64))
        m["c_mkd"] = np.broadcast_to(mkd[:, :, None], (128, 2, 4, 2, 64))
        m["c_istack"] = np.broadcast_to(istack[:, None], (128, 4, 64))
        m["c_reset"] = reset
        for k_, v_ in m.items():
            want = np.int32 if k_ == "qoff" else np.float32
            m[k_] = np.ascontiguousarray(v_, dtype=want)
            assert list(m[k_].shape) == IN_SPECS[k_][0], (k_, m[k_].shape)
        in_maps.append(m)
    if "nc" not in _NC_CACHE:
        _NC_CACHE["nc"] = build_nc()
    res = run_bass_kernel_spmd(_NC_CACHE["nc"], in_maps, core_ids=list(range(8)))
    outp = np.zeros((2, T, D), np.float32)
    for core in range(8):
        b, g = divmod(core, 4)
        outp[b, 4096 * g:4096 * (g + 1)] = res.results[core]["out"]
    return outp
```

```python
import contextlib
import threading
import numpy as np
import concourse.bass as bass
import concourse.mybir as mybir
from concourse.bass_utils import run_bass_kernel_spmd

F32 = mybir.dt.float32
BF16 = mybir.dt.bfloat16
I32 = mybir.dt.int32
AF = mybir.ActivationFunctionType
ALU = mybir.AluOpType
AX = mybir.AxisListType

EP = 30000
T = 16384
TC = 256
NT = T + TC
D = 1024
NEG = -30000.0
EPS = 1e-6


class Buf:
    def __init__(self, name=""):
        self.name = name
        self.w = None
        self.r = {}


class Prog:
    ENGS = ['pe', 'act', 'dve', 'pool', 'sp']
    _uid = [0]

    def __init__(self, nc, ndma=8):
        self.nc = nc
        Prog._uid[0] += 1
        self.uid = Prog._uid[0]
        self.q = {e: [] for e in self.ENGS}
        self.ops = {e: [None] for e in self.ENGS}
        self.seen = {e: {} for e in self.ENGS}
        self.ndma = ndma
        self.dma_cnt = {}
        self.dma_eng = {}
        self.dma_next = {e: 0 for e in self.ENGS}
        self.hook = None

    def _deps(self, eng, reads, writes):
        deps = {}

        def add(tok):
            if tok is None:
                return
            k, v = tok
            if deps.get(k, 0) < v:
                deps[k] = v
        for b in reads:
            add(b.w)
        for b in writes:
            add(b.w)
            for k, v in b.r.items():
                if k == eng:
                    continue
                add((k, v))
        waits = []
        for k, v in deps.items():
            if k == eng == 'pe':
                continue
            if self.seen[eng].get(k, 0) < v:
                self.seen[eng][k] = v
                waits.append((k, v))
                if not isinstance(k, tuple):
                    self.ops[k][v][4] = True
        return waits

    def op(self, eng, fn, reads=(), writes=()):
        if self.hook is not None:
            self.hook()
        waits = self._deps(eng, reads, writes)
        idx = len(self.ops[eng])
        item = ['op', waits, fn, idx, False]
        self.ops[eng].append(item)
        self.q[eng].append(item)
        for b in reads:
            b.r[eng] = idx
        for b in writes:
            b.w = (eng, idx)
            b.r = {}

    def dma(self, eng, fn, reads=(), writes=(), inc=16):
        if self.hook is not None:
            self.hook()
        slot = self.dma_next[eng]
        self.dma_next[eng] = (slot + 1) % self.ndma
        key = ('dma', eng, slot)
        prev = self.dma_cnt.get(key, 0)
        waits = self._deps(eng, reads, writes)
        if prev > 0 and self.seen[eng].get(key, 0) < prev:
            self.seen[eng][key] = prev
            waits.append((key, prev))
        val = prev + inc
        self.dma_cnt[key] = val
        self.dma_eng[key] = eng
        self.q[eng].append(['dma', waits, fn, key, inc])
        for b in reads:
            b.r[key] = val
        for b in writes:
            b.w = (key, val)
            b.r = {}
        return (key, val)

    def flush(self):
        nc = self.nc
        for e in self.ENGS:
            n = len(self.ops[e]) - 1
            if n >= 1:
                self.ops[e][n][4] = True
                self.q[e].append(['wait', [(e, n)]])
        for key, val in self.dma_cnt.items():
            self.q[self.dma_eng[key]].append(['wait', [(key, val)]])
        sig = {}
        for e in self.ENGS:
            cnt = 0
            arr = [0]
            for it in self.ops[e][1:]:
                if it[4]:
                    cnt += 1
                arr.append(cnt)
            sig[e] = arr
        with contextlib.ExitStack() as st:
            sems = {}
            for e in self.ENGS:
                nep = sig[e][-1] // EP + 1
                for k in range(nep):
                    sems[(e, k)] = st.enter_context(nc.semaphore(f"s{self.uid}_{e}_{k}"))
            for key in self.dma_cnt:
                sems[key] = st.enter_context(nc.semaphore(f"d{self.uid}_{key[1]}_{key[2]}"))
            block = st.enter_context(nc.Block())

            def emit_wait(eng, k, v):
                if isinstance(k, tuple):
                    eng.wait_ge(sems[k], v)
                else:
                    assert self.ops[k][v][4]
                    c = sig[k][v]
                    ep = (c - 1) // EP
                    eng.wait_ge(sems[(k, ep)], c - ep * EP)

            def run(ename, eng):
                for it in self.q[ename]:
                    if it[0] == 'op':
                        _, waits, fn, idx, marked = it
                        for k, v in waits:
                            emit_wait(eng, k, v)
                        ins = fn(eng)
                        if marked:
                            c = sig[ename][idx]
                            ins.then_inc(sems[(ename, (c - 1) // EP)], 1)
                    elif it[0] == 'dma':
                        _, waits, fn, key, inc = it
                        for k, v in waits:
                            emit_wait(eng, k, v)
                        fn(eng).then_inc(sems[key], inc)
                    elif it[0] == 'raw':
                        it[1](eng)
                    else:
                        for k, v in it[1]:
                            emit_wait(eng, k, v)

            @block.tensor
            def _(e):
                run('pe', e)

            @block.scalar
            def _(e):
                run('act', e)

            @block.vector
            def _(e):
                run('dve', e)

            @block.gpsimd
            def _(e):
                run('pool', e)

            @block.sync
            def _(e):
                run('sp', e)


class Interleaver:
    def __init__(self, prog, quota=(1, 1)):
        self.P = prog
        self.quota = quota

    def run(self, f0, f1):
        sems = [threading.Semaphore(0), threading.Semaphore(0)]
        alive = [True, True]
        count = [0, 0]
        local = threading.local()
        errs = []

        def hook():
            me = local.idx
            count[me] += 1
            if alive[1 - me] and count[me] % self.quota[me] == 0:
                sems[1 - me].release()
                sems[me].acquire()

        def wrap(i, f):
            local.idx = i
            sems[i].acquire()
            try:
                f()
            except BaseException as ex:
                errs.append(ex)
            finally:
                alive[i] = False
                sems[1 - i].release()
        self.P.hook = hook
        ths = [threading.Thread(target=wrap, args=(i, f)) for i, f in enumerate((f0, f1))]
        for t in ths:
            t.start()
        sems[0].release()
        for t in ths:
            t.join()
        self.P.hook = None
        if errs:
            raise errs[0]


IN_SPECS = {
    "xcat": ([NT, D], F32), "x_own": ([4096, D], F32), "c2T": ([128, 8, 2], F32),
    "w_ada": ([D, 6 * D], F32), "b_adaT": ([128, 48], F32), "b_ada_bc": ([128, 4 * D], F32),
    "g1T": ([128, 8], F32), "g2_bc": ([128, D], F32),
    "w_na": ([D, 384], F32), "w_rw": ([D, 704], F32),
    "muT": ([128, 2, 6], F32), "gqk_bc": ([128, 2, 64], F32),
    "bias": ([128, 2, 21, 128], F32),
    "w_upT": ([128, 128], F32), "a_upT": ([64, 2, 128], F32), "g_up": ([128, 128], F32),
    "cv": ([128, 9], F32),
    "w_out": ([D, D], F32), "ffn_w_in": ([D, 5632], F32), "ffn_w_out": ([2816, D], F32),
    "qoff": ([1, 1], I32),
    "c_ident": ([128, 128], F32), "c_bones": ([128, 128], F32), "c_mk": ([128, 2, 4, 2, 128], F32),
    "c_mst": ([128, 2, 4, 64], F32), "c_mkd": ([128, 2, 4, 2, 64], F32), "c_istack": ([128, 4, 64], F32), "c_reset": ([128, 256], F32),
}


def build_nc():
    nc = bass.Bass("TRN2", target_bir_lowering=False)
    I = {k: nc.dram_tensor(k, s, d, kind="ExternalInput").ap() for k, (s, d) in IN_SPECS.items()}
    out = nc.dram_tensor("out", [4096, D], F32, kind="ExternalOutput").ap()
    U = nc.dram_tensor("U_scr", [D, NT], BF16).ap()
    gin = nc.dram_tensor("gin", [8, 256, 2048], BF16)
    gout = nc.dram_tensor("gout", [8, 1024, 2048], BF16)
    Uv = U.rearrange("(kc p) t -> p kc t", p=128)
    dbgU = dbgG = dbgW = dbgF = None

    outer = contextlib.ExitStack()
    with outer:
        def sbo(name, shape, dt):
            return outer.enter_context(nc.sbuf_tensor('S0_' + name, shape, dt))
        A1 = sbo("A1", [128, 2, 8], F32)
        SH1 = sbo("SH1", [128, 2, 8], F32)
        modbc = sbo("modbc", [128, 4, D], F32)
        identb = sbo("identb", [128, 128], BF16)
        identf = sbo("identf", [128, 128], F32)
        bones = sbo("bones", [128, 128], F32)

        with contextlib.ExitStack() as st:
            P = Prog(nc)

            def sb(name, shape, dt):
                return st.enter_context(nc.sbuf_tensor('S%d_' % P.uid + name, shape, dt))

            def ps(name, shape, dt):
                return st.enter_context(nc.psum_tensor('P%d_' % P.uid + name, shape, dt))
            c2 = sb("c2", [128, 8, 2], F32)
            sT = sb("sT", [128, 8, 2], F32)
            sbc = sb("sbc", [128, 8, 128], F32)
            onesf = sb("onesf", [128, 128], F32)
            bT = sb("bT", [128, 48], F32)
            bbc = sb("bbc", [128, 4 * D], F32)
            g1 = sb("g1", [128, 8], F32)
            g2bc = sb("g2bc", [128, D], F32)
            modT = sb("modT", [128, 2, 48], F32)
            wblk = [sb(f"wblk{i}", [128, 8, D], F32) for i in range(2)]
            pmod = ps("pmod", [128, 8, 2], F32)
            pbc = [ps(f"pbc{i}", [128, 512], F32) for i in range(2)]
            B = {n: Buf(n) for n in ["c2", "sT", "sbc", "onesf", "bT", "bbc", "g1", "g2bc", "modT", "wblk0", "wblk1",
                                     "pmod", "pbc0", "pbc1", "A1", "SH1", "modbc", "ident", "bones"]}
            P.dma('sp', lambda e: e.dma_start(out=c2[:], in_=I["c2T"]), writes=[B["c2"]])
            P.dma('sp', lambda e: e.dma_start(out=bT[:], in_=I["b_adaT"]), writes=[B["bT"]])
            P.dma('sp', lambda e: e.dma_start(out=bbc[:], in_=I["b_ada_bc"]), writes=[B["bbc"]])
            P.dma('sp', lambda e: e.dma_start(out=g1[:], in_=I["g1T"]), writes=[B["g1"]])
            P.dma('sp', lambda e: e.dma_start(out=g2bc[:], in_=I["g2_bc"]), writes=[B["g2bc"]])
            P.dma('sp', lambda e: e.dma_start(out=identf[:], in_=I["c_ident"]), writes=[B["ident"]])
            P.dma('sp', lambda e: e.dma_start(out=bones[:], in_=I["c_bones"]), writes=[B["bones"]])
            P.op('dve', lambda e: e.tensor_copy(out=identb[:], in_=identf[:]), reads=[B["ident"]], writes=[B["ident"]])
            P.op('act', lambda e: e.activation(out=sT[:], in_=c2[:], func=AF.Silu), reads=[B["c2"]], writes=[B["sT"]])
            P.op('dve', lambda e: e.memset(onesf[:], 1.0), writes=[B["onesf"]])
            for kc in range(8):
                P.op('dve', lambda e, kc=kc: e.tensor_scalar(out=sbc[:, kc, :], in0=onesf[:], scalar1=sT[:, kc, 0:1],
                                                             scalar2=None, op0=ALU.mult),
                     reads=[B["onesf"], B["sT"]], writes=[B["sbc"]])
            wv = I["w_ada"].rearrange("(kc p) n -> p kc n", p=128)
            for m in range(6):
                wb = wblk[m % 2]
                Bw = B[f"wblk{m % 2}"]
                for hh in range(2):
                    P.dma('sp', lambda e, m=m, wb=wb, hh=hh: e.dma_start(out=wb[:, 4 * hh:4 * hh + 4, :],
                                                                         in_=wv[:, 4 * hh:4 * hh + 4, m * D:(m + 1) * D]),
                          writes=[Bw])
                for jj in range(8):
                    for kc in range(8):
                        P.op('pe', lambda e, wb=wb, jj=jj, kc=kc: e.matmul(pmod[:, jj, :], lhsT=wb[:, kc, jj * 128:(jj + 1) * 128],
                                                                           rhs=sT[:, kc, :], start=(kc == 0), stop=(kc == 7)),
                             reads=[Bw, B["sT"]], writes=[B["pmod"]])
                for i in range(2):
                    P.op('dve', lambda e, m=m, i=i: e.tensor_tensor(out=modT[:, i, m * 8:(m + 1) * 8], in0=pmod[:, :, i],
                                                                   in1=bT[:, m * 8:(m + 1) * 8], op=ALU.add),
                         reads=[B["pmod"], B["bT"]], writes=[B["modT"]])
                if m >= 2:
                    for nh in range(2):
                        pb = pbc[nh]
                        for kc in range(8):
                            P.op('pe', lambda e, wb=wb, pb=pb, nh=nh, kc=kc: e.matmul(
                                pb[:, :], lhsT=sbc[:, kc, :], rhs=wb[:, kc, nh * 512:(nh + 1) * 512],
                                start=(kc == 0), stop=(kc == 7)),
                                reads=[Bw, B["sbc"]], writes=[B[f"pbc{nh}"]])
                        P.op('dve', lambda e, m=m, pb=pb, nh=nh: e.tensor_tensor(
                            out=modbc[:, m - 2, nh * 512:(nh + 1) * 512], in0=pb[:, :],
                            in1=bbc[:, (m - 2) * D + nh * 512:(m - 2) * D + (nh + 1) * 512], op=ALU.add),
                            reads=[B[f"pbc{nh}"], B["bbc"]], writes=[B["modbc"]])
            P.op('dve', lambda e: e.scalar_tensor_tensor(out=modbc[:, 2, :], in0=modbc[:, 2, :], scalar=1.0, in1=g2bc[:],
                                                         op0=ALU.add, op1=ALU.mult),
                 reads=[B["modbc"], B["g2bc"]], writes=[B["modbc"]])
            for i in range(2):
                P.op('dve', lambda e, i=i: e.scalar_tensor_tensor(out=A1[:, i, :], in0=modT[:, i, 8:16], scalar=1.0, in1=g1[:],
                                                                  op0=ALU.add, op1=ALU.mult),
                     reads=[B["modT"], B["g1"]], writes=[B["A1"]])
                P.op('dve', lambda e, i=i: e.tensor_copy(out=SH1[:, i, :], in_=modT[:, i, 0:8]),
                     reads=[B["modT"]], writes=[B["SH1"]])

            NB = 3
            xt = [sb(f"xt{i}", [128, D], F32) for i in range(NB)]
            sq = sb("sqscr", [128, D], F32)
            ss = [sb(f"ss{i}", [128, 1], F32) for i in range(NB)]
            rs = [sb(f"rs{i}", [128, 1], F32) for i in range(NB)]
            xn = [sb(f"xn{i}", [128, D], BF16) for i in range(NB)]
            uT = [sb(f"uT{i}", [128, 8, 128], BF16) for i in range(NB)]
            ptr = [ps(f"ptr{i}", [128, 8, 128], BF16) for i in range(2)]
            Bx = [Buf() for _ in range(NB)]
            Bss = [Buf() for _ in range(NB)]
            Bxn = [Buf() for _ in range(NB)]
            BuT = [Buf() for _ in range(NB)]
            Bpt = [Buf() for _ in range(2)]
            Bsq = Buf()
            BU = Buf()
            for ti in range(NT // 128):
                k = ti % NB
                i = 1 if ti < 2 else 0
                P.dma('sp', lambda e, ti=ti, k=k: e.dma_start(out=xt[k][:], in_=I["xcat"][ti * 128:(ti + 1) * 128, :]),
                      writes=[Bx[k]])
                P.op('act', lambda e, k=k: e.activation(out=sq[:], in_=xt[k][:], func=AF.Square, accum_out=ss[k][:]),
                     reads=[Bx[k]], writes=[Bsq, Bss[k]])
                P.op('dve', lambda e, k=k: e.tensor_scalar(out=rs[k][:], in0=ss[k][:], scalar1=1.0 / D, scalar2=EPS,
                                                          op0=ALU.mult, op1=ALU.add), reads=[Bss[k]], writes=[Bss[k]])
                P.op('act', lambda e, k=k: e.activation(out=rs[k][:], in_=rs[k][:], func=AF.Sqrt), reads=[Bss[k]], writes=[Bss[k]])
                P.op('dve', lambda e, k=k: e.reciprocal(out=rs[k][:], in_=rs[k][:]), reads=[Bss[k]], writes=[Bss[k]])
                P.op('dve', lambda e, k=k: e.tensor_scalar(out=xn[k][:], in0=xt[k][:], scalar1=rs[k][:, 0:1], scalar2=None,
                                                          op0=ALU.mult), reads=[Bx[k], Bss[k]], writes=[Bxn[k]])
                pk = ti % 2
                for kc in range(8):
                    P.op('pe', lambda e, k=k, pk=pk, kc=kc: e.transpose(out=ptr[pk][:, kc, :], in_=xn[k][:, kc * 128:(kc + 1) * 128],
                                                                        identity=identb[:]),
                         reads=[Bxn[k], B["ident"]], writes=[Bpt[pk]])
                for kc in range(8):
                    eng = 'act' if kc % 2 == 0 else 'dve'
                    if eng == 'act':
                        P.op('act', lambda e, k=k, pk=pk, kc=kc, i=i: e.activation(
                            out=uT[k][:, kc, :], in_=ptr[pk][:, kc, :], func=AF.Identity,
                            bias=SH1[:, i, kc:kc + 1], scale=A1[:, i, kc:kc + 1]),
                            reads=[Bpt[pk], B["A1"], B["SH1"]], writes=[BuT[k]])
                    else:
                        P.op('dve', lambda e, k=k, pk=pk, kc=kc, i=i: e.tensor_scalar(
                            out=uT[k][:, kc, :], in0=ptr[pk][:, kc, :], scalar1=A1[:, i, kc:kc + 1],
                            scalar2=SH1[:, i, kc:kc + 1], op0=ALU.mult, op1=ALU.add),
                            reads=[Bpt[pk], B["A1"], B["SH1"]], writes=[BuT[k]])
                P.dma('pool', lambda e, ti=ti, k=k: e.dma_start(out=Uv[:, :, ti * 128:(ti + 1) * 128], in_=uT[k][:]),
                      reads=[BuT[k]], writes=[BU])
            P.flush()
        nc.all_engine_barrier()
        phase_na(nc, I, Uv, gin, identb)
        nc.all_engine_barrier()
        phase_rw(nc, I, Uv, gin, identb, bones, dbgW, dbgF)
        nc.all_engine_barrier()
        phase_tail(nc, I, gin, gout, out, modbc, identb, U, dbgU, dbgG)
    return nc


def na_blocks():
    res = []
    for m in range(128):
        if m == 0:
            res.append([(kt, 5 + kt) for kt in range(4)])
        elif m == 1:
            res.append([(kt, 9 + kt) for kt in range(4)])
        elif m == 126:
            res.append([(124 + j, 13 + j) for j in range(4)])
        elif m == 127:
            res.append([(124 + j, 17 + j) for j in range(4)])
        else:
            res.append([(m + dl, dl + 2) for dl in range(-2, 3)])
    return res


def phase_na(nc, I, Uv, gin, identb):
    with contextlib.ExitStack() as st:
        P = Prog(nc)

        def sb(name, shape, dt):
            return st.enter_context(nc.sbuf_tensor('S%d_' % P.uid + name, shape, dt))

        def ps(name, shape, dt):
            return st.enter_context(nc.psum_tensor('P%d_' % P.uid + name, shape, dt))
        NTI = NT // 128
        qT = sb("qT", [128, NT], BF16)
        kT = sb("kT", [128, NT], BF16)
        vS = sb("vS", [128, NTI, 2, 65], BF16)
        biasS = sb("biasS", [128, 2, 21, 128], F32)
        wst = sb("wst", [128, 8, 384], F32)
        wb = sb("wnab", [128, 8, 384], BF16)
        gqk = sb("gqk", [128, 2, 64], F32)
        Bq, Bk, Bv, Bbias, Bw, Bg = Buf(), Buf(), Buf(), Buf(), Buf(), Buf()
        Bid = Buf()
        P.dma('sp', lambda e: e.dma_start(out=wst[:], in_=I["w_na"].rearrange("(kc p) n -> p kc n", p=128)), writes=[Bw])
        P.op('dve', lambda e: e.tensor_copy(out=wb[:], in_=wst[:]), reads=[Bw], writes=[Bw])
        P.dma('sp', lambda e: e.dma_start(out=biasS[:], in_=I["bias"]), writes=[Bbias])
        P.dma('sp', lambda e: e.dma_start(out=gqk[:], in_=I["gqk_bc"]), writes=[Bg])
        P.op('dve', lambda e: e.memset(vS[:], 1.0), writes=[Bv])
        NB = 3
        uT = [sb(f"nuT{i}", [128, 8, 128], BF16) for i in range(NB)]
        BuT = [Buf() for _ in range(NB)]
        pp = [ps(f"npp{i}", [128, 512], F32) for i in range(2)]
        nbf = ps("nbf", [128, 1024], BF16)
        Bpp = [Buf() for _ in range(2)]
        sq = sb("nsq", [128, 256], F32)
        ssq = sb("nssq", [128, 4], F32)
        qkn = [sb(f"qkn{i}", [128, 256], BF16) for i in range(2)]
        Bsq, Bssq = Buf(), Buf()
        Bqkn = [Buf() for _ in range(2)]
        ptq = [nbf[:, 256 * i:256 * i + 256].rearrange("p (a b) -> p a b", b=128) for i in range(2)]
        Bnbf = Buf()
        Bptq = [Bnbf, Bnbf]
        for ti in range(NTI):
            k = ti % NB
            k2 = ti % 2
            P.dma('sp', lambda e, ti=ti, k=k: e.dma_start(out=uT[k][:], in_=Uv[:, :, ti * 128:(ti + 1) * 128]), writes=[BuT[k]])
            for kc in range(8):
                P.op('pe', lambda e, k=k, k2=k2, kc=kc: e.matmul(pp[k2][:, 0:384], lhsT=uT[k][:, kc, :], rhs=wb[:, kc, :],
                                                                start=(kc == 0), stop=(kc == 7)),
                     reads=[BuT[k], Bw], writes=[Bpp[k2]])
            P.op('act', lambda e, k2=k2: e.activation(out=sq[:], in_=pp[k2][:, 0:256], func=AF.Square), reads=[Bpp[k2]], writes=[Bsq])
            P.op('dve', lambda e: e.tensor_reduce(out=ssq[:], in_=sq[:].rearrange("p (a b) -> p a b", b=64), axis=AX.X, op=ALU.add),
                 reads=[Bsq], writes=[Bssq])
            P.op('dve', lambda e: e.tensor_scalar(out=ssq[:, 0:2], in0=ssq[:, 0:2], scalar1=64 * EPS, scalar2=None,
                                                  op0=ALU.add), reads=[Bssq], writes=[Bssq])
            P.op('dve', lambda e: e.tensor_scalar(out=ssq[:, 2:4], in0=ssq[:, 2:4], scalar1=1.0 / 64, scalar2=EPS,
                                                  op0=ALU.mult, op1=ALU.add), reads=[Bssq], writes=[Bssq])
            P.op('act', lambda e: e.activation(out=ssq[:], in_=ssq[:], func=AF.Sqrt), reads=[Bssq], writes=[Bssq])
            P.op('dve', lambda e: e.reciprocal(out=ssq[:], in_=ssq[:]), reads=[Bssq], writes=[Bssq])
            for j in range(4):
                P.op('dve', lambda e, k2=k2, j=j: e.scalar_tensor_tensor(
                    out=qkn[k2][:, j * 64:(j + 1) * 64], in0=pp[k2][:, j * 64:(j + 1) * 64], scalar=ssq[:, j:j + 1],
                    in1=gqk[:, j // 2, :], op0=ALU.mult, op1=ALU.mult),
                    reads=[Bpp[k2], Bssq, Bg], writes=[Bqkn[k2]])
            P.op('act', lambda e, k2=k2, ti=ti: e.activation(out=vS[:, ti, :, 0:64],
                                                             in_=pp[k2][:, 256:384].rearrange("p (h d) -> p h d", d=64),
                                                             func=AF.Copy), reads=[Bpp[k2]], writes=[Bv])
            for j in range(2):
                P.op('pe', lambda e, k2=k2, j=j: e.transpose(out=ptq[k2][:, j, :], in_=qkn[k2][:, j * 128:(j + 1) * 128],
                                                             identity=identb[:]), reads=[Bqkn[k2], Bid], writes=[Bptq[k2]])
            P.op('act', lambda e, k2=k2, ti=ti: e.activation(out=qT[:, ti * 128:(ti + 1) * 128], in_=ptq[k2][:, 0, :], func=AF.Copy),
                 reads=[Bptq[k2]], writes=[Bq])
            P.op('dve', lambda e, k2=k2, ti=ti: e.tensor_copy(out=kT[:, ti * 128:(ti + 1) * 128], in_=ptq[k2][:, 1, :]),
                 reads=[Bptq[k2]], writes=[Bk])
        pS = [ps(f"pS{i}", [128, 8, 128], F32) for i in range(2)]
        BpS = [Buf() for _ in range(2)]
        sS = [sb(f"sS{i}", [128, 5, 128], F32) for i in range(2)]
        BsS = [Buf() for _ in range(2)]
        pT = [sb(f"pT{i}", [128, 7, 128], BF16) for i in range(2)]
        BpT = [Buf() for _ in range(2)]
        pOb = ps("pOb", [128, 512], F32)
        pO = [pOb[:, 256 * i:256 * i + 130].rearrange("p (a b) -> p a b", b=65) for i in range(2)]
        _b = Buf()
        BpO = [_b, _b]
        rc = [sb(f"rc{i}", [128, 2], F32) for i in range(2)]
        Brc = [Buf() for _ in range(2)]
        oS = [sb(f"oS{i}", [128, 128], BF16) for i in range(2)]
        BoS = [Buf() for _ in range(2)]
        pOT = [nbf[:, 512:640]]
        BpOT = [Bnbf]
        oT = [sb(f"oT{i}", [128, 128], BF16) for i in range(2)]
        BoT = [Buf() for _ in range(2)]
        Bgin = Buf()
        blocks = na_blocks()
        it = 0
        for m in range(128):
            kl = blocks[m]
            nk = len(kl)
            qc0 = (m + 2) * 128
            mb = m % 2
            for h in range(2):
                x2 = it % 2
                it += 1
                hs = slice(64 * h, 64 * h + 64)
                tiles = [kt + 2 for kt, _ in kl] + [0, 1]
                for j, tt in enumerate(tiles):
                    P.op('pe', lambda e, x2=x2, j=j, tt=tt, hs=hs, qc0=qc0: e.matmul(
                        pS[x2][:, j, :], lhsT=kT[hs, tt * 128:(tt + 1) * 128], rhs=qT[hs, qc0:qc0 + 128], start=True, stop=True),
                        reads=[Bq, Bk], writes=[BpS[x2]])
                s0 = kl[0][1]
                P.op('dve', lambda e, x2=x2, nk=nk, h=h, s0=s0: e.tensor_tensor(
                    out=sS[x2][:, 0:nk, :], in0=pS[x2][:, 0:nk, :], in1=biasS[:, h, s0:s0 + nk, :], op=ALU.add),
                    reads=[BpS[x2], Bbias], writes=[BsS[x2]])
                P.op('act', lambda e, x2=x2, nk=nk: e.activation(out=pT[x2][:, 0:nk, :], in_=sS[x2][:, 0:nk, :], func=AF.Exp),
                     reads=[BsS[x2]], writes=[BpT[x2]])
                P.op('act', lambda e, x2=x2, nk=nk: e.activation(out=pT[x2][:, nk:nk + 2, :], in_=pS[x2][:, nk:nk + 2, :], func=AF.Exp),
                     reads=[BpS[x2]], writes=[BpT[x2]])
                for j, tt in enumerate(tiles):
                    P.op('pe', lambda e, x2=x2, j=j, tt=tt, h=h, mb=mb, n=len(tiles): e.matmul(
                        pO[mb][:, h, :], lhsT=pT[x2][:, j, :], rhs=vS[:, tt, h, :], start=(j == 0), stop=(j == n - 1)),
                        reads=[BpT[x2], Bv], writes=[BpO[mb]])
            P.op('dve', lambda e, mb=mb: e.reciprocal(out=rc[mb][:], in_=pO[mb][:, :, 64]), reads=[BpO[mb]], writes=[Brc[mb]])
            for h in range(2):
                P.op('dve', lambda e, mb=mb, h=h: e.tensor_scalar(out=oS[mb][:, h * 64:(h + 1) * 64], in0=pO[mb][:, h, 0:64],
                                                                   scalar1=rc[mb][:, h:h + 1], scalar2=None, op0=ALU.mult),
                     reads=[BpO[mb], Brc[mb]], writes=[BoS[mb]])
            P.op('pe', lambda e, mb=mb: e.transpose(out=pOT[0][:, :], in_=oS[mb][:, :], identity=identb[:]),
                 reads=[BoS[mb]], writes=[BpOT[0]])
            P.op('act', lambda e, mb=mb: e.activation(out=oT[mb][:], in_=pOT[0][:, :], func=AF.Copy),
                 reads=[BpOT[0]], writes=[BoT[mb]])
            P.dma('pool', lambda e, mb=mb, m=m: e.dma_start(out=gin.ap()[m // 16, 0:128, (m % 16) * 128:(m % 16 + 1) * 128], in_=oT[mb][:]),
                  reads=[BoT[mb]], writes=[Bgin])
        P.flush()


def phase_rw(nc, I, Uv, gin, identb, bones, dbgW=None, dbgF=None):
    with contextlib.ExitStack() as st:
        P = Prog(nc)

        def sb(name, shape, dt):
            return st.enter_context(nc.sbuf_tensor('S%d_' % P.uid + name, shape, dt))

        def ps(name, shape, dt):
            return st.enter_context(nc.psum_tensor('P%d_' % P.uid + name, shape, dt))
        wb = sb("rwb", [128, 8, 704], BF16)
        mu = sb("mu", [128, 2, 6], F32)
        c0 = sb("c0", [128, 6], F32)
        cv = sb("cv", [128, 9], F32)
        omka = sb("omka", [128, 1], F32)
        omka2 = sb("omka2", [128, 1], F32)
        wupf = sb("wupf", [128, 128], F32)
        wupb = sb("wupb", [128, 128], BF16)
        aupf = sb("aupf", [64, 2, 128], F32)
        aupb = sb("aupb", [64, 2, 128], BF16)
        gupf = sb("gupf", [128, 128], F32)
        gupb = sb("gupb", [128, 128], BF16)
        mk = sb("mk", [128, 2, 4, 2, 128], F32)
        mst = sb("mst", [128, 2, 4, 64], F32)
        mkd = sb("mkd", [128, 2, 4, 2, 64], F32)
        istack = sb("istack", [128, 4, 64], F32)
        reset = sb("reset", [128, 256], F32)
        bavg = sb("bavg", [128, 128], F32)
        wkvT = sb("wkvT", [128, T], F32)
        wst = wkvT[:, 0:8 * 704].rearrange("p (a b) -> p a b", b=704)
        Bc = Buf("consts")
        Bwkv = Buf("wkv")
        for dst, src in [(None, I["w_rw"].rearrange("(kc p) n -> p kc n", p=128)), (mu, I["muT"]), (cv, I["cv"]), (wupf, I["w_upT"]),
                         (aupf, I["a_upT"]), (gupf, I["g_up"]), (mk, I["c_mk"]), (mst, I["c_mst"]), (mkd, I["c_mkd"]), (istack, I["c_istack"]),
                         (reset, I["c_reset"])]:
            if dst is None:
                P.dma('sp', lambda e, src=src: e.dma_start(out=wst, in_=src), writes=[Bc, Bwkv])
            else:
                P.dma('sp', lambda e, dst=dst, src=src: e.dma_start(out=dst[:], in_=src), writes=[Bc])
        P.op('dve', lambda e: e.tensor_copy(out=wb[:], in_=wst), reads=[Bc, Bwkv], writes=[Bc])
        P.op('dve', lambda e: e.tensor_copy(out=wupb[:], in_=wupf[:]), reads=[Bc], writes=[Bc])
        P.op('dve', lambda e: e.tensor_copy(out=aupb[:], in_=aupf[:]), reads=[Bc], writes=[Bc])
        P.op('dve', lambda e: e.tensor_copy(out=gupb[:], in_=gupf[:]), reads=[Bc], writes=[Bc])
        P.op('dve', lambda e: e.tensor_tensor(out=c0[:], in0=mu[:, 0, :], in1=mu[:, 1, :], op=ALU.add), reads=[Bc], writes=[Bc])
        P.op('dve', lambda e: e.tensor_scalar(out=c0[:], in0=c0[:], scalar1=-1.0, scalar2=1.0, op0=ALU.mult, op1=ALU.add),
             reads=[Bc], writes=[Bc])
        P.op('dve', lambda e: e.tensor_scalar(out=omka[:], in0=cv[:, 5:6], scalar1=-1.0, scalar2=1.0, op0=ALU.mult, op1=ALU.add),
             reads=[Bc], writes=[Bc])
        P.op('dve', lambda e: e.tensor_scalar(out=omka2[:], in0=omka[:], scalar1=2.0, scalar2=None, op0=ALU.mult),
             reads=[Bc], writes=[Bc])
        P.op('dve', lambda e: e.tensor_scalar(out=bavg[:], in0=bones[:], scalar1=1.0 / 64, scalar2=None, op0=ALU.mult),
             reads=[Bc], writes=[Bc])
        NBU = 2
        uT = [sb(f"ruT{i}", [128, 8, 258], BF16) for i in range(NBU)]
        BuT = [Buf() for _ in range(NBU)]
        _ppj = ps("rpp", [128, 512], F32)
        ppj = [_ppj, _ppj]
        _bj = Buf()
        Bppj = [_bj, _bj]
        pmisc = ps("rpmisc", [128, 512], F32)
        pax = [pmisc[:, 256:512], pmisc[:, 256:512]]
        _bp = Buf()
        Bpax = [_bp, _bp]
        FT = {}

        def feat(name, dt=F32, n=2, w=256):
            FT[name] = ([sb(f"f_{name}{i}", [128, w], dt) for i in range(n)], [Buf() for _ in range(n)])
        for nm in ["t1", "r", "k", "v", "lw", "rate", "kk", "kdir", "bb", "cum", "epos", "eneg", "eprev", "tmp", "tmp2"]:
            feat(nm)
        feat("vb", BF16)
        feat("wd", BF16)
        feat("gd", BF16)
        feat("ad", BF16)
        feat("ptot", F32, 2, 4)
        FM = [sb(f"FM{i}", [128, 4, 4, 64], BF16) for i in range(2)]
        BFM = [Buf() for _ in range(2)]
        pTM = ps("pTM", [128, 4, 4, 64], BF16)
        pSM = ps("pSM", [128, 4, 2, 128], F32)
        pPOW = ps("pPOW", [128, 4, 128], F32)
        pW0 = ps("pW0", [128, 4, 128], F32)
        pWS = ps("pWS", [128, 4, 128], F32)
        pY = [pmisc[:, 0:256]]
        BpTM, BpSM, BpPOW, BpW0, BpWS = Buf(), Buf(), Buf(), Buf(), Buf()
        BpY = [_bp]
        TM = sb("TM", [128, 4, 4, 64], BF16)
        SM = sb("SM", [128, 4, 2, 128], BF16)
        L1 = sb("L1", [128, 4, 64], F32)
        LTd = sb("LTd", [128, 4, 2, 64], F32)
        POW = sb("POW", [128, 4, 4, 128], F32)
        Wf = sb("Wf", [128, 4, 128], F32)
        Cf = sb("Cf", [128, 4, 128], F32)
        Zf = sb("Zf", [128, 4, 128], F32)
        TTf = sb("TTf", [128, 4, 64], F32)
        BZf, BTT = Buf(), Buf()
        Wb = sb("Wb", [128, 4, 128], BF16)
        QT = sb("QT", [128, 4, 64], BF16)
        GT = sb("GT", [128, 4, 64], BF16)
        BTM, BSM, BL1, BLTd, BWf, BCf, BWb, BQT, BGT = [Buf() for _ in range(9)]
        BPOW = [Buf() for _ in range(4)]
        Hf = sb("Hf", [128, 64], F32)
        Hb = [sb(f"Hb{i}", [128, 64], BF16) for i in range(2)]
        BHf = Buf()
        BHb = [Buf() for _ in range(2)]
        yS = sb("yS", [128, 256], F32)
        cen = sb("cen", [128, 256], F32)
        sq2 = sb("sq2", [128, 256], F32)
        rstd = sb("rstd", [128, 256], F32)
        bon = sb("bon", [128, 256], F32)
        gS = sb("gS", [128, 256], F32)
        oR = [sb(f"oR{i}", [128, 256], BF16) for i in range(2)]
        ByS, Bcen, Bsq2, Brstd, Bbon, BgS = Buf(), Buf(), Buf(), Buf(), Buf(), Buf()
        BoR = [Buf() for _ in range(2)]
        Bgin = Buf()
        hsl = [slice(0, 64), slice(64, 128)]
        state = {"hcur": 0, "tile_it": 0, "lane_it": 0}

        def F(name, k):
            a, b = FT[name]
            return a[k], b[k]

        def proj(cc, k, ku, ncol=128):
            pk = state.setdefault("pk", 0)
            state["pk"] = 1 - pk
            for kc in range(8):
                P.op('pe', lambda e, pk=pk, kc=kc, cc=cc, ku=ku, ncol=ncol: e.matmul(
                    ppj[pk][0:ncol, 0:258], lhsT=wb[:, kc, cc * 128:cc * 128 + ncol], rhs=uT[ku][:, kc, :],
                    start=(kc == 0), stop=(kc == 7)), reads=[BuT[ku], Bc], writes=[Bppj[pk]])
            return pk

        def shift(cc, pk, k, dst, Bdst, ncol=128):
            t1, Bt1 = F("t1", k)
            P.op('act', lambda e: e.activation(out=t1[0:ncol, :], in_=ppj[pk][0:ncol, 1:257], func=AF.Identity, scale=c0[0:ncol, cc:cc + 1]),
                 reads=[Bppj[pk], Bc], writes=[Bt1])
            P.op('dve', lambda e: e.scalar_tensor_tensor(out=t1[0:ncol, :], in0=ppj[pk][0:ncol, 0:256], scalar=mu[0:ncol, 0, cc:cc + 1],
                                                         in1=t1[0:ncol, :], op0=ALU.mult, op1=ALU.add),
                 reads=[Bppj[pk], Bt1, Bc], writes=[Bt1])
            P.op('dve', lambda e: e.scalar_tensor_tensor(out=dst[0:ncol, :], in0=ppj[pk][0:ncol, 2:258], scalar=mu[0:ncol, 1, cc:cc + 1],
                                                         in1=t1[0:ncol, :], op0=ALU.mult, op1=ALU.add),
                 reads=[Bppj[pk], Bt1, Bc], writes=[Bdst])

        def do_tile(d, tok0, is_ctx, first, last):
            k = state["tile_it"] % 2
            state["tile_it"] += 1
            ku = k
            lo = 0 if first else -1
            hi = 256 if last else 257
            if first:
                P.op('pool', lambda e: e.memset(uT[ku][:, :, 0:1], 0.0), writes=[BuT[ku]])
            if last:
                P.op('pool', lambda e: e.memset(uT[ku][:, :, 257:258], 0.0), writes=[BuT[ku]])
            P.dma('sp', lambda e: e.dma_start(out=uT[ku][:, :, 1 + lo:1 + hi], in_=Uv[:, :, tok0 + lo:tok0 + hi]), writes=[BuT[ku]])
            r, Br = F("r", k)
            kf, Bk = F("k", k)
            v, Bv = F("v", k)
            vb, Bvb = F("vb", k)
            wd, Bwd = F("wd", k)
            gd, Bgd = F("gd", k)
            ad, Bad = F("ad", k)
            tmp, Btmp = F("tmp", k)
            tmp2, Btmp2 = F("tmp2", k)
            lw, Blw = F("lw", k)
            rate, Brate = F("rate", k)
            kk, Bkk = F("kk", k)
            kdir, Bkdir = F("kdir", k)
            bb, Bbb = F("bb", k)
            cum, Bcum = F("cum", k)
            epos, Bepos = F("epos", k)
            eneg, Beneg = F("eneg", k)
            eprev, Beprev = F("eprev", k)
            ptot, Bptot = F("ptot", k)
            for cc, dst, Bd in [(0, r, Br), (1, kf, Bk), (2, v, Bv), (3, tmp, Btmp)]:
                pk = proj(cc, k, ku)
                shift(cc, pk, k, dst, Bd)
                if cc == 3:
                    P.op('act', lambda e: e.activation(out=wd[:], in_=tmp[:], func=AF.Tanh), reads=[Btmp], writes=[Bwd])
            pk = proj(5, k, ku, 64)
            shift(5, pk, k, tmp2, Btmp2, 64)
            P.op('act', lambda e: e.activation(out=ad[0:64, :], in_=tmp2[0:64, :], func=AF.Copy), reads=[Btmp2], writes=[Bad])
            P.op('pool', lambda e: e.tensor_copy(out=vb[:], in_=v[:]), reads=[Bv], writes=[Bvb])
            ds_ = slice(64 * d, 64 * d + 64)
            P.op('pe', lambda e: e.matmul(pax[0][:, :], lhsT=wupb[ds_, :], rhs=wd[ds_, :], start=True, stop=True),
                 reads=[Bwd, Bc], writes=[Bpax[0]])
            P.op('act', lambda e: e.activation(out=lw[:], in_=pax[0][:, :], func=AF.Sigmoid, bias=cv[:, d:d + 1]),
                 reads=[Bpax[0], Bc], writes=[Blw])
            P.op('pool', lambda e: e.tensor_scalar(out=lw[:], in0=lw[:], scalar1=-0.6065306597126334, scalar2=None, op0=ALU.mult),
                 reads=[Blw], writes=[Blw])
            P.op('pe', lambda e: e.matmul(pax[1][:, :], lhsT=aupb[:, d, :], rhs=ad[0:64, :], start=True, stop=True),
                 reads=[Bad, Bc], writes=[Bpax[1]])
            P.op('act', lambda e: e.activation(out=rate[:], in_=pax[1][:, :], func=AF.Sigmoid, bias=cv[:, 2 + d:3 + d]),
                 reads=[Bpax[1], Bc], writes=[Brate])
            P.op('dve', lambda e: e.tensor_scalar(out=kk[:], in0=kf[:], scalar1=cv[:, 4:5], scalar2=None, op0=ALU.mult),
                 reads=[Bk, Bc], writes=[Bkk])
            P.op('pool', lambda e: e.tensor_tensor(out=tmp[:], in0=kk[:], in1=kk[:], op=ALU.mult), reads=[Bkk], writes=[Btmp])
            P.op('pe', lambda e: e.matmul(pax[0][:, :], lhsT=bones[:], rhs=tmp[:], start=True, stop=True),
                 reads=[Btmp, Bc], writes=[Bpax[0]])
            P.op('dve', lambda e: e.tensor_scalar(out=tmp2[:], in0=pax[0][:, :], scalar1=1e-24, scalar2=None, op0=ALU.max),
                 reads=[Bpax[0]], writes=[Btmp2])
            P.op('act', lambda e: e.activation(out=tmp2[:], in_=tmp2[:], func=AF.Sqrt), reads=[Btmp2], writes=[Btmp2])
            P.op('dve', lambda e: e.reciprocal(out=tmp2[:], in_=tmp2[:]), reads=[Btmp2], writes=[Btmp2])
            P.op('dve', lambda e: e.tensor_tensor(out=kk[:], in0=kk[:], in1=tmp2[:], op=ALU.mult), reads=[Bkk, Btmp2], writes=[Bkk])
            P.op('dve', lambda e: e.tensor_scalar(out=kdir[:], in0=rate[:], scalar1=cv[:, 5:6], scalar2=omka[:, 0:1],
                                                  op0=ALU.mult, op1=ALU.add), reads=[Brate, Bc], writes=[Bkdir])
            P.op('pool', lambda e: e.tensor_tensor(out=kdir[:], in0=kdir[:], in1=kf[:], op=ALU.mult), reads=[Bkdir, Bk], writes=[Bkdir])
            P.op('pool', lambda e: e.tensor_tensor(out=bb[:], in0=kk[:], in1=rate[:], op=ALU.mult), reads=[Bkk, Brate], writes=[Bbb])
            P.op('dve', lambda e: e.tensor_tensor_scan(out=cum[:], data0=reset[:], data1=lw[:], initial=0.0, op0=ALU.mult, op1=ALU.add),
                 reads=[Blw, Bc], writes=[Bcum])
            if d == 1:
                for c in range(4):
                    cs = slice(64 * c, 64 * c + 64)
                    P.op('dve', lambda e, c=c, cs=cs: e.tensor_scalar(out=tmp[:, cs], in0=cum[:, cs], scalar1=-1.0,
                                                                      scalar2=cum[:, 64 * c + 63:64 * c + 64],
                                                                      op0=ALU.mult, op1=ALU.add), reads=[Bcum], writes=[Btmp])
                P.op('dve', lambda e: e.tensor_tensor(out=cum[:], in0=tmp[:], in1=lw[:], op=ALU.add), reads=[Btmp, Blw], writes=[Bcum])
            P.op('act', lambda e: e.activation(out=epos[:], in_=cum[:], func=AF.Exp), reads=[Bcum], writes=[Bepos])
            P.op('act', lambda e: e.activation(out=eneg[:], in_=cum[:], func=AF.Exp, scale=-1.0), reads=[Bcum], writes=[Beneg])
            P.op('pool', lambda e: e.tensor_tensor(out=tmp2[:], in0=cum[:], in1=lw[:], op=ALU.subtract), reads=[Bcum, Blw], writes=[Btmp2])
            P.op('act', lambda e: e.activation(out=eprev[:], in_=tmp2[:], func=AF.Exp), reads=[Btmp2], writes=[Beprev])
            last_col = 63 if d == 0 else 0
            P.op('pool', lambda e: e.tensor_copy(out=ptot[:, 0:4], in_=epos[:].rearrange("p (c t) -> p c t", t=64)[:, :, last_col]),
                 reads=[Bepos], writes=[Bptot])
            fm = FM[k]
            v3 = lambda a: a[:].rearrange("p (c t) -> p c t", t=64)
            P.op('dve', lambda e: e.tensor_tensor(out=fm[:, :, 0, :], in0=v3(bb), in1=v3(eneg), op=ALU.mult),
                 reads=[Bbb, Beneg], writes=[BFM[k]])
            P.op('pool', lambda e: e.tensor_tensor(out=fm[:, :, 1, :], in0=v3(kdir), in1=v3(eneg), op=ALU.mult),
                 reads=[Bkdir, Beneg], writes=[BFM[k]])
            P.op('dve', lambda e: e.scalar_tensor_tensor(out=fm[:, :, 2, :], in0=v3(kk), scalar=-1.0, in1=v3(eprev),
                                                         op0=ALU.mult, op1=ALU.mult), reads=[Bkk, Beprev], writes=[BFM[k]])
            P.op('pool', lambda e: e.tensor_tensor(out=fm[:, :, 3, :], in0=v3(r), in1=v3(epos), op=ALU.mult),
                 reads=[Br, Bepos], writes=[BFM[k]])
            if dbgF is not None and d == 0 and tok0 == TC:
                for j, (a_, b_) in enumerate([(r, Br), (kf, Bk), (v, Bv), (lw, Blw), (rate, Brate), (kk, Bkk), (kdir, Bkdir), (bb, Bbb),
                                              (cum, Bcum), (epos, Bepos), (eneg, Beneg), (eprev, Beprev)]):
                    P.dma('sp', lambda e, j=j, a_=a_: e.dma_start(out=dbgF[j], in_=a_[:]), reads=[b_])
            def part2():
                do_chunks(d, k, is_ctx, vb, Bvb, ptot, Bptot)
                if is_ctx:
                    return
                lt0 = tok0 - TC
                if d == 0:
                    P.op('act', lambda e: e.activation(out=wkvT[:, lt0:lt0 + 256].rearrange("p (c t) -> p c t", t=64), in_=pW0[:, :, 64:128], func=AF.Copy),
                         reads=[BpW0], writes=[Bwkv])
                    return
                P.op('dve', lambda e: e.tensor_tensor(out=yS[:].rearrange("p (c t) -> p c t", t=64), in0=pW0[:, :, 64:128],
                                                          in1=wkvT[:, lt0:lt0 + 256].rearrange("p (c t) -> p c t", t=64), op=ALU.add),
                     reads=[BpW0, Bwkv], writes=[ByS])
                P.op('pe', lambda e: e.matmul(pax[0][:, :], lhsT=bavg[:], rhs=yS[:], start=True, stop=True),
                     reads=[ByS, Bc], writes=[Bpax[0]])
                P.op('dve', lambda e: e.tensor_tensor(out=cen[:], in0=yS[:], in1=pax[0][:, :], op=ALU.subtract),
                     reads=[ByS, Bpax[0]], writes=[Bcen])
                P.op('pool', lambda e: e.tensor_tensor(out=sq2[:], in0=cen[:], in1=cen[:], op=ALU.mult), reads=[Bcen], writes=[Bsq2])
                P.op('pe', lambda e: e.matmul(pax[1][:, :], lhsT=bavg[:], rhs=sq2[:], start=True, stop=True),
                     reads=[Bsq2, Bc], writes=[Bpax[1]])
                P.op('dve', lambda e: e.tensor_scalar(out=rstd[:], in0=pax[1][:, :], scalar1=64e-5, scalar2=None, op0=ALU.add),
                     reads=[Bpax[1]], writes=[Brstd])
                P.op('act', lambda e: e.activation(out=rstd[:], in_=rstd[:], func=AF.Sqrt), reads=[Brstd], writes=[Brstd])
                P.op('dve', lambda e: e.reciprocal(out=rstd[:], in_=rstd[:]), reads=[Brstd], writes=[Brstd])
                P.op('dve', lambda e: e.tensor_tensor(out=cen[:], in0=cen[:], in1=rstd[:], op=ALU.mult), reads=[Bcen, Brstd], writes=[Bcen])
                P.op('dve', lambda e: e.tensor_scalar(out=cen[:], in0=cen[:], scalar1=cv[:, 7:8], scalar2=cv[:, 8:9], op0=ALU.mult, op1=ALU.add),
                     reads=[Bcen, Bc], writes=[Bcen])
                P.op('pe', lambda e: e.matmul(pax[0][:, :], lhsT=aupb[:, 0, :], rhs=ad[0:64, :], start=True, stop=True),
                     reads=[Bad, Bc, Bcen], writes=[Bpax[0]])
                P.op('act', lambda e: e.activation(out=tmp[:], in_=pax[0][:, :], func=AF.Sigmoid, bias=cv[:, 2:3]),
                     reads=[Bpax[0], Bc], writes=[Btmp])
                P.op('dve', lambda e: e.tensor_tensor(out=tmp[:], in0=tmp[:], in1=rate[:], op=ALU.add), reads=[Btmp, Brate], writes=[Btmp])
                P.op('dve', lambda e: e.tensor_scalar(out=tmp[:], in0=tmp[:], scalar1=cv[:, 5:6], scalar2=None, op0=ALU.mult),
                     reads=[Btmp, Bc], writes=[Btmp])
                P.op('dve', lambda e: e.tensor_scalar(out=tmp[:], in0=tmp[:], scalar1=omka2[:, 0:1], scalar2=None, op0=ALU.add),
                     reads=[Btmp, Bc], writes=[Btmp])
                P.op('dve', lambda e: e.tensor_tensor(out=tmp[:], in0=tmp[:], in1=kf[:], op=ALU.mult), reads=[Btmp, Bk], writes=[Btmp])
                P.op('dve', lambda e: e.scalar_tensor_tensor(out=tmp[:], in0=tmp[:], scalar=cv[:, 6:7], in1=r[:], op0=ALU.mult, op1=ALU.mult),
                     reads=[Btmp, Br, Bc], writes=[Btmp])
                P.op('pe', lambda e: e.matmul(pax[1][:, :], lhsT=bones[:], rhs=tmp[:], start=True, stop=True),
                     reads=[Btmp, Bc, Brstd], writes=[Bpax[1]])
                P.op('dve', lambda e: e.tensor_tensor(out=bon[:], in0=pax[1][:, :], in1=v[:], op=ALU.mult), reads=[Bpax[1], Bv], writes=[Bbon])
                P.op('dve', lambda e: e.tensor_tensor(out=cen[:], in0=cen[:], in1=bon[:], op=ALU.add), reads=[Bcen, Bbon], writes=[Bcen])
                pk = proj(4, k, ku)
                shift(4, pk, k, tmp2, Btmp2)
                P.op('act', lambda e: e.activation(out=gd[:], in_=tmp2[:], func=AF.Sigmoid), reads=[Btmp2], writes=[Bgd])
                P.op('pe', lambda e: e.matmul(pax[0][:, :], lhsT=gupb[:], rhs=gd[:], start=True, stop=True),
                     reads=[Bgd, Bc], writes=[Bpax[0]])
                P.op('dve', lambda e: e.tensor_tensor(out=oR[k][:], in0=cen[:], in1=pax[0][:, :], op=ALU.mult),
                     reads=[Bcen, Bpax[0]], writes=[BoR[k]])
                P.dma('pool', lambda e: e.dma_start(out=gin.ap()[lt0 // 2048, 128:256, lt0 % 2048:lt0 % 2048 + 256], in_=oR[k][:]), reads=[BoR[k]], writes=[Bgin])
            return part2

        def do_chunks(d, k, is_ctx, vb, Bvb, ptot, Bptot):
            fm = FM[k]
            Bfm = BFM[k]
            CH = [(c, h, hsl[h]) for c in range(4) for h in range(2)]
            for c, h, hs in CH:
                for j, src in enumerate([fm[hs, c, 2, :], fm[hs, c, 0, :], fm[hs, c, 1, :], vb[hs, 64 * c:64 * c + 64]]):
                    P.op('pe', lambda e, hs=hs, c=c, j=j, src=src: e.transpose(out=pTM[hs, c, j, :], in_=src, identity=identb[hs, hs]),
                         reads=[Bfm, Bvb], writes=[BpTM])
            P.op('act', lambda e: e.activation(out=TM[:], in_=pTM[:], func=AF.Copy), reads=[BpTM], writes=[BTM])
            for c, h, hs in CH:
                for j in range(2):
                    P.op('pe', lambda e, hs=hs, c=c, j=j: e.matmul(pSM[hs, c, j, :], lhsT=fm[hs, c, j, :],
                                                                   rhs=fm[hs, c, 2:4, :], start=True, stop=True),
                         reads=[Bfm], writes=[BpSM])
            P.op('dve', lambda e: e.tensor_tensor(out=SM[:], in0=pSM[:], in1=mk[:, d], op=ALU.mult),
                 reads=[BpSM, Bc], writes=[BSM])
            for jj in range(2):
                P.op('dve', lambda e, jj=jj: e.tensor_tensor(out=LTd[:, :, jj, :], in0=pSM[:, :, 0, 0:64],
                                                             in1=mkd[:, d, :, jj, :], op=ALU.mult),
                     reads=[BpSM, Bc], writes=[BLTd])
            for c, h, hs in CH:
                P.op('pe', lambda e, hs=hs, c=c: e.matmul(pWS[hs, c, 0:64], lhsT=fm[hs, c, 2, :], rhs=fm[hs, c, 0, :], start=True, stop=True),
                     reads=[Bfm], writes=[BpWS])
            P.op('dve', lambda e: e.tensor_tensor(out=L1[:], in0=pWS[:, :, 0:64], in1=mst[:, d], op=ALU.mult),
                 reads=[BpWS, Bc], writes=[BL1])
            P.op('dve', lambda e: e.tensor_tensor(out=TTf[:], in0=LTd[:, :, 0, :], in1=istack[:], op=ALU.add),
                 reads=[BLTd, Bc], writes=[BTT])
            for lvl in range(4):
                if lvl == 0:
                    LTp = lambda hs, c: LTd[hs, c, 0, :]
                    Lp = lambda hs, c: L1[hs, c, :]
                    rd = [BLTd, BL1]
                else:
                    LTp = lambda hs, c, lvl=lvl: POW[hs, lvl - 1, c, 0:64]
                    Lp = lambda hs, c, lvl=lvl: POW[hs, lvl - 1, c, 64:128]
                    rd = [BPOW[lvl - 1]]
                for c, h, hs in CH:
                    if lvl < 3:
                        P.op('pe', lambda e, hs=hs, c=c, LTp=LTp, Lp=Lp: e.matmul(pPOW[hs, c, 0:64], lhsT=Lp(hs, c), rhs=LTp(hs, c), start=True, stop=True),
                             reads=rd, writes=[BpPOW])
                    P.op('pe', lambda e, hs=hs, c=c, LTp=LTp, Lp=Lp: e.matmul(pPOW[hs, c, 64:128], lhsT=LTp(hs, c), rhs=Lp(hs, c), start=True, stop=True),
                         reads=rd, writes=[BpPOW])
                lo_ = 0 if lvl < 3 else 64
                if lvl % 2 == 0:
                    P.op('act', lambda e, lvl=lvl, lo_=lo_: e.activation(out=POW[:, lvl, :, lo_:128], in_=pPOW[:, :, lo_:128], func=AF.Copy),
                         reads=[BpPOW], writes=[BPOW[lvl]])
                else:
                    P.op('dve', lambda e, lvl=lvl, lo_=lo_: e.tensor_copy(out=POW[:, lvl, :, lo_:128], in_=pPOW[:, :, lo_:128]),
                         reads=[BpPOW], writes=[BPOW[lvl]])
                for c, h, hs in CH:
                    P.op('pe', lambda e, hs=hs, c=c, lvl=lvl: e.matmul(pWS[hs, c, 0:64], lhsT=POW[hs, lvl, c, 64:128], rhs=TTf[hs, c, :], start=True, stop=True),
                         reads=[BPOW[lvl], BTT], writes=[BpWS])
                P.op('dve', lambda e: e.tensor_tensor(out=TTf[:], in0=TTf[:], in1=pWS[:, :, 0:64], op=ALU.add),
                     reads=[BTT, BpWS], writes=[BTT])
            for c, h, hs in CH:
                P.op('pe', lambda e, hs=hs, c=c: e.matmul(pW0[hs, c, 64:128], lhsT=SM[hs, c, 1, 0:64], rhs=TM[hs, c, 3, :], start=True, stop=True),
                     reads=[BSM, BTM], writes=[BpW0])
            P.op('pool', lambda e: e.tensor_copy(out=Wf[:, :, 0:64], in_=TM[:, :, 0, :]), reads=[BTM], writes=[BWf])
            P.op('dve', lambda e: e.tensor_copy(out=Wf[:, :, 64:128], in_=pW0[:, :, 64:128]), reads=[BpW0], writes=[BWf])
            for c, h, hs in CH:
                P.op('pe', lambda e, hs=hs, c=c: e.matmul(pWS[hs, c, :], lhsT=TTf[hs, c, :], rhs=Wf[hs, c, :], start=True, stop=True),
                     reads=[BTT, BWf], writes=[BpWS])
            P.op('act', lambda e: e.activation(out=Zf[:], in_=pWS[:], func=AF.Copy), reads=[BpWS], writes=[BZf])
            for c, h, hs in CH:
                P.op('pe', lambda e, hs=hs, c=c: e.matmul(pW0[hs, c, :], lhsT=LTd[hs, c, 1, :], rhs=Zf[hs, c, :], start=True, stop=True),
                     reads=[BLTd, BZf], writes=[BpW0])
            P.op('act', lambda e: e.activation(out=Cf[:], in_=pW0[:], func=AF.Copy), reads=[BpW0], writes=[BCf])
            for c, h, hs in CH:
                P.op('pe', lambda e, hs=hs, c=c: e.matmul(pWS[hs, c, :], lhsT=TTf[hs, c, :], rhs=Cf[hs, c, :], start=True, stop=True),
                     reads=[BTT, BCf], writes=[BpWS])
            P.op('dve', lambda e: e.tensor_tensor(out=Wb[:], in0=Zf[:], in1=pWS[:], op=ALU.add),
                 reads=[BZf, BpWS], writes=[BWb])
            for c, h, hs in CH:
                if not is_ctx:
                    P.op('pe', lambda e, hs=hs, c=c: e.matmul(pPOW[hs, c, 0:64], lhsT=Wb[hs, c, 0:64], rhs=SM[hs, c, 0, 64:128], start=True, stop=True),
                         reads=[BWb, BSM], writes=[BpPOW])
                P.op('pe', lambda e, hs=hs, c=c: e.matmul(pPOW[hs, c, 64:128], lhsT=Wb[hs, c, 0:64], rhs=TM[hs, c, 1, :], start=True, stop=True),
                     reads=[BWb, BTM], writes=[BpPOW])
            if not is_ctx:
                P.op('dve', lambda e: e.tensor_tensor(out=QT[:], in0=pPOW[:, :, 0:64], in1=fm[:, :, 3, :], op=ALU.add),
                     reads=[BpPOW, Bfm], writes=[BQT])
            P.op('dve', lambda e: e.tensor_tensor(out=GT[:], in0=pPOW[:, :, 64:128], in1=istack[:], op=ALU.add),
                 reads=[BpPOW, Bc], writes=[BGT])
            corder = range(4) if d == 0 else range(3, -1, -1)
            for c in corder:
                hc = state["hcur"]
                hn = 1 - hc
                if not is_ctx:
                    for h in range(2):
                        hs = hsl[h]
                        ysl = slice(64 * c, 64 * c + 64)
                        P.op('pe', lambda e, hs=hs, ysl=ysl, c=c: e.matmul(pW0[hs, c, 64:128], lhsT=Wb[hs, c, 64:128], rhs=SM[hs, c, 0, 64:128], start=True, stop=False),
                             reads=[BWb, BSM], writes=[BpW0])
                        P.op('pe', lambda e, hs=hs, ysl=ysl, c=c: e.matmul(pW0[hs, c, 64:128], lhsT=TM[hs, c, 3, :], rhs=SM[hs, c, 1, 64:128], start=False, stop=False),
                             reads=[BTM, BSM], writes=[BpW0])
                        P.op('pe', lambda e, hs=hs, ysl=ysl, c=c, hc=hc: e.matmul(pW0[hs, c, 64:128], lhsT=Hb[hc][hs, :], rhs=QT[hs, c, :], start=False, stop=True),
                             reads=[BHb[hc], BQT], writes=[BpW0])
                for h in range(2):
                    hs = hsl[h]
                    P.op('pe', lambda e, hs=hs, c=c: e.matmul(pW0[hs, c, 0:64], lhsT=TM[hs, c, 1, :], rhs=Wb[hs, c, 64:128], start=True, stop=False),
                         reads=[BTM, BWb], writes=[BpW0])
                    P.op('pe', lambda e, hs=hs, c=c: e.matmul(pW0[hs, c, 0:64], lhsT=TM[hs, c, 2, :], rhs=TM[hs, c, 3, :], start=False, stop=False),
                         reads=[BTM], writes=[BpW0])
                    P.op('pe', lambda e, hs=hs, c=c, hc=hc: e.matmul(pW0[hs, c, 0:64], lhsT=GT[hs, c, :], rhs=Hb[hc][hs, :], start=False, stop=True),
                         reads=[BGT, BHb[hc]], writes=[BpW0])
                P.op('act', lambda e, c=c: e.activation(out=Hf[:], in_=pW0[:, c, 0:64], func=AF.Identity, scale=ptot[:, c:c + 1]),
                     reads=[BpW0, Bptot], writes=[BHf])
                P.op('dve', lambda e, hn=hn: e.tensor_copy(out=Hb[hn][:], in_=Hf[:]), reads=[BHf], writes=[BHb[hn]])
                state["hcur"] = hn

        tiles = []
        for d in range(2):
            tiles.append((d, 0, True, True, True))
            order = range(64) if d == 0 else range(63, -1, -1)
            for ti in order:
                tiles.append((d, TC + ti * 256, False, ti == 0, ti == 63))
        part2s = {0: do_tile(*tiles[0])}
        IL = Interleaver(P, quota=(2, 1))
        for n in range(len(tiles)):
            def main(n=n):
                if tiles[n][2]:
                    hc0 = state["hcur"]
                    P.op('dve', lambda e, hc0=hc0: e.memset(Hb[hc0][:], 0.0), writes=[BHb[hc0]])
                part2s.pop(n)()

            def prep(n=n):
                part2s[n + 1] = do_tile(*tiles[n + 1])
            if n + 1 < len(tiles):
                IL.run(main, prep)
            else:
                main()
        P.flush()


def phase_tail(nc, I, gin, gout, out, modbc, identb, U=None, dbgU=None, dbgG=None):
    with contextlib.ExitStack() as st:
        P = Prog(nc)

        def sb(name, shape, dt):
            return st.enter_context(nc.sbuf_tensor('S%d_' % P.uid + name, shape, dt))

        def ps(name, shape, dt):
            return st.enter_context(nc.psum_tensor('P%d_' % P.uid + name, shape, dt))
        Bg = Buf()
        if dbgU is not None:
            for j in range(8):
                P.dma('sp', lambda e, j=j: e.dma_start(out=dbgU[j * 128:(j + 1) * 128, :], in_=U[j * 128:(j + 1) * 128, :]))
                P.dma('sp', lambda e, j=j: e.dma_start(out=dbgG[j], in_=gin.ap()[j]))
        for gi in range(8):
            P.dma('pool', lambda e, gi=gi: e.collective_compute("AllGather", ALU.bypass, replica_groups=[[0, 1, 2, 3], [4, 5, 6, 7]],
                                                                 ins=[gin.ap()[gi].opt()], outs=[gout.ap()[gi].opt()]), writes=[Bg], inc=1)
        wo = sb("wo", [128, 8, D], BF16)
        w1 = sb("w1", [128, 8, 5632], BF16)
        w2 = sb("w2", [128, 22, D], BF16)
        stg = [sb(f"stg{i}", [128, 512], F32) for i in range(2)]
        Bstg = [Buf() for _ in range(2)]
        Bw = Buf()
        si = 0
        jobs = []
        wov = I["w_out"].rearrange("(kc p) n -> p kc n", p=128)
        w1v = I["ffn_w_in"].rearrange("(kc p) n -> p kc n", p=128)
        w2v = I["ffn_w_out"].rearrange("(kc p) n -> p kc n", p=128)
        for kc in range(0, 8, 2):
            pass
        for kc in range(8):
            for n0 in range(0, D, 512):
                jobs.append((wov[:, kc:kc + 1, n0:n0 + 512], wo[:, kc:kc + 1, n0:n0 + 512], 1, 512))
        for kc in range(8):
            for n0 in range(0, 5632, 512):
                jobs.append((w1v[:, kc:kc + 1, n0:n0 + 512], w1[:, kc:kc + 1, n0:n0 + 512], 1, 512))
        for kc in range(22):
            for n0 in range(0, D, 512):
                jobs.append((w2v[:, kc:kc + 1, n0:n0 + 512], w2[:, kc:kc + 1, n0:n0 + 512], 1, 512))
        for ji, (src, dst, a, b) in enumerate(jobs):
            k = ji % 2
            sv = stg[k][:, 0:a * b].rearrange("p (a b) -> p a b", b=b)
            P.dma('sp', lambda e, sv=sv, src=src: e.dma_start(out=sv, in_=src), writes=[Bstg[k]])
            eng = ['dve', 'pool', 'act'][ji % 3]
            if eng == 'act':
                P.op('act', lambda e, sv=sv, dst=dst: e.activation(out=dst, in_=sv, func=AF.Copy), reads=[Bstg[k]], writes=[Bw])
            else:
                P.op(eng, lambda e, sv=sv, dst=dst: e.tensor_copy(out=dst, in_=sv), reads=[Bstg[k]], writes=[Bw])
        reg = st.enter_context(nc.sync.register("qoffr"))
        stt = {"val": None}

        def ldreg(e):
            ins = e.reg_load(reg, I["qoff"][0:1, 0:1])
            stt["val"] = e.snap(reg)
            return ins
        P.q['sp'].append(['raw', ldreg])
        goutq = gout.ap().rearrange("(q a) f t -> q a f t", a=2)
        oq = nc.dram_tensor("oq_scr", [1024, 4096], BF16).ap()
        oqv = oq.rearrange("(fc p) t -> p fc t", p=128)
        Boq = Buf()
        for fc in range(8):
            for a in range(2):
                def cpq(e, fc=fc, a=a):
                    return e.dma_start(out=oq[fc * 128:(fc + 1) * 128, a * 2048:(a + 1) * 2048],
                                       in_=goutq[bass.ds(stt["val"], 1), a, fc * 128:(fc + 1) * 128, :].squeeze(0))
                P.dma('sp', cpq, reads=[Bg], writes=[Boq])
        oT = [sb(f"toT{i}", [128, 8, 128], BF16) for i in range(2)]
        BoT = [Buf() for _ in range(2)]
        xo = sb("xo", [128, D], F32)
        Bxo = Buf()
        pM = [ps(f"pM{i}", [128, 512], F32) for i in range(2)]
        BpM = [Buf() for _ in range(2)]
        h1 = [sb(f"h1{i}", [128, D], F32) for i in range(2)]
        Bh1 = [Buf() for _ in range(2)]
        ss = sb("tss", [128, 1], F32)
        Bss = Buf()
        u2b = sb("u2b", [128, D], BF16)
        Bu2b = Buf()
        ptr = ps("tptr", [128, 8, 128], BF16)
        Bptr = Buf()
        u2T = sb("u2T", [128, 8, 256], BF16)
        Bu2T = Buf()
        pG = [ps(f"pG{i}", [128, 2, 256], F32) for i in range(2)]
        BpG = [Buf() for _ in range(2)]
        sg = [sb(f"sg{i}", [128, 256], F32) for i in range(2)]
        Bsg = [Buf() for _ in range(2)]
        aT = sb("aT", [128, 22, 256], BF16)
        BaT = Buf()
        Bout = Buf()
        Bm = Buf()
        for tp_ in range(16):
            for a in range(2):
                ti = 2 * tp_ + a
                P.dma('sp', lambda e, a=a, ti=ti: e.dma_start(out=oT[a][:], in_=oqv[:, :, ti * 128:(ti + 1) * 128]), reads=[Boq], writes=[BoT[a]])
                P.dma('sp', lambda e, ti=ti: e.dma_start(out=xo[:], in_=I["x_own"][ti * 128:(ti + 1) * 128, :]), writes=[Bxo])
                for nh in range(2):
                    for fc in range(8):
                        P.op('pe', lambda e, a=a, nh=nh, fc=fc: e.matmul(pM[nh][:, :], lhsT=oT[a][:, fc, :], rhs=wo[:, fc, nh * 512:(nh + 1) * 512],
                                                                          start=(fc == 0), stop=(fc == 7)),
                             reads=[BoT[a], Bw], writes=[BpM[nh]])
                    P.op('dve', lambda e, a=a, nh=nh: e.tensor_tensor(out=h1[a][:, nh * 512:(nh + 1) * 512], in0=pM[nh][:, :],
                                                                       in1=modbc[:, 0, nh * 512:(nh + 1) * 512], op=ALU.mult),
                         reads=[BpM[nh], Bm], writes=[Bh1[a]])
                P.op('pool', lambda e, a=a: e.tensor_tensor(out=h1[a][:], in0=h1[a][:], in1=xo[:], op=ALU.add),
                     reads=[Bh1[a], Bxo], writes=[Bh1[a]])
                P.op('act', lambda e, a=a: e.activation(out=xo[:], in_=h1[a][:], func=AF.Square, accum_out=ss[:]),
                     reads=[Bh1[a]], writes=[Bxo, Bss])
                P.op('dve', lambda e: e.tensor_scalar(out=ss[:], in0=ss[:], scalar1=1.0 / D, scalar2=EPS, op0=ALU.mult, op1=ALU.add),
                     reads=[Bss], writes=[Bss])
                P.op('act', lambda e: e.activation(out=ss[:], in_=ss[:], func=AF.Sqrt), reads=[Bss], writes=[Bss])
                P.op('dve', lambda e: e.reciprocal(out=ss[:], in_=ss[:]), reads=[Bss], writes=[Bss])
                P.op('dve', lambda e, a=a: e.scalar_tensor_tensor(out=xo[:], in0=h1[a][:], scalar=ss[:, 0:1], in1=modbc[:, 2, :],
                                                                  op0=ALU.mult, op1=ALU.mult), reads=[Bh1[a], Bss, Bm], writes=[Bxo])
                P.op('pool', lambda e: e.tensor_tensor(out=u2b[:], in0=xo[:], in1=modbc[:, 1, :], op=ALU.add), reads=[Bxo, Bm], writes=[Bu2b])
                for kc in range(8):
                    P.op('pe', lambda e, kc=kc: e.transpose(out=ptr[:, kc, :], in_=u2b[:, kc * 128:(kc + 1) * 128], identity=identb[:]),
                         reads=[Bu2b], writes=[Bptr])
                P.op('act', lambda e, a=a: e.activation(out=u2T[:, :, a * 128:(a + 1) * 128], in_=ptr[:], func=AF.Copy), reads=[Bptr], writes=[Bu2T])
            for fc in range(22):
                g2 = fc % 2
                for part in range(2):
                    c0_ = part * 2816 + fc * 128
                    for kc in range(8):
                        P.op('pe', lambda e, g2=g2, part=part, c0_=c0_, kc=kc: e.matmul(
                            pG[g2][:, part, :], lhsT=w1[:, kc, c0_:c0_ + 128], rhs=u2T[:, kc, :], start=(kc == 0), stop=(kc == 7)),
                            reads=[Bw, Bu2T], writes=[BpG[g2]])
                P.op('act', lambda e, g2=g2: e.activation(out=sg[g2][:], in_=pG[g2][:, 0, :], func=AF.Silu), reads=[BpG[g2]], writes=[Bsg[g2]])
                P.op('dve', lambda e, g2=g2, fc=fc: e.tensor_tensor(out=aT[:, fc, :], in0=sg[g2][:], in1=pG[g2][:, 1, :], op=ALU.mult),
                     reads=[Bsg[g2], BpG[g2]], writes=[BaT])
            for a in range(2):
                ti = 2 * tp_ + a
                for nh in range(2):
                    for fc in range(22):
                        P.op('pe', lambda e, a=a, nh=nh, fc=fc: e.matmul(pM[nh][:, :], lhsT=aT[:, fc, a * 128:(a + 1) * 128], rhs=w2[:, fc, nh * 512:(nh + 1) * 512],
                                                                          start=(fc == 0), stop=(fc == 21)),
                             reads=[BaT, Bw], writes=[BpM[nh]])
                    P.op('dve', lambda e, nh=nh: e.tensor_tensor(out=xo[:, nh * 512:(nh + 1) * 512], in0=pM[nh][:, :],
                                                                 in1=modbc[:, 3, nh * 512:(nh + 1) * 512], op=ALU.mult),
                         reads=[BpM[nh], Bm], writes=[Bxo])
                P.op('pool', lambda e, a=a: e.tensor_tensor(out=xo[:], in0=xo[:], in1=h1[a][:], op=ALU.add),
                     reads=[Bxo, Bh1[a]], writes=[Bxo])
                P.dma('pool', lambda e, ti=ti: e.dma_start(out=out[ti * 128:(ti + 1) * 128, :], in_=xo[:]), reads=[Bxo], writes=[Bout])
        P.flush()


def _bias_tiles(rpb2):
    blocks = na_blocks()
    combos = {}
    for m in (2, 0, 1, 126, 127):
        for kt, slot in blocks[m]:
            combos[slot] = (m, kt)
    kr_l, kc_ = np.divmod(np.arange(128), 64)
    outb = np.full((2, 21, 128, 128), NEG, np.float32)
    for slot, (m, kt) in combos.items():
        qr = 2 * m + kr_l[None, :]
        qc = kc_[None, :]
        kr = 2 * kt + kr_l[:, None]
        kc = kc_[:, None]
        rs = np.clip(qr - 4, 0, 248)
        cs = np.clip(qc - 8, 0, 48)
        ok = (kr >= rs) & (kr < rs + 8) & (kc >= cs) & (kc < cs + 16)
        di = np.clip(kr - qr + 7, 0, 14)
        dj = np.clip(kc - qc + 15, 0, 30)
        for h in range(2):
            outb[h, slot] = np.where(ok, rpb2[h][di, dj], np.float32(NEG))
    return np.ascontiguousarray(outb.transpose(2, 0, 1, 3))


_NC_CACHE = {}


def kernel(**inp):
    f = lambda a: np.ascontiguousarray(np.asarray(a, dtype=np.float32))
    x, c, ctx, c_ctx = f(inp["x"]), f(inp["c"]), f(inp["ctx"]), f(inp["c_ctx"])
    w_in = f(inp["w_in"])[0]
    w_ada = f(inp["w_ada"])[0]
    b_ada = f(inp["b_ada"])[0]
    mu_p = f(inp["rw_mu_prev"])[0]
    mu_n = f(inp["rw_mu_next"])[0]
    ident = np.eye(128, dtype=np.float32)
    bones = np.kron(np.eye(2, dtype=np.float32), np.ones((64, 64), np.float32))
    s_i, t_i = np.meshgrid(np.arange(64), np.arange(64), indexing="ij")
    mk = np.zeros((128, 2, 2, 128), np.float32)
    mst = np.zeros((128, 2, 64), np.float32)
    for d in range(2):
        strict = (s_i < t_i) if d == 0 else (s_i > t_i)
        incl = (s_i <= t_i) if d == 0 else (s_i >= t_i)
        blk = np.concatenate([strict, incl], axis=1).astype(np.float32)
        for h in range(2):
            for j in range(2):
                mk[64 * h:64 * h + 64, d, j, :] = blk
            mst[64 * h:64 * h + 64, d, :] = strict.T.astype(np.float32)
    mkd = np.zeros((128, 2, 2, 64), np.float32)
    bd = (s_i // 32 == t_i // 32)
    for d in range(2):
        strict = (s_i < t_i) if d == 0 else (s_i > t_i)
        for h in range(2):
            mkd[64 * h:64 * h + 64, d, 0, :] = (strict & bd).astype(np.float32)
            mkd[64 * h:64 * h + 64, d, 1, :] = (strict & ~bd).astype(np.float32)
            mst[64 * h:64 * h + 64, d, :] = (strict & bd).T.astype(np.float32)
    istack = np.concatenate([np.eye(64, dtype=np.float32)] * 2, axis=0)
    reset = np.ones((128, 256), np.float32)
    reset[:, 0::64] = 0.0
    in_maps = []
    for core in range(8):
        b, g = divmod(core, 4)
        hc = slice(128 * g, 128 * g + 128)
        m = {}
        m["xcat"] = np.concatenate([ctx[b], x[b]], axis=0)
        m["x_own"] = x[b, 4096 * g:4096 * (g + 1)]
        c2 = np.stack([c[b], c_ctx], axis=0)
        m["c2T"] = c2.reshape(2, 8, 128).transpose(2, 1, 0)
        m["w_ada"] = w_ada
        m["b_adaT"] = b_ada.reshape(48, 128).T
        m["b_ada_bc"] = np.broadcast_to(b_ada[None, 2 * D:], (128, 4 * D))
        m["g1T"] = f(inp["norm1_g"])[0].reshape(8, 128).T
        m["g2_bc"] = np.broadcast_to(f(inp["norm2_g"])[0][None, :], (128, D))
        qc = np.arange(128 * g, 128 * g + 128)
        m["w_na"] = w_in[:, np.concatenate([qc, 512 + qc, 1024 + qc])]
        rwc = np.concatenate([1536 + qc, 2048 + qc, 2560 + qc, np.arange(3072, 3200), np.arange(3264, 3392), np.arange(3200, 3264)])
        m["w_rw"] = w_in[:, rwc]
        rc_ = rwc - 1536
        muT = np.zeros((128, 2, 6), np.float32)
        for j, mv in enumerate((mu_p, mu_n)):
            col = np.zeros(768, np.float32)
            col[:704] = mv[rc_]
            muT[:, j, :] = col.reshape(6, 128).T
        m["muT"] = muT
        gq = f(inp["na_q_g"])[0]
        gk = f(inp["na_k_g"])[0]
        m["gqk_bc"] = np.broadcast_to(np.stack([gq, gk], 0)[None], (128, 2, 64))
        m["bias"] = _bias_tiles(f(inp["na_rpb"])[0][2 * g:2 * g + 2])
        m["w_upT"] = f(inp["rw_w_up"])[0][:, :, hc].reshape(128, 128)
        m["a_upT"] = f(inp["rw_a_up"])[0][:, :, hc].transpose(1, 0, 2)
        m["g_up"] = f(inp["rw_g_up"])[0][:, hc]
        cvv = np.stack([f(inp["rw_w0"])[0][0, hc], f(inp["rw_w0"])[0][1, hc], f(inp["rw_a0"])[0][0, hc], f(inp["rw_a0"])[0][1, hc],
                        f(inp["rw_k_k"])[0][hc], f(inp["rw_k_a"])[0][hc], f(inp["rw_r_k"])[0].reshape(512)[hc],
                        f(inp["rw_ln_g"])[0][hc], f(inp["rw_ln_b"])[0][hc]], axis=1)
        m["cv"] = cvv
        perm = np.concatenate([np.concatenate([np.arange(128 * gg, 128 * gg + 128), 512 + np.arange(128 * gg, 128 * gg + 128)])
                               for gg in range(4)])
        m["w_out"] = f(inp["w_out"])[0][perm]
        m["ffn_w_in"] = f(inp["ffn_w_in"])[0]
        m["ffn_w_out"] = f(inp["ffn_w_out"])[0]
        m["qoff"] = np.array([[g]], np.int32)
        m["c_ident"] = ident
        m["c_bones"] = bones
        m["c_mk"] = np.broadcast_to(mk[:, :, None], (128, 2, 4, 2, 128))
        m["c_mst"] = np.broadcast_to(mst[:, :, None], (128, 2, 4, 64))
        m["c_mkd"] = np.broadcast_to(mkd[:, :, None], (128, 2, 4, 2, 64))
        m["c_istack"] = np.broadcast_to(istack[:, None], (128, 4, 64))
        m["c_reset"] = reset
        for k_, v_ in m.items():
            want = np.int32 if k_ == "qoff" else np.float32
            m[k_] = np.ascontiguousarray(v_, dtype=want)
            assert list(m[k_].shape) == IN_SPECS[k_][0], (k_, m[k_].shape)
        in_maps.append(m)
    if "nc" not in _NC_CACHE:
        _NC_CACHE["nc"] = build_nc()
    res = run_bass_kernel_spmd(_NC_CACHE["nc"], in_maps, core_ids=list(range(8)))
    outp = np.zeros((2, T, D), np.float32)
    for core in range(8):
        b, g = divmod(core, 4)
        outp[b, 4096 * g:4096 * (g + 1)] = res.results[core]["out"]
    return outp
```
